# Optimizing a Trainium2 kernel written in Bass

```python
import math
import jax, jax.numpy as jnp
from jax import lax
import numpy as np

D_MODEL = 1024
BATCH = 8
SEQ = 2048
DEPTH = 2

CHUNK = 64
Q_BLOCK = 128
HEAD_DIM = 64
MEM_LEN = 256
MEM_HEADS = 4
MEM_W = MEM_HEADS * HEAD_DIM
MIX_W = D_MODEL
BR_W = MIX_W - MEM_W
A_HEADS = BR_W // HEAD_DIM
B_HEADS = BR_W // (2 * HEAD_DIM)
LORA_W = 64
LORA_A = 64
N_A = DEPTH // 2
N_B = DEPTH - N_A
A_SHIFT = 3 * BR_W + LORA_W + LORA_A
A_IN = A_SHIFT + BR_W + 2 * MEM_W
B_IN = 2 * BR_W + 2 * MEM_W
NORM_EPS = 1e-6
LNX_EPS = 64e-5

kernel_name = 'hybrid_rwkv7_diffattn_yoco'


def rms_norm(x, g):
    xf = x.astype(jnp.float32)
    y = xf * lax.rsqrt(jnp.mean(xf * xf, axis=-1, keepdims=True) + NORM_EPS)
    return (y * g.astype(jnp.float32)).astype(x.dtype)


def token_shift(z):
    return jnp.pad(z, ((0, 0), (1, 0), (0, 0)))[:, :-1]


def alibi_slopes(n):
    def pow2(m):
        start = 2.0 ** (-8.0 / m)
        return [start ** (i + 1) for i in range(m)]
    if math.log2(n).is_integer():
        s = pow2(n)
    else:
        c = 2 ** int(math.floor(math.log2(n)))
        s = pow2(c) + pow2(2 * c)[0::2][: n - c]
    return jnp.asarray(np.array(s, dtype=np.float32))


def rwkv7_time_mix(z, mu, w0, w2, a0, a2, k_k, k_a, r_k, lnx_w, lnx_b):
    B, S, _ = z.shape
    z = z.astype(jnp.float32)
    z = z + (token_shift(z) - z) * mu
    r, k, v, wd, ad = jnp.split(z, [BR_W, 2 * BR_W, 3 * BR_W, 3 * BR_W + LORA_W], axis=-1)
    w = -jax.nn.softplus(-(w0 + jnp.tanh(wd) @ w2)) - 0.5
    decay = jnp.exp(-jnp.exp(w))
    a = jax.nn.sigmoid(a0 + ad @ a2)
    heads = lambda t: t.reshape(B, S, A_HEADS, HEAD_DIM)
    kk = heads(k * k_k)
    kk = kk * lax.rsqrt(jnp.maximum(jnp.sum(kk * kk, axis=-1, keepdims=True), 1e-12))
    k = k * (1.0 + (a - 1.0) * k_a)
    r, k, v, decay, a = heads(r), heads(k), heads(v), heads(decay), heads(a)

    def step(state, inp):
        r_t, w_t, k_t, v_t, kk_t, a_t = inp
        sa = jnp.einsum('bhvk,bhk->bhv', state, -kk_t)
        state = (state * w_t[:, :, None, :]
                 + sa[..., None] * (kk_t * a_t)[:, :, None, :]
                 + v_t[..., None] * k_t[:, :, None, :])
        y_t = jnp.einsum('bhvk,bhk->bhv', state, r_t)
        return state, y_t

    tm = lambda t: jnp.moveaxis(t, 1, 0)
    state0 = jnp.zeros((B, A_HEADS, HEAD_DIM, HEAD_DIM), jnp.float32)
    _, y = lax.scan(step, state0, (tm(r), tm(decay), tm(k), tm(v), tm(kk), tm(a)))
    y = jnp.moveaxis(y, 0, 1)
    mean = jnp.mean(y, axis=-1, keepdims=True)
    var = jnp.mean(jnp.square(y - mean), axis=-1, keepdims=True)
    y = ((y - mean) * lax.rsqrt(var + LNX_EPS)).reshape(B, S, BR_W) * lnx_w + lnx_b
    bonus = jnp.sum(r * k * r_k, axis=-1, keepdims=True) * v
    return y + bonus.reshape(B, S, BR_W)


def diff_attention(q, k_sh, v_sh, lq1, lk1, lq2, lk2, subln, lam_init):
    B, S, _ = q.shape
    q = q.reshape(B, S, B_HEADS, 2, HEAD_DIM) * (HEAD_DIM ** -0.5)
    k = k_sh.reshape(B, S, B_HEADS, 2, HEAD_DIM)
    v = v_sh.reshape(B, S, B_HEADS, 2 * HEAD_DIM)
    lq1f, lk1f = lq1.astype(jnp.float32), lk1.astype(jnp.float32)
    lq2f, lk2f = lq2.astype(jnp.float32), lk2.astype(jnp.float32)
    lam = jnp.exp(jnp.sum(lq1f * lk1f)) - jnp.exp(jnp.sum(lq2f * lk2f)) + lam_init
    slopes = alibi_slopes(B_HEADS)
    pos = jnp.arange(S)
    nb = S // Q_BLOCK
    qb = jnp.moveaxis(q.reshape(B, nb, Q_BLOCK, B_HEADS, 2, HEAD_DIM), 1, 0)

    def block(args):
        q_blk, i = args
        t = i * Q_BLOCK + jnp.arange(Q_BLOCK)
        s = jnp.einsum('bqhcd,bkhcd->bhcqk', q_blk, k, preferred_element_type=jnp.float32)
        dist = jnp.abs(t[:, None] - pos[None, :]).astype(jnp.float32)
        bias = -slopes[:, None, None] * dist
        allowed = (pos[None, :] // CHUNK) <= (t[:, None] // CHUNK)
        s = jnp.where(allowed, s + bias[None, :, None], -jnp.inf)
        p = jax.nn.softmax(s, axis=-1)
        attn = (p[:, :, 0] - lam * p[:, :, 1]).astype(v.dtype)
        return jnp.einsum('bhqk,bkhe->bqhe', attn, v)

    o = lax.map(block, (qb, jnp.arange(nb)))
    o = jnp.moveaxis(o, 0, 1).reshape(B, S, B_HEADS, 2 * HEAD_DIM)
    o = rms_norm(o, subln) * (1.0 - lam_init)
    return o.reshape(B, S, BR_W)


def memory_cross_attention(q, mem_n, w_mem_kv):
    B, S, _ = q.shape
    k, v = jnp.split(mem_n @ w_mem_kv, 2, axis=-1)
    k = k.reshape(B, -1, MEM_HEADS, HEAD_DIM)
    v = v.reshape(B, -1, MEM_HEADS, HEAD_DIM)
    q = q.reshape(B, S, MEM_HEADS, HEAD_DIM)
    s = jnp.einsum('bshd,bmhd->bhsm', q, k, preferred_element_type=jnp.float32) * (HEAD_DIM ** -0.5)
    p = jax.nn.softmax(s, axis=-1).astype(v.dtype)
    o = jnp.einsum('bhsm,bmhd->bshd', p, v)
    return o.reshape(B, S, MEM_W)


def setup_inputs(seed: int = 0) -> dict:
    key = jax.random.key(seed)
    ks = jax.random.split(key, 32)
    nrm = lambda k, shape, scale: scale * jax.random.normal(k, shape, jnp.float32)
    gain = lambda k, shape: 1.0 + 0.05 * jax.random.normal(k, shape, jnp.float32)
    return {
        'x': jax.random.normal(ks[0], (BATCH, SEQ, D_MODEL), jnp.float32),
        'mem': jax.random.normal(ks[1], (BATCH, MEM_LEN, D_MODEL), jnp.float32),
        'pre_norm': gain(ks[2], (DEPTH, D_MODEL)),
        'post_norm': gain(ks[3], (DEPTH, D_MODEL)),
        'w_out': nrm(ks[4], (DEPTH, MIX_W, D_MODEL), MIX_W ** -0.5),
        'mem_norm': gain(ks[5], (DEPTH, D_MODEL)),
        'w_mem_kv': nrm(ks[6], (DEPTH, D_MODEL, 2 * MEM_W), D_MODEL ** -0.5),
        'a_w_in': nrm(ks[7], (N_A, D_MODEL, A_IN), D_MODEL ** -0.5),
        'a_shift_mu': jax.random.uniform(ks[8], (N_A, A_SHIFT), jnp.float32),
        'a_w0': jax.random.uniform(ks[9], (N_A, BR_W), jnp.float32, minval=-6.0, maxval=0.0),
        'a_w2': nrm(ks[10], (N_A, LORA_W, BR_W), 0.5 * LORA_W ** -0.5),
        'a_a0': nrm(ks[11], (N_A, BR_W), 0.1),
        'a_a2': nrm(ks[12], (N_A, LORA_A, BR_W), 0.5 * LORA_A ** -0.5),
        'a_k_k': 0.85 + 0.05 * jax.random.normal(ks[13], (N_A, BR_W), jnp.float32),
        'a_k_a': gain(ks[14], (N_A, BR_W)),
        'a_r_k': nrm(ks[15], (N_A, A_HEADS, HEAD_DIM), 0.1),
        'a_lnx_w': gain(ks[16], (N_A, BR_W)),
        'a_lnx_b': nrm(ks[17], (N_A, BR_W), 0.02),
        'kv_norm': gain(ks[18], (D_MODEL,)),
        'w_kv': nrm(ks[19], (D_MODEL, 2 * BR_W), D_MODEL ** -0.5),
        'b_w_in': nrm(ks[20], (N_B, D_MODEL, B_IN), D_MODEL ** -0.5),
        'b_lam_q1': nrm(ks[21], (N_B, HEAD_DIM), 0.1),
        'b_lam_k1': nrm(ks[22], (N_B, HEAD_DIM), 0.1),
        'b_lam_q2': nrm(ks[23], (N_B, HEAD_DIM), 0.1),
        'b_lam_k2': nrm(ks[24], (N_B, HEAD_DIM), 0.1),
        'b_subln': gain(ks[25], (N_B, 2 * HEAD_DIM)),
    }


def reference(x, mem, pre_norm, post_norm, w_out, mem_norm, w_mem_kv,
              a_w_in, a_shift_mu, a_w0, a_w2, a_a0, a_a2, a_k_k, a_k_a, a_r_k, a_lnx_w, a_lnx_b,
              kv_norm, w_kv,
              b_w_in, b_lam_q1, b_lam_k1, b_lam_q2, b_lam_k2, b_subln):
    k_sh = None
    v_sh = None
    for l in range(DEPTH):
        h = rms_norm(x, pre_norm[l])
        mem_n = rms_norm(mem, mem_norm[l])
        if l < N_A:
            i = l
            proj = h @ a_w_in[i]
            z, gate, q_mem, g_mem = jnp.split(
                proj, [A_SHIFT, A_SHIFT + BR_W, A_SHIFT + BR_W + MEM_W], axis=-1)
            y_br = rwkv7_time_mix(z, a_shift_mu[i], a_w0[i], a_w2[i], a_a0[i], a_a2[i],
                                  a_k_k[i], a_k_a[i], a_r_k[i], a_lnx_w[i], a_lnx_b[i])
        else:
            if l == N_A:
                k_sh, v_sh = jnp.split(rms_norm(x, kv_norm) @ w_kv, 2, axis=-1)
            i = l - N_A
            proj = h @ b_w_in[i]
            q, gate, q_mem, g_mem = jnp.split(
                proj, [BR_W, 2 * BR_W, 2 * BR_W + MEM_W], axis=-1)
            lam_init = 0.8 - 0.6 * math.exp(-0.3 * l)
            y_br = diff_attention(q, k_sh, v_sh, b_lam_q1[i], b_lam_k1[i], b_lam_q2[i],
                                  b_lam_k2[i], b_subln[i], lam_init)
        y_mem = memory_cross_attention(q_mem, mem_n, w_mem_kv[l])
        y = jnp.concatenate([(y_br * jax.nn.silu(gate.astype(jnp.float32))).astype(x.dtype),
                             y_mem * jax.nn.silu(g_mem)], axis=-1) @ w_out[l]
        x = x + rms_norm(y, post_norm[l])
    return x
```

```python
import math
import os
TOG = 'nomerge,nobank2'
SCHED = os.environ.get('KSCHED', '1') == '1'
PEF, ACTF, DVEF, POOLF, LATF = [float(v) for v in os.environ.get('KF', '0.93,1.07,0.98,0.80,0.25').split(',')]
FREE_SP = os.environ.get('KFSP', '1') == '1'
DMAX = float(os.environ.get('KDX', '0.9'))
QUANT = float(os.environ.get('KQ', '0.05'))
RATES0 = [int(v) for v in os.environ.get('KR0', '1,1,1').split(',')]
from contextlib import ExitStack

import numpy as np
import concourse.bass as bass
import concourse.mybir as mybir
from concourse.bass_utils import run_bass_kernel_spmd

F32 = mybir.dt.float32
BF16 = mybir.dt.bfloat16
AF = mybir.ActivationFunctionType
ALU = mybir.AluOpType
AX = mybir.AxisListType

S = 2048
D = 1024
NT = 16
CDEC = math.exp(-0.5)
NORM_EPS = 1e-6
LNX_EPS = 64e-5
SLOPES = [0.25, 0.0625, 0.015625, 0.00390625, 0.5, 0.125]
LAM_INIT1 = 0.8 - 0.6 * math.exp(-0.3 * 1)

NDMA_SLOTS = 8
COMPUTE = ("pe", "act", "dve", "pool")
QUEUES = ("sp", "actq", "poolq")
STREAM_OF = {"pe": "tensor", "act": "scalar", "dve": "vector", "pool": "gpsimd",
             "sp": "sync", "actq": "scalar", "poolq": "gpsimd"}
STREAMS = ["tensor", "scalar", "vector", "gpsimd", "sync"]


class Prog:
    def __init__(self, nc):
        self.nc = nc
        self.es = ExitStack()
        self.ops = []
        self.last_w = {}
        self.readers = {}
        self.ndma = {q: 0 for q in QUEUES}
        self.last_slot = {}
        self.barrier_deps = []
        self.nbank = 0
        self.nbank2 = 0
        self.bank_rng = (0, 8)
        self.pending = {}
        self.last_q = {}
        self.seg = 0
        self.banks = None

    def sb(self, name, shape, dt=F32):
        return self.es.enter_context(self.nc.sbuf_tensor(name, list(shape), dt))

    def ps(self, name, shape, dt=F32):
        return self.es.enter_context(self.nc.psum_tensor(name, list(shape), dt))

    def bank(self, lo=None, hi=None, nr=1):
        if lo is None:
            lo, hi = self.bank_rng
        n = hi - lo
        for t in range(n):
            i = lo + ((self.nbank + t) % n)
            key = "bank%d" % i
            if self.pending.get(key, 0) <= 0:
                self.nbank += t + 1
                self.pending[key] = nr
                return self.banks[i], key
        raise RuntimeError("no free PSUM bank in [%d,%d): %s" % (lo, hi, self.pending))

    def bank2(self, lo=0, hi=8):
        n = (hi - lo) // 2
        i = lo + 2 * (self.nbank2 % n)
        self.nbank2 += 1
        return self.psall[:, i:i + 2, :], ["bank%d" % i, "bank%d" % (i + 1)]

    def barrier(self):
        self.seg += 1
        last = {}
        for i, o in enumerate(self.ops):
            last[o["veng"]] = i
        self.barrier_deps = sorted(last.values())

    def op(self, eng, fn, reads=(), writes=(), dur=None):
        idx = len(self.ops)
        if eng != "pe":
            for k in reads:
                if k in self.pending and self.pending[k] > 0:
                    self.pending[k] -= 1
        raw = set(self.barrier_deps) if not SCHED else set()
        oth = set()
        for k in reads:
            if k in self.last_w:
                raw.add(self.last_w[k])
        for k in writes:
            if k in self.last_w:
                oth.add(self.last_w[k])
            for r in self.readers.get(k, ()):
                oth.add(r)
        if eng in QUEUES:
            n = self.ndma[eng]
            self.ndma[eng] += 1
            veng = (eng, n % NDMA_SLOTS)
            if veng in self.last_slot:
                raw.add(self.last_slot[veng])
            self.last_slot[veng] = idx
        else:
            veng = (eng, 0)
        deps = set(raw)
        odeps = set()
        for d in oth:
            if self.ops[d]["veng"] == veng and eng in COMPUTE:
                odeps.add(d)
                continue
            deps.add(d)
        if eng in QUEUES:
            if eng in self.last_q and not (eng == "sp" and FREE_SP):
                odeps.add(self.last_q[eng])
            self.last_q[eng] = idx
        deps.discard(idx)
        odeps.discard(idx)
        if dur is None:
            dur = {"pe": 0.11, "act": 0.5, "dve": 0.6, "pool": 1.0}.get(eng, 3.0)
        self.ops.append(dict(eng=eng, veng=veng, fn=fn, deps=sorted(deps), odeps=sorted(odeps - deps),
                             stream=STREAM_OF[eng], seg=self.seg, dur=dur))
        for k in reads:
            self.readers.setdefault(k, []).append(idx)
        for k in writes:
            self.last_w[k] = idx
            self.readers[k] = []
        return idx

    @staticmethod
    def _n(ap):
        n = 1
        for d in ap.shape[1:]:
            n *= d
        return n

    def _dur(self, eng, out):
        n = self._n(out)
        if eng == "pe":
            return PEF * (0.06 + 0.0004 * n)
        if eng == "act":
            return ACTF * (0.22 + 0.00075 * n)
        if eng == "dve":
            return DVEF * (0.2 + 0.00105 * n)
        return POOLF * (0.3 + 0.0021 * n)

    def tt(self, eng, out, in0, in1, op, r, w):
        d = self._dur(eng, out) if op != ALU.pow else 2.3
        return self.op(eng, lambda e: e.tensor_tensor(out=out, in0=in0, in1=in1, op=op), r, w, dur=d)

    def ts(self, eng, out, in0, s1, s2, op0, op1, r, w):
        if op1 is None:
            return self.op(eng, lambda e: e.tensor_scalar(out=out, in0=in0, scalar1=s1, scalar2=None, op0=op0), r, w,
                           dur=self._dur(eng, out))
        return self.op(eng, lambda e: e.tensor_scalar(out=out, in0=in0, scalar1=s1, scalar2=s2, op0=op0, op1=op1), r, w,
                       dur=self._dur(eng, out))

    def stt(self, out, in0, scalar, in1, op0, op1, r, w):
        return self.op("dve", lambda e: e.scalar_tensor_tensor(out=out, in0=in0, scalar=scalar, in1=in1, op0=op0, op1=op1), r, w,
                       dur=self._dur("dve", out))

    def act(self, out, in_, func, r, w, bias=None, scale=None, accum_out=None):
        kw = {}
        if bias is not None:
            kw["bias"] = bias
        if scale is not None:
            kw["scale"] = scale
        if accum_out is not None:
            kw["accum_out"] = accum_out
        return self.op("act", lambda e: e.activation(out=out, in_=in_, func=func, **kw), r, w,
                       dur=self._dur("act", out) + (0.1 if accum_out is not None else 0.0))

    def copy(self, eng, out, in_, r, w):
        if eng == "act":
            return self.op("act", lambda e: e.activation(out=out, in_=in_, func=AF.Copy), r, w, dur=self._dur("act", out))
        return self.op(eng, lambda e: e.tensor_copy(out=out, in_=in_), r, w, dur=self._dur(eng, out))

    def mm(self, out, lhsT, rhs, start, stop, r, w):
        return self.op("pe", lambda e: e.matmul(out, lhsT=lhsT, rhs=rhs, start=start, stop=stop), r, w,
                       dur=self._dur("pe", out))

    def tr(self, out, in_, ident, r, w):
        return self.op("pe", lambda e: e.transpose(out=out, in_=in_, identity=ident), r, w, dur=0.1)

    def dma(self, q, out, in_, r, w, **kw):
        return self.op(q, lambda e: e.dma_start(out=out, in_=in_, **kw), r, w)

    def memset(self, eng, ap, val, w):
        return self.op(eng, lambda e: e.memset(ap, val), (), w, dur=self._dur(eng, ap))

    def schedule(self):
        ops = self.ops
        n = len(ops)
        succ = [[] for _ in range(n)]
        npred = [0] * n
        for i, o in enumerate(ops):
            preds = set(o["deps"]) | set(o["odeps"])
            o["preds"] = preds
            for d in preds:
                succ[d].append(i)
            npred[i] = len(preds)
        prio = [0.0] * n
        for i in range(n - 1, -1, -1):
            m = 0.0
            for q in succ[i]:
                if prio[q] > m:
                    m = prio[q]
            prio[i] = ops[i]["dur"] + m
        finish = [0.0] * n
        order = []
        nseg = self.seg + 1
        by_seg = [[] for _ in range(nseg)]
        for i, o in enumerate(ops):
            by_seg[o["seg"]].append(i)
        tbase = 0.0
        chan_free = [0.0]
        LAT = LATF
        for sg in range(nseg):
            stream_free = {st: tbase for st in STREAMS}
            ready = {st: [] for st in STREAMS}
            rtime = {}
            left = len(by_seg[sg])
            in_seg = set(by_seg[sg])
            for i in by_seg[sg]:
                cnt = 0
                for d in ops[i]["preds"]:
                    if d in in_seg:
                        cnt += 1
                npred[i] = cnt
                if cnt == 0:
                    rtime[i] = tbase
                    ready[ops[i]["stream"]].append(i)
            while left:
                best = None
                for st in STREAMS:
                    rl = ready[st]
                    if not rl:
                        continue
                    sf = stream_free[st]
                    bi = None
                    bk = None
                    for i in rl:
                        t0 = rtime[i]
                        if t0 < sf:
                            t0 = sf
                        k = (int(t0 / QUANT), -prio[i])
                        if bk is None or k < bk:
                            bk = k
                            bi = i
                    if best is None or bk < best[0]:
                        best = (bk, bi, st)
                _, i, st = best
                t0 = max(rtime[i], stream_free[st])
                ready[st].remove(i)
                o = ops[i]
                if o["eng"] in QUEUES:
                    stream_free[st] = t0 + 0.08
                    if FREE_SP:
                        tx = max(t0 + 0.3, chan_free[0])
                        finish[i] = tx + o["dur"] * DMAX
                        chan_free[0] = finish[i]
                    else:
                        finish[i] = t0 + o["dur"]
                else:
                    stream_free[st] = t0 + o["dur"]
                    finish[i] = t0 + o["dur"]
                order.append(i)
                left -= 1
                for q in succ[i]:
                    if q not in in_seg:
                        continue
                    npred[q] -= 1
                    if npred[q] == 0:
                        rt = tbase
                        oq = ops[q]
                        for d in oq["preds"]:
                            if d in in_seg:
                                lat = 0.0 if ops[d]["stream"] == oq["stream"] and d in oq["odeps"] else LAT
                                if ops[d]["stream"] == oq["stream"] and d not in oq["odeps"]:
                                    lat = 0.2
                                f = finish[d] + lat
                                if f > rt:
                                    rt = f
                        rtime[q] = rt
                        ready[oq["stream"]].append(q)
            tbase = max([tbase] + [finish[i] for i in by_seg[sg]]) + 2.0
        self.sim_time = tbase
        return order

    def emit(self, final_wait_ops=()):
        nc = self.nc
        ops = self.ops
        if SCHED:
            order = self.schedule()
        else:
            order = list(range(len(ops)))
        cnt = {}
        for i in order:
            o = ops[i]
            v = o["veng"]
            cnt[v] = cnt.get(v, 0) + 1
            o["cnt"] = cnt[v]
        vengs = sorted(cnt.keys())
        sems = {}
        for v in vengs:
            sems[v] = self.es.enter_context(nc.semaphore("s_%s_%d" % v))
        mult = {v: (16 if v[0] in QUEUES else 1) for v in vengs}
        seen = {s: {} for s in STREAMS}
        know = [None] * len(ops)
        plan = {s: [] for s in STREAMS}
        nwaits = 0
        cur_seg = 0
        last_of_veng = {}
        seg_barrier = {}
        for i in order:
            o = ops[i]
            s = o["stream"]
            sd = seen[s]
            deps = list(o["deps"])
            if SCHED:
                if o["seg"] != cur_seg:
                    cur_seg = o["seg"]
                    seg_barrier = dict(last_of_veng)
                if o["seg"] > 0:
                    deps += [d for d in seg_barrier.values()]
            waits = {}
            for d in sorted(set(deps), reverse=True):
                od = ops[d]
                v = od["veng"]
                if sd.get(v, 0) >= od["cnt"]:
                    continue
                waits[v] = max(waits.get(v, 0), od["cnt"])
                for kv, kc in know[d].items():
                    if sd.get(kv, 0) < kc:
                        sd[kv] = kc
                if sd.get(v, 0) < od["cnt"]:
                    sd[v] = od["cnt"]
            o["waits"] = waits
            nwaits += len(waits)
            know[i] = dict(sd)
            plan[s].append(i)
            last_of_veng[o["veng"]] = i
        self.nwaits = nwaits
        blk = self.es.enter_context(nc.Block())

        def make(sname):
            def body(eng):
                for i in plan[sname]:
                    o = ops[i]
                    for v, c in o["waits"].items():
                        eng.wait_ge(sems[v], c * mult[v])
                    ins = o["fn"](eng)
                    ins.then_inc(sems[o["veng"]], mult[o["veng"]])
                if sname == "sync":
                    for i in final_wait_ops:
                        o = ops[i]
                        eng.wait_ge(sems[o["veng"]], o["cnt"] * mult[o["veng"]])
            return body

        for sname in STREAMS:
            if not plan[sname] and sname != "sync":
                continue
            getattr(blk, sname)(make(sname))


class Arena:
    def __init__(self, P, name, nbytes):
        self.cap = nbytes
        self.t = P.sb(name, [128, nbytes // 4], F32)
        self.off = 0

    def reset(self):
        self.off = 0

    def alloc(self, shape, dt=F32, parts=128):
        n = 1
        for s in shape:
            n *= s
        nb = n * (2 if dt == BF16 else 4)
        nb32 = (nb + 31) // 32 * 32
        a = self.off
        self.off += nb32
        self.peak = max(getattr(self, "peak", 0), self.off)
        if self.off > self.cap:
            print("ARENA OVERFLOW", self.off, self.cap)
        v = self.t[0:parts, a // 4:(a + nb32) // 4]
        if dt == BF16:
            v = v.bitcast(BF16)
        v = v[:, 0:n]
        if len(shape) == 2:
            v = v.rearrange("p (a b) -> p a b", a=shape[0], b=shape[1])
        elif len(shape) == 3:
            v = v.rearrange("p (a b c) -> p a b c", a=shape[0], b=shape[1], c=shape[2])
        return v


def rsqrt_op(P, out, in_, mh, rk, wk, tmp=None):
    if 'nopow' in TOG:
        P.act(out, in_, AF.Sqrt, rk, wk)
        P.op("dve", lambda e: e.reciprocal(out=out, in_=out), wk, wk)
    else:
        P.tt("pool", out, in_, mh, ALU.pow, list(rk) + ["mhalf"], wk)


def bc(ap, shape):
    return ap.broadcast_to(list(shape))


def build(dbg=False, nlayers=2):
    nc = bass.Bass("TRN2", target_bir_lowering=False)
    P = Prog(nc)

    def din(name, shape):
        return nc.dram_tensor(name, list(shape), F32, kind="ExternalInput").ap()

    x_d = din("x", [S, D])
    mem_d = din("mem", [256, D])
    win_d = din("a_w_in", [D, 3712])
    wout_d = din("w_out", [2, D, D])
    wmkv_d = din("w_mem_kv", [2, D, 512])
    wkv_d = din("w_kv", [D, 1536])
    bwin_d = din("b_w_in", [D, 2048])
    gains_d = din("gains", [128, 5, 8])
    mu64_d = din("mu64", [64, 26])
    muv_d = din("muv_b", [128, 768])
    w2a_d = din("w2aug", [65, 768])
    a2a_d = din("a2aug", [65, 768])
    hp_d = din("headp", [64, 3, 12])
    lnx_d = din("lnx_b", [128, 2, 768])
    postg_d = din("postg_b", [128, 2, 1024])
    subln_d = din("subln_b", [128, 128])
    lamv_d = din("lamv_b", [128, 4, 64])
    out_d = nc.dram_tensor("out", [S, D], F32, kind="ExternalOutput").ap()
    if dbg:
        dbg_d = nc.dram_tensor("dbg", [S, D], F32, kind="ExternalOutput").ap()
    class WS:
        def __init__(self, name, groups):
            self.groups = groups
            self.pieces = {}
            idx = 0
            for (ca, cb, w) in groups:
                for c0 in range(ca, cb, w):
                    self.pieces[c0] = (idx, w)
                    idx += 1
            self.t = nc.dram_tensor(name, [idx, 128, 8, 256], BF16).ap()
            self.name = name

        def cast(self, src, order=None):
            order = order if order is not None else list(range(len(self.groups)))
            for gi in order:
                ca, cb, w = self.groups[gi]
                J = (cb - ca) // w
                j0 = self.pieces[ca][0]
                for kc in range(8):
                    dst = self.t[j0:j0 + J, :, kc, 0:w].rearrange("j p c -> p j c")
                    srcv = src[kc * 128:(kc + 1) * 128, ca:cb].rearrange("p (j c) -> p j c", c=w)
                    P.dma("poolq", dst, srcv, (), ["%s_%d_%d" % (self.name, kc, gi)])
            return self.keymap()

        def keymap(self):
            km = {}
            for gi, (ca, cb, w) in enumerate(self.groups):
                for c0 in range(ca, cb, w):
                    km[c0] = ["%s_%d_%d" % (self.name, kc, gi) for kc in range(8)]
            return km

    ws_in = WS("ws_in", [(0, 768, 256), (768, 1536, 256), (1536, 2304, 256), (2304, 2432, 128), (2432, 3712, 256)])
    ws_out = [WS("ws_out%d" % l, [(0, 1024, 256)]) for l in range(2)]
    ws_mkv = [WS("ws_mkv%d" % l, [(0, 512, 256)]) for l in range(2)]
    ws_kv = WS("ws_kv", [(0, 1536, 256)])
    ws_bin = WS("ws_bin", [(0, 2048, 256)])

    psall = P.ps("psall", [128, 8, 512], F32)
    P.psall = psall
    P.banks = [psall[:, i, :] for i in range(8)]

    x_res = P.sb("x_res", [128, NT, D], F32)
    NWST = 4
    Wst = [P.sb("wst%d" % i, [128, 8, 256], BF16) for i in range(NWST)]
    wst_i = [0]
    cst = Arena(P, "cst", 22016)
    identf = cst.alloc([128], F32)
    ident = cst.alloc([128], BF16)
    maskSU = cst.alloc([128], BF16)
    maskU = cst.alloc([128], BF16)
    maskSL = cst.alloc([128], BF16)
    mtmp = cst.alloc([128], F32)
    ones64 = cst.alloc([64], F32, parts=64)
    onescol = cst.alloc([2], BF16, parts=64)
    cmask = cst.alloc([512], F32, parts=64)
    epsn = cst.alloc([1], F32)
    epsl = cst.alloc([1], F32)
    mhalf = cst.alloc([12], F32)
    rkc = cst.alloc([12, 2], BF16, parts=64)
    gains = cst.alloc([5, 8], F32)
    mu64 = cst.alloc([26], F32, parts=64)
    muv_b = cst.alloc([768], F32)
    w2aug = cst.alloc([768], BF16, parts=65)
    a2aug = cst.alloc([768], BF16, parts=65)
    headp = cst.alloc([3, 12], F32, parts=64)
    postg_b = cst.alloc([1024], F32)
    subln_b = cst.alloc([128], F32)
    lamc = cst.alloc([8], F32)
    KmT = [cst.alloc([4, 256], BF16, parts=64) for _ in range(2)]
    Vma = [cst.alloc([2, 4, 65], BF16) for _ in range(2)]

    k_in = ws_in.cast(win_d, order=[0, 1])
    for c0 in ws_in.pieces:
        if c0 >= 1536:
            k_in[c0] = ["ws_in_p%d" % c0]
    k_mkv = [ws_mkv[l].cast(wmkv_d[l]) for l in range(2)]
    k_out = [ws_out[0].cast(wout_d[0]), ws_out[1].keymap()]
    k_kv = ws_kv.keymap()
    k_bin = ws_bin.keymap()

    def late_precasts():
        ws_out[1].cast(wout_d[1])
        ws_kv.cast(wkv_d)
        ws_bin.cast(bwin_d)

    wst_list = [list(Wst)]

    resident = [{}]

    def stage(scr, keys, c0, n):
        if (scr.name, c0) in resident[0]:
            return resident[0][(scr.name, c0)]
        Wst_ = wst_list[0]
        NW_ = len(Wst_)
        return _stage(scr, keys, c0, n, Wst_, NW_)

    boot = [False]
    boot_n = [0]

    def _stage(scr, keys, c0, n, Wst, NWST):
        i = wst_i[0] % NWST
        assert n <= 256
        wst_i[0] += 1
        t = Wst[i]
        pi, w_ = scr.pieces[c0]
        assert n <= w_
        if scr is ws_in and boot[0] and c0 >= 1536:
            b_ = boot_n[0] % 4
            boot_n[0] += 1
            sb_ = x_res[:, 8 + 2 * b_:10 + 2 * b_, :].rearrange("p a (k c) -> p (a k) c", c=256)
            skeys = ["x%d" % (8 + 2 * b_), "x%d" % (9 + 2 * b_)]
            P.dma("sp", sb_[:, :, 0:n], win_d[:, c0:c0 + n].rearrange("(kc p) c -> p kc c", p=128), (), skeys)
            P.copy("act" if boot_n[0] % 2 else "dve", t[:, :, 0:n], sb_[:, :, 0:n], skeys, ["wst%d" % i])
            P.dma("sp", scr.t[pi][:, :, 0:n], t[:, :, 0:n], ["wst%d" % i], ["ws_in_p%d" % c0])
            return t, "wst%d" % i
        P.dma("sp", t[:, :, 0:n], scr.t[pi][:, :, 0:n], keys[c0], ["wst%d" % i])
        return t, "wst%d" % i

    P.memset("pool", identf, 1.0, ["identf"])
    P.op("pool", lambda e: e.affine_select(out=identf, in_=identf, pattern=[[1, 128]], compare_op=ALU.is_equal,
                                           fill=0.0, base=0, channel_multiplier=-1), ["identf"], ["identf"])
    P.copy("dve", ident, identf, ["identf"], ["ident"])
    for mk, name, pat, cm, cop in ((maskSU, "maskSU", 1, -1, ALU.is_gt), (maskU, "maskU", 1, -1, ALU.is_ge),
                                   (maskSL, "maskSL", -1, 1, ALU.is_gt)):
        P.memset("pool", mtmp, 1.0, ["mtmp"])
        P.op("pool", lambda e, pat=pat, cm=cm, cop=cop: e.affine_select(
            out=mtmp, in_=mtmp, pattern=[[pat, 128]], compare_op=cop, fill=0.0, base=0, channel_multiplier=cm),
            ["mtmp"], ["mtmp"])
        P.copy("pool", mk, mtmp, ["mtmp"], [name])
    P.memset("pool", ones64, 1.0, ["ones64"])
    P.memset("pool", onescol, 1.0, ["onescol"])
    P.memset("pool", cmask, 1.0, ["cmask"])
    P.memset("pool", cmask.rearrange("p (a b) -> p a b", a=4)[:, :, 0:1], 0.0, ["cmask"])
    P.memset("pool", epsn, NORM_EPS, ["epsn"])
    P.memset("pool", epsl, LNX_EPS, ["epsl"])
    P.memset("pool", mhalf, -0.5, ["mhalf"])
    P.dma("sp", gains, gains_d, (), ["gains"])
    P.dma("sp", mu64, mu64_d, (), ["mu64"])
    P.dma("sp", muv_b, muv_d, (), ["muv_b"])
    P.dma("poolq", w2aug, w2a_d, (), ["w2aug"])
    P.dma("poolq", a2aug, a2a_d, (), ["a2aug"])
    P.dma("sp", headp, hp_d, (), ["headp"])
    P.dma("sp", subln_b, subln_d, (), ["subln_b"])
    def load_x(i):
        P.dma("sp", x_res[:, i, :], x_d[i * 128:(i + 1) * 128, :], (), ["x%d" % i])
    for i in range(2):
        load_x(i)

    arena = Arena(P, "arena", 108864)

    def rmsnorm_T(src, src_keys, outs, A):
        xb, junk, ss, rt, rstd = A["xb"], A["junk"], A["ss"], A["rt"], A["rstd"]
        P.act(xb, src, AF.Square, src_keys, ["xb", "ss"], accum_out=ss)
        P.ts("dve", rt, ss, 1.0 / D, NORM_EPS, ALU.mult, ALU.add, ["ss"], ["rt"])
        rsqrt_op(P, rstd, rt, mhalf[:, 0:1], ["rt"], ["rstd"])
        P.act(xb, src, AF.Copy, list(src_keys) + ["rstd"], ["xb"], scale=rstd)
        bk, bkey = P.bank(nr=len(outs))
        bv = bk[:, 0:512].bitcast(BF16).rearrange("p (a b) -> p a b", a=8)
        for kc in range(8):
            P.tr(bv[:, kc, :], xb[:, kc * 128:(kc + 1) * 128], ident, ["xb", "ident"], [bkey])
        for gi, oap, okey in outs:
            P.tt("dve", oap, bv, bc(gains[:, gi, :].unsqueeze(2), [128, 8, 128]), ALU.mult,
                 [bkey, "gains"], [okey])

    def mem_kv(l, A):
        memt, memT = A["memt"], A["memT"]
        wt, wkey = stage(ws_mkv[l], k_mkv[l], 0, 256)
        wt2, wkey2 = stage(ws_mkv[l], k_mkv[l], 256, 256)
        for mt in range(2):
            P.dma("sp", memt, mem_d[mt * 128:(mt + 1) * 128, :], (), ["memt"])
            rmsnorm_T(memt, ["memt"], [(2 + l, memT[:, :, mt * 128:(mt + 1) * 128], "memT")], A)
        for hp in range(2):
            bk, bkey = P.bank()
            for hh in range(2):
                h = hp * 2 + hh
                for kc in range(8):
                    P.mm(bk[0:64, hh * 256:(hh + 1) * 256], wt[:, kc, h * 64:(h + 1) * 64], memT[:, kc, :],
                         kc == 0, kc == 7, [wkey, "memT"], [bkey])
            P.copy("act", KmT[l][:, hp * 2:(hp + 1) * 2, :], bk[0:64, :].rearrange("p (a b) -> p a b", a=2),
                   [bkey], ["KmT%d" % l])
        P.memset("pool", Vma[l], 2.0, ["Vma%d" % l])
        for mt in range(2):
            bk, bkey = P.bank()
            for kc in range(8):
                P.mm(bk[:, 0:256], memT[:, kc, mt * 128:(mt + 1) * 128], wt2[:, kc, 0:256], kc == 0, kc == 7,
                     [wkey2, "memT"], [bkey])
            P.copy("act", Vma[l][:, mt, :, 0:64], bk[:, 0:256].rearrange("p (a b) -> p a b", a=4),
                   [bkey, "Vma%d" % l], ["Vma%d" % l])

    def mem_attn(l, A, qmT, sg, YG, qk="qmT", sk="sg", yk="xb"):
        PmT, ym, rz = A["PmT"], A["ym"], A["rz"]
        for mt in range(2):
            bk, bkey = P.bank()
            for h in range(4):
                P.mm(bk[:, h * 128:(h + 1) * 128], KmT[l][:, h, mt * 128:(mt + 1) * 128], qmT[:, h, :], True, True,
                     ["KmT%d" % l, qk], [bkey])
            P.act(PmT[mt], bk[:, :].rearrange("p (a b) -> p a b", a=4), AF.Exp, [bkey], ["PmT%d" % mt], scale=0.125)
        bk, bkey = P.bank(nr=2)
        ov = bk[:, 0:260].rearrange("p (a b) -> p a b", a=4)
        for h in range(4):
            for mt in range(2):
                P.mm(ov[:, h, :], PmT[mt][:, h, :], Vma[l][:, mt, h, :], mt == 0, mt == 1,
                     ["PmT%d" % mt, "Vma%d" % l], [bkey])
        P.op("dve", lambda e: e.reciprocal(out=rz, in_=ov[:, :, 64:65]), [bkey], ["rz"])
        P.tt("dve", ym, ov[:, :, 0:64], bc(rz, [128, 4, 64]), ALU.mult, [bkey, "rz"], ["ym"])
        P.tt("dve", YG[:, 768:1024], ym.rearrange("p a b -> p (a b)"), sg[:, 768:1024], ALU.mult, ["ym", sk], [yk])

    def out_proj(l, i, A, YG, final, yk="xb"):
        YGT, otmp, ssa, rt, rstd = A["YGT"], A["otmp"], A["ssa"], A["rt2"], A["rstd2"]
        junk = A["junk"]
        bk, bkey = P.bank()
        bv = bk[:, 0:512].bitcast(BF16).rearrange("p (a b) -> p a b", a=8)
        for kc in range(8):
            P.tr(bv[:, kc, :], YG[:, kc * 128:(kc + 1) * 128], ident, [yk, "ident"], [bkey])
        P.copy("act", YGT, bv, [bkey], ["YGT"])
        obk = []
        for cg in range(2):
            bk, bkey = P.bank(nr=2)
            for hf in range(2):
                wt, wkey = stage(ws_out[l], k_out[l], cg * 512 + hf * 256, 256)
                for kc in range(8):
                    P.mm(bk[:, hf * 256:(hf + 1) * 256], YGT[:, kc, :], wt[:, kc, :], kc == 0, kc == 7, ["YGT", wkey], [bkey])
            P.act(junk[:, 0:512], bk[:, :], AF.Square, [bkey], ["otmp0", "ssa%d" % cg], accum_out=ssa[:, cg:cg + 1])
            obk.append((bk, bkey))
        P.tt("dve", ssa[:, 2:3], ssa[:, 0:1], ssa[:, 1:2], ALU.add, ["ssa0", "ssa1"], ["ssa2"])
        P.ts("dve", rt, ssa[:, 2:3], 1.0 / D, NORM_EPS, ALU.mult, ALU.add, ["ssa2"], ["rt2"])
        rsqrt_op(P, rstd, rt, mhalf[:, 0:1], ["rt2"], ["rstd2"])
        xk = "x%d" % i
        for cg in range(2):
            bk, bkey = obk[cg]
            P.stt(otmp[cg], bk[:, :], rstd, postg_b[:, cg * 512:(cg + 1) * 512], ALU.mult, ALU.mult,
                  [bkey, "rstd2", "postg_b"], ["otmp%d" % cg])
            P.tt("pool", x_res[:, i, cg * 512:(cg + 1) * 512], x_res[:, i, cg * 512:(cg + 1) * 512], otmp[cg],
                 ALU.add, [xk, "otmp%d" % cg], [xk])
        if final:
            return P.dma("sp", out_d[i * 128:(i + 1) * 128, :], x_res[:, i, :], [xk], [])
        return None

    def tm_job(scr, keys, c0, n, lhs_list, nrs):
        wt, wkey = stage(scr, keys, c0, n)
        res = []
        for li, (hv, hkey) in enumerate(lhs_list):
            bk, bkey = P.bank(nr=(nrs[li] if nrs else 1))
            for kc in range(8):
                P.mm(bk[:, 0:n], hv[:, kc, :], wt[:, kc, 0:n], kc == 0, kc == 7, [hkey, wkey], [bkey])
            res.append((bk, bkey))
        return res

    def fm64_job(scr, keys, c0, hv, hkey):
        wt, wkey = stage(scr, keys, c0, 256)
        bk, bkey = P.bank()
        for g in range(4):
            for kc in range(8):
                P.mm(bk[0:64, g * 128:(g + 1) * 128], wt[:, kc, g * 64:(g + 1) * 64], hv[:, kc, :], kc == 0, kc == 7,
                     [hkey, wkey], [bkey])
        return bk, bkey

    A = {}
    A["xb"] = arena.alloc([1024], BF16)
    for nm in ("ss", "rt", "rstd", "rt2", "rstd2"):
        A[nm] = arena.alloc([1], F32)
    A["ssa"] = arena.alloc([4], F32)
    A["PmT"] = [arena.alloc([4, 128], BF16) for _ in range(2)]
    A["ym"] = arena.alloc([4, 64], F32)
    A["rz"] = arena.alloc([4, 1], F32)
    A["YGT"] = arena.alloc([8, 128], BF16)
    A["otmp"] = [arena.alloc([512], F32) for _ in range(2)]
    A["junk"] = A["otmp"][0].bitcast(BF16)
    common_off = arena.off
    A["memt"] = arena.alloc([1024], F32)
    A["memT"] = arena.alloc([8, 256], BF16)
    for l in range(2):
        mem_kv(l, A)
    P.barrier()
    arena.off = common_off

    lnx_b = arena.alloc([2, 768], F32)
    P.dma("sp", lnx_b, lnx_d, (), ["lnx_b"])
    hT = arena.alloc([8, 129], BF16)
    carry = arena.alloc([26], F32, parts=64)
    Zg = arena.alloc([8, 129], F32, parts=64)
    Zd = arena.alloc([8, 128], F32, parts=64)
    Zw = arena.alloc([2, 129], F32, parts=64)
    Zwd = arena.alloc([2, 128], F32, parts=64)
    twd = arena.alloc([128], BF16, parts=65)
    adb = arena.alloc([128], BF16, parts=65)
    sgw, av, Lp, kk, t2, t3 = [arena.alloc([4, 128], F32, parts=64) for _ in range(6)]
    eP, ePi, ePm = [arena.alloc([4, 128], BF16, parts=64) for _ in range(3)]
    DS = []
    for sl in range(3):
        d = {}
        for nm in ("aT", "rT", "bT", "kT", "rkT"):
            d[nm] = arena.alloc([4, 128], BF16, parts=64)
        d["PCt"] = arena.alloc([4, 1], F32, parts=64)
        DS.append(d)
    IO = []
    for sl in range(2):
        d = {}
        for nm in ("Tm0", "AakT", "ArbT", "ArkT"):
            d[nm] = arena.alloc([4, 128], BF16)
        d["BK"] = arena.alloc([8, 64], BF16)
        IO.append(d)
    PQs = [arena.alloc([2, 4, 128], BF16) for _ in range(2)]
    Tm1s = arena.alloc([4, 128], BF16)
    R1s = arena.alloc([4, 64], BF16)
    Ub = arena.alloc([4, 64], BF16)
    bs2 = [arena.alloc([12, 1], F32) for _ in range(2)]
    vps = arena.alloc([768], F32)
    Vt2 = [arena.alloc([768], BF16) for _ in range(2)]
    sg2 = [arena.alloc([1024], BF16) for _ in range(2)]
    qmT2 = [arena.alloc([4, 128], BF16, parts=64) for _ in range(2)]
    Mf = arena.alloc([12, 64], F32, parts=64)
    Mb = arena.alloc([12, 64], BF16, parts=64)
    Y2 = [arena.alloc([12, 64], F32) for _ in range(2)]
    Ysq = arena.alloc([12, 64], F32)
    st = arena.alloc([8, 12], F32)
    YG0 = arena.alloc([1024], BF16)

    if nlayers >= 1:
        P.dma("sp", postg_b, postg_d[:, 0, :], (), ["postg_b"])
        P.memset("pool", carry, 0.0, ["carry"])
        P.memset("pool", Mf, 0.0, ["Mf"])
        P.memset("pool", Mb, 0.0, ["Mb"])
        P.memset("pool", hT, 0.0, ["hT"])
        P.memset("pool", twd, 1.0, ["twd"])
        P.memset("pool", adb, 1.0, ["adb"])
    kk64 = headp[:, 0, :]
    ka64 = headp[:, 1, :]
    rk64 = headp[:, 2, :]
    if nlayers >= 1:
        P.ts("dve", lnx_b, lnx_b, 0.5, None, ALU.mult, None, ["lnx_b"], ["lnx_b"])
        P.ts("dve", rkc, bc(rk64.unsqueeze(2), [64, 12, 2]), 0.5, None, ALU.mult, None, ["headp"], ["rkc"])
    flat = lambda t: t.rearrange("p a b -> p (a b)")
    v4 = lambda t, a=4: t.rearrange("p (a b) -> p a b", a=a)

    def pre(i):
        boot[0] = (i == 0)
        xk = "x%d" % i
        tp = i % 2
        Vt, sg, qmT = Vt2[tp], sg2[tp], qmT2[tp]
        vk, sk, qk = "Vt%d" % tp, "sg%d" % tp, "qmT%d" % tp
        P.copy("pool", hT[:, :, 0:1], hT[:, :, 128:129], ["hT"], ["hT"])
        yield
        rmsnorm_T(x_res[:, i, :], [xk], [(0, hT[:, :, 1:129], "hT")], A)
        yield
        hcur = (hT[:, :, 1:129], "hT")
        hprev = (hT[:, :, 0:128], "hT")
        for q in range(3):
            (v1, v1k), (vp1, vp1k) = tm_job(ws_in, k_in, 1536 + q * 256, 256, [hcur, hprev], [2, 1])
            yield
            cs = slice(q * 256, (q + 1) * 256)
            P.copy("act", vps[:, cs], vp1[:, 0:256], [vp1k], ["vps%d" % q])
            P.tt("dve", vps[:, cs], vps[:, cs], v1[:, 0:256], ALU.subtract, ["vps%d" % q, v1k], ["vps%d" % q])
            yield
            P.tt("pool", vps[:, cs], vps[:, cs], muv_b[:, cs], ALU.mult, ["vps%d" % q, "muv_b"], ["vps%d" % q])
            P.tt("dve", Vt[:, cs], v1[:, 0:256], vps[:, cs], ALU.add, ["vps%d" % q, v1k], [vk])
            yield
        for q in range(4):
            c0 = 2432 + q * 256 if q < 3 else 3456
            (g1, g1k), = tm_job(ws_in, k_in, c0, 256, [hcur], [2])
            P.act(sg[:, q * 256:(q + 1) * 256], g1[:, 0:256], AF.Tanh, [g1k], [sk], scale=0.5)
            P.stt(sg[:, q * 256:(q + 1) * 256], sg[:, q * 256:(q + 1) * 256], 1.0, g1[:, 0:256], ALU.add, ALU.mult,
                  [sk, g1k], [sk])
            yield
        qb, qbk = fm64_job(ws_in, k_in, 3200, hT[:, :, 1:129], "hT")
        P.copy("act", qmT, v4(qb[0:64, :]), [qbk], [qk])
        yield
        wt, wkey = stage(ws_in, k_in, 2304, 128)
        bk, bkey = P.bank()
        for g in range(2):
            for kc in range(8):
                P.mm(bk[0:64, g * 128:(g + 1) * 128], wt[:, kc, g * 64:(g + 1) * 64], hT[:, kc, 1:129], kc == 0, kc == 7,
                     ["hT", wkey], [bkey])
        P.copy("pool", Zw[:, :, 0:1], carry[:, 24:26].unsqueeze(2), ["carry"], ["Zw"])
        P.copy("act", Zw[:, :, 1:129], v4(bk[0:64, 0:256], 2), [bkey], ["Zw"])
        yield
        P.tt("pool", Zwd, Zw[:, :, 0:128], Zw[:, :, 1:129], ALU.subtract, ["Zw"], ["Zwd"])
        P.tt("pool", Zwd, Zwd, bc(mu64[:, 24:26].unsqueeze(2), [64, 2, 128]), ALU.mult, ["Zwd", "mu64"], ["Zwd"])
        P.tt("dve", Zwd, Zw[:, :, 1:129], Zwd, ALU.add, ["Zw", "Zwd"], ["Zwd"])
        P.copy("pool", carry[:, 24:26].unsqueeze(2), Zw[:, :, 128:129], ["Zw"], ["carry"])
        yield
        P.act(twd[0:64, :], Zwd[:, 0, :], AF.Tanh, ["Zwd"], ["twd"])
        P.copy("act", adb[0:64, :], Zwd[:, 1, :], ["Zwd"], ["adb"])
        yield

    def dphase(i, g, sl):
        boot[0] = (i == 0)
        S_ = DS[sl]
        sk_ = lambda nm: "%s_%d" % (nm, sl)
        aT, rT, bT, kT, rkT, PCt = (S_[n] for n in ("aT", "rT", "bT", "kT", "rkT", "PCt"))
        rb, rbk = fm64_job(ws_in, k_in, g * 256, hT[:, :, 1:129], "hT")
        yield
        kb, kbk = fm64_job(ws_in, k_in, 768 + g * 256, hT[:, :, 1:129], "hT")
        yield
        P.copy("pool", Zg[:, :, 0:1], carry[:, g * 8:(g + 1) * 8].unsqueeze(2), ["carry"], ["Zg"])
        P.copy("act", Zg[:, 0:4, 1:129], v4(rb[0:64, :]), [rbk], ["Zg"])
        P.copy("act", Zg[:, 4:8, 1:129], v4(kb[0:64, :]), [kbk], ["Zg"])
        yield
        P.tt("pool", Zd, Zg[:, :, 0:128], Zg[:, :, 1:129], ALU.subtract, ["Zg"], ["Zd"])
        yield
        P.tt("pool", Zd, Zd, bc(mu64[:, g * 8:(g + 1) * 8].unsqueeze(2), [64, 8, 128]), ALU.mult, ["Zd", "mu64"], ["Zd"])
        yield
        P.tt("dve", Zd, Zg[:, :, 1:129], Zd, ALU.add, ["Zg", "Zd"], ["Zd"])
        P.copy("pool", carry[:, g * 8:(g + 1) * 8].unsqueeze(2), Zg[:, :, 128:129], ["Zg"], ["carry"])
        yield
        r_ = Zd[:, 0:4, :]
        k_ = Zd[:, 4:8, :]
        bw, bwk = P.bank()
        ba, bak = P.bank()
        for j in range(4):
            h = g * 4 + j
            P.mm(bw[0:64, j * 128:(j + 1) * 128], w2aug[:, h * 64:(h + 1) * 64], twd[:, :], True, True,
                 ["w2aug", "twd"], [bwk])
        for j in range(4):
            h = g * 4 + j
            P.mm(ba[0:64, j * 128:(j + 1) * 128], a2aug[:, h * 64:(h + 1) * 64], adb[:, :], True, True,
                 ["a2aug", "adb"], [bak])
        yield
        P.act(sgw, v4(bw[0:64, :]), AF.Tanh, [bwk], ["sgw"], scale=0.5)
        P.act(av, v4(ba[0:64, :]), AF.Tanh, [bak], ["av"], scale=0.5)
        P.act(sgw, sgw, AF.Copy, ["sgw"], ["sgw"], bias=0.5, scale=0.5)
        P.act(av, av, AF.Copy, ["av"], ["av"], bias=0.5, scale=0.5)
        yield
        P.op("dve", lambda e: e.tensor_tensor_scan(out=flat(Lp), data0=cmask, data1=flat(sgw), initial=0.0,
                                                   op0=ALU.mult, op1=ALU.add), ["sgw", "cmask"], ["Lp"])
        P.tt("dve", kk, k_, bc(kk64[:, g * 4:(g + 1) * 4].unsqueeze(2), [64, 4, 128]), ALU.mult, ["Zd", "headp"], ["kk"])
        yield
        P.tt("pool", t2, kk, kk, ALU.mult, ["kk"], ["t2"])
        P.act(eP, Lp, AF.Exp, ["Lp"], ["eP"], scale=-CDEC)
        yield
        P.act(ePi, Lp, AF.Exp, ["Lp"], ["ePi"], scale=CDEC)
        P.tt("pool", sgw, Lp, sgw, ALU.subtract, ["Lp", "sgw"], ["sgw"])
        yield
        P.act(PCt, Lp[:, :, 127:128], AF.Exp, ["Lp"], [sk_("PCt")], scale=-CDEC)
        yield
        bsq, bsqk = P.bank()
        P.mm(bsq[0:64, :], ones64, flat(t2), True, True, ["ones64", "t2"], [bsqk])
        P.act(ePm, sgw, AF.Exp, ["sgw"], ["ePm"], scale=-CDEC)
        yield
        P.ts("dve", flat(t2), bsq[0:64, :], 1e-12, None, ALU.max, None, [bsqk], ["t2"])
        yield
        P.act(t2, t2, AF.Ln, ["t2"], ["t2"])
        P.stt(t3, av, -1.0, bc(ka64[:, g * 4:(g + 1) * 4].unsqueeze(2), [64, 4, 128]), ALU.add, ALU.mult,
              ["av", "headp"], ["t3"])
        yield
        P.act(t2, t2, AF.Exp, ["t2"], ["t2"], scale=-0.5)
        P.stt(t3, t3, 1.0, k_, ALU.add, ALU.mult, ["t3", "Zd"], ["t3"])
        yield
        P.tt("dve", rT, r_, eP, ALU.mult, ["Zd", "eP"], [sk_("rT")])
        P.tt("pool", kT, t3, ePi, ALU.mult, ["t3", "ePi"], [sk_("kT")])
        yield
        P.tt("dve", kk, kk, t2, ALU.mult, ["kk", "t2"], ["kk"])
        yield
        P.stt(aT, kk, -1.0, ePm, ALU.mult, ALU.mult, ["kk", "ePm"], [sk_("aT")])
        P.tt("pool", av, kk, av, ALU.mult, ["kk", "av"], ["av"])
        yield
        P.tt("dve", bT, av, ePi, ALU.mult, ["av", "ePi"], [sk_("bT")])
        P.tt("pool", rkT, r_, t3, ALU.mult, ["Zd", "t3"], [sk_("rkT")])
        yield

    def istage(i, g, n):
        dsl, isl = n % 3, n % 2
        S_ = DS[dsl]
        O_ = IO[isl]
        dk_ = lambda nm: "%s_%d" % (nm, dsl)
        ik_ = lambda nm: "%s_i%d" % (nm, isl)

        def sk_(nm):
            if nm in ("aT", "rT", "bT", "kT", "rkT", "PCt"):
                return dk_(nm)
            if nm in ("Tm0", "AakT", "ArbT", "ArkT", "BK"):
                return ik_(nm)
            return nm + "_s"
        aT, rT, bT, kT, rkT, PCt = (S_[q] for q in ("aT", "rT", "bT", "kT", "rkT", "PCt"))
        BK, AakT, ArbT, ArkT = (O_[q] for q in ("BK", "AakT", "ArbT", "ArkT"))
        Qm = [PQs[q][:, 0, :, :] for q in range(2)]
        Pm = [PQs[q][:, 1, :, :] for q in range(2)]
        Tm = [O_["Tm0"], Tm1s]
        tp = i % 2

        def amat(lh, lkey, rh, rkey, mask, mkey, eng, oap, okey):
            bk, bkey = P.bank()
            for j in range(4):
                P.mm(bk[:, j * 128:(j + 1) * 128], lh[:, j, :], rh[:, j, :], True, True, [lkey, rkey], [bkey])
            P.tt(eng, oap, v4(bk[:, :]), bc(mask.unsqueeze(1), [128, 4, 128]), ALU.mult, [bkey, mkey], [okey])

        amat(bT, sk_("bT"), aT, sk_("aT"), maskSU, "maskSU", "dve", Pm[0], sk_("Pm0"))
        yield
        amat(aT, sk_("aT"), bT, sk_("bT"), maskSL, "maskSL", "dve", Qm[0], sk_("Qm0"))
        P.tt("pool", Tm[0], Pm[0], bc(ident.unsqueeze(1), [128, 4, 128]), ALU.add, [sk_("Pm0"), "ident"], [sk_("Tm0")])
        yield
        cur = 0
        extra = [
            lambda: amat(kT, sk_("kT"), aT, sk_("aT"), maskSU, "maskSU", "dve", AakT, sk_("AakT")),
            lambda: amat(bT, sk_("bT"), rT, sk_("rT"), maskU, "maskU", "dve", ArbT, sk_("ArbT")),
            lambda: amat(kT, sk_("kT"), rT, sk_("rT"), maskU, "maskU", "dve", ArkT, sk_("ArkT")),
        ]

        def ephase_a():
            bk, bkey = P.bank()
            bv = bk[:, 0:256].bitcast(BF16).rearrange("p (a b) -> p a b", a=8)
            for j in range(4):
                P.tr(bv[:, j, :], bT[:, j, :], ident[0:64, 0:64], [sk_("bT"), "ident"], [bkey])
            for j in range(4):
                P.tr(bv[:, 4 + j, :], kT[:, j, :], ident[0:64, 0:64], [sk_("kT"), "ident"], [bkey])
            P.copy("act", BK, bv, [bkey], [sk_("BK")])

        def ephase_b():
            bk, bkey = P.bank()
            for j in range(4):
                P.mm(bk[:, 2 * j:2 * j + 2], rkT[:, j, :], onescol if 'norkc' in TOG else rkc[:, g * 4 + j, :], True, True, [sk_("rkT"), "rkc", "onescol"], [bkey])
            P.copy("dve", bs2[tp][:, g * 4:(g + 1) * 4, :], v4(bk[:, 0:8])[:, :, 0:1], [bkey], ["bs%d" % tp])

        extra += [ephase_a, ephase_b]
        for kr in range(1, 7):
            nxt = 1 - cur
            bA, bAk = P.bank()
            for j in range(4):
                P.mm(bA[:, j * 128:(j + 1) * 128], Pm[cur][:, j, :], Qm[cur][:, j, :], True, True,
                     [sk_("Pm%d" % cur), sk_("Qm%d" % cur)], [bAk])
            if kr <= 5:
                bB, bBk = P.bank()
                for j in range(4):
                    P.mm(bB[:, j * 128:(j + 1) * 128], Qm[cur][:, j, :], Pm[cur][:, j, :], True, True,
                         [sk_("Pm%d" % cur), sk_("Qm%d" % cur)], [bBk])
            if extra:
                extra.pop(0)()
            P.copy("act", Qm[nxt], v4(bA[:, :]), [bAk], [sk_("Qm%d" % nxt)])
            if kr <= 5:
                P.copy("act", Pm[nxt], v4(bB[:, :]), [bBk], [sk_("Pm%d" % nxt)])
            yield
            btk, btkk = P.bank()
            for j in range(4):
                P.mm(btk[:, j * 128:(j + 1) * 128], Qm[nxt][:, j, :], Tm[cur][:, j, :], True, True,
                     [sk_("Qm%d" % nxt), (sk_("Tm0") if cur == 0 else "Tm1_s")], [btkk])
            yield
            P.tt("dve", Tm[nxt], v4(btk[:, :]), Tm[cur], ALU.add, [btkk, (sk_("Tm0") if cur == 0 else "Tm1_s")], [(sk_("Tm0") if nxt == 0 else "Tm1_s")])
            yield
            cur = nxt
        while extra:
            extra.pop(0)()
            yield
        assert cur == 0
    def cstage(i, g, n):
        dsl, isl = n % 3, n % 2
        S_ = DS[dsl]
        O_ = IO[isl]
        dk_ = lambda nm: "%s_%d" % (nm, dsl)
        ik_ = lambda nm: "%s_i%d" % (nm, isl)

        def sk_(nm):
            if nm in ("aT", "rT", "bT", "kT", "rkT", "PCt"):
                return dk_(nm)
            if nm in ("Tm0", "AakT", "ArbT", "ArkT", "BK"):
                return ik_(nm)
            return nm + "_s"
        aT, rT, PCt = (S_[q] for q in ("aT", "rT", "PCt"))
        BK, AakT, ArbT, ArkT = (O_[q] for q in ("BK", "AakT", "ArbT", "ArkT"))
        tp = i % 2
        Vt, vk = Vt2[tp], "Vt%d" % tp
        Y = Y2[tp]
        Tt = O_["Tm0"]
        Ttk = ik_("Tm0")
        Mbk = "Mb%d" % g
        Mfk = "Mf%d" % g
        b1, b1k = P.bank()
        for j in range(4):
            h = g * 4 + j
            P.mm(b1[:, j * 64:(j + 1) * 64], aT[:, j, :], Mb[:, h, :], True, False, [sk_("aT"), Mbk, "Mb"], [b1k])
            P.mm(b1[:, j * 64:(j + 1) * 64], AakT[:, j, :], Vt[:, h * 64:(h + 1) * 64], False, True, [sk_("AakT"), vk], [b1k])
        P.copy("act", R1s, v4(b1[:, 0:256]), [b1k], [sk_("R1s")])
        yield
        b2, b2k = P.bank()
        for j in range(4):
            P.mm(b2[:, j * 64:(j + 1) * 64], Tt[:, j, :], R1s[:, j, :], True, True, [Ttk, sk_("R1s")], [b2k])
        P.copy("act", Ub, v4(b2[:, 0:256]), [b2k], [sk_("Ub")])
        yield
        b4, b4k = P.bank()
        for j in range(4):
            h = g * 4 + j
            P.mm(b4[0:64, j * 64:(j + 1) * 64], BK[:, j, :], Ub[:, j, :], True, False, [sk_("BK"), sk_("Ub")], [b4k])
            P.mm(b4[0:64, j * 64:(j + 1) * 64], BK[:, 4 + j, :], Vt[:, h * 64:(h + 1) * 64], False, True, [sk_("BK"), vk], [b4k])
        b3, b3k = P.bank()
        for j in range(4):
            h = g * 4 + j
            P.mm(b3[:, j * 64:(j + 1) * 64], rT[:, j, :], Mb[:, h, :], True, False, [sk_("rT"), Mbk, "Mb"], [b3k])
            P.mm(b3[:, j * 64:(j + 1) * 64], ArbT[:, j, :], Ub[:, j, :], False, False, [sk_("ArbT"), sk_("Ub")], [b3k])
            P.mm(b3[:, j * 64:(j + 1) * 64], ArkT[:, j, :], Vt[:, h * 64:(h + 1) * 64], False, True, [sk_("ArkT"), vk], [b3k])
        yield
        Mfg = Mf[:, g * 4:(g + 1) * 4, :]
        P.tt("dve", Mfg, Mfg, v4(b4[0:64, 0:256]), ALU.add, [Mfk, "Mf", b4k], [Mfk])
        P.tt("pool", Mfg, Mfg, bc(PCt, [64, 4, 64]), ALU.mult, [Mfk, sk_("PCt")], [Mfk])
        P.copy("act", Mb[:, g * 4:(g + 1) * 4, :], Mfg, [Mfk], [Mbk])
        yield
        P.copy("dve", Y[:, g * 4:(g + 1) * 4, :], v4(b3[:, 0:256]), [b3k], ["Y%d" % tp])
        yield

    def post(i):
        tp = i % 2
        Y = Y2[tp]
        Vt, sg, qmT = Vt2[tp], sg2[tp], qmT2[tp]
        vk, sk, qk = "Vt%d" % tp, "sg%d" % tp, "qmT%d" % tp
        s1, s2, mean, m2, var, sd, rs = [st[:, q, :].unsqueeze(2) for q in range(7)]
        P.op("dve", lambda e: e.tensor_reduce(out=s1, in_=Y, axis=AX.X, op=ALU.add), ["Y%d" % tp], ["s1"])
        P.tt("pool", Ysq, Y, Y, ALU.mult, ["Y%d" % tp], ["Ysq"])
        yield
        P.op("dve", lambda e: e.tensor_reduce(out=s2, in_=Ysq, axis=AX.X, op=ALU.add), ["Ysq"], ["s2"])
        P.ts("dve", mean, s1, 1.0 / 64, None, ALU.mult, None, ["s1"], ["mean"])
        P.tt("dve", m2, mean, mean, ALU.mult, ["mean"], ["m2"])
        P.stt(var, s2, 1.0 / 64, m2, ALU.mult, ALU.subtract, ["s2", "m2"], ["var"])
        yield
        P.ts("dve", sd, var, LNX_EPS, None, ALU.add, None, ["var"], ["sd"])
        rsqrt_op(P, rs, sd, mhalf.unsqueeze(2), ["sd"], ["rs"])
        yield
        P.tt("dve", Ysq, Y, bc(mean, [128, 12, 64]), ALU.subtract, ["Y%d" % tp, "mean", "Ysq"], ["Ysq"])
        yield
        P.tt("pool", Ysq, Ysq, bc(rs, [128, 12, 64]), ALU.mult, ["Ysq", "rs"], ["Ysq"])
        yield
        Yf = Ysq.rearrange("p a b -> p (a b)")
        P.tt("pool", Yf, Yf, lnx_b[:, 0, :], ALU.mult, ["Ysq", "lnx_b"], ["Ysq"])
        yield
        P.tt("pool", Yf, Yf, lnx_b[:, 1, :], ALU.add, ["Ysq", "lnx_b"], ["Ysq"])
        yield
        for h in range(12):
            P.stt(Ysq[:, h, :], Vt[:, h * 64:(h + 1) * 64], bs2[tp][:, h, :], Ysq[:, h, :], ALU.mult, ALU.add,
                  [vk, "bs%d" % tp, "Ysq"], ["Ysq"])
            if h % 4 == 3:
                yield
        P.tt("dve", YG0[:, 0:768], Yf, sg[:, 0:768], ALU.mult, ["Ysq", sk], ["YG0"])
        yield
        mem_attn(0, A, qmT, sg, YG0, qk=qk, sk=sk, yk="YG0")
        yield
        out_proj(0, i, A, YG0, final=False, yk="YG0")
        yield

    def interleave_n(gens, rates=None):
        gens = list(gens)
        rates = rates or [1] * len(gens)
        alive = [True] * len(gens)
        while any(alive):
            for q in range(len(gens)):
                for _ in range(rates[q]):
                    if alive[q]:
                        try:
                            next(gens[q])
                        except StopIteration:
                            alive[q] = False

    def chain_gens(gs):
        for g_ in gs:
            yield from g_

    def interleave(ga, gb):
        alive = [True, True]
        gens = [ga, gb]
        while alive[0] or alive[1]:
            for q in range(2):
                if alive[q]:
                    try:
                        next(gens[q])
                    except StopIteration:
                        alive[q] = False

    if nlayers >= 1:
        units = [(i, g) for i in range(NT) for g in range(3)]

        NU = len(units)

        def dstage(n):
            i, g = units[n]
            gs = []
            if g == 0:
                if i + 2 < NT:
                    load_x(i + 2)
                if i == 1:
                    late_precasts()
                gs.append(pre(i))
            gs.append(dphase(i, g, n % 3))
            return chain_gens(gs)

        CSLOW = int(os.environ.get('KCS', '4'))

        def slow(gen, k):
            for _ in gen:
                yield
                for _q in range(k - 1):
                    yield

        for n in range(-2, NU):
            gens = []
            if n >= 0:
                gens.append(slow(cstage(units[n][0], units[n][1], n), CSLOW))
            if 0 <= n + 1 < NU:
                gens.append(istage(units[n + 1][0], units[n + 1][1], n + 1))
            if n + 2 < NU:
                gens.append(dstage(n + 2))
            if n >= 0 and units[n][1] == 0 and units[n][0] > 0:
                gens.append(post(units[n][0] - 1))
            interleave_n(gens)
        for _ in post(NT - 1):
            pass

    final_ops = []
    if dbg or nlayers < 2:
        tgt = dbg_d if dbg else out_d
        for i in range(NT):
            final_ops.append(P.dma("sp", tgt[i * 128:(i + 1) * 128, :], x_res[:, i, :], ["x%d" % i], []))

    if nlayers >= 2:
        P.barrier()
        arena.off = common_off
        P.dma("sp", postg_b, postg_d[:, 1, :], (), ["postg_b"])
        mark1 = arena.off
        dtmp = arena.alloc([128], F32)
        tq64 = arena.alloc([128], F32)
        itab = arena.alloc([16], F32)
        ndtmp = arena.alloc([128], F32)
        lamv = arena.alloc([4, 64], F32)
        lamw = arena.alloc([4, 64], F32)
        arena.off = mark1
        arena.off = arena.off + 2048 + 2048 + 512 + 512 + 64
        KTt, Vat = [], []
        for i_ in range(NT):
            if i_ < 3:
                KTt.append(arena.alloc([6, 128], BF16))
                Vat.append(arena.alloc([6, 129], BF16))
            else:
                xv = x_res[:, i_ - 3, :].bitcast(BF16)
                KTt.append(xv[:, 0:768].rearrange("p (h t) -> p h t", h=6))
                Vat.append(xv[:, 768:768 + 774].rearrange("p (h e) -> p h e", h=6))
        PT4 = [arena.alloc([2, NT, 128], BF16) for _ in range(2)]
        P.dma("sp", lamv, lamv_d, (), ["lamv"])
        hT1 = arena.alloc([8, 128], BF16)
        kvT = arena.alloc([8, 128], BF16)
        QT2 = [arena.alloc([6, 2, 128], BF16) for _ in range(2)]
        sg2b = [arena.alloc([1024], BF16) for _ in range(3)]
        qmT2b = [arena.alloc([4, 128], BF16, parts=64) for _ in range(3)]
        AT = arena.alloc([6, 128], F32)
        ATs = arena.alloc([6, 128], F32)
        YG = arena.alloc([1024], BF16)
        Dh = arena.alloc([6, 128], BF16)
        btab = arena.alloc([6, 16], F32)
        rzz = arena.alloc([2, 1], F32)
        w1 = arena.alloc([1], F32)
        s6 = arena.alloc([6, 1], F32)
        r6 = arena.alloc([6, 1], F32)
        stmp = arena.alloc([2, 128], F32)
        t1 = arena.alloc([128], F32)
        RES = {}
        cand = ([(ws_kv, k_kv, c0) for c0 in range(0, 1536, 256)] + [(ws_out[1], k_out[1], c0) for c0 in range(0, 1024, 256)]
                + [(ws_bin, k_bin, c0) for c0 in range(0, 768, 256)])
        for (ws_, kk_, c0) in cand:
            if arena.off + 4096 > arena.cap:
                break
            t_ = arena.alloc([8, 256], BF16)
            rk_ = "res_%s_%d" % (ws_.name, c0)
            P.dma("sp", t_, ws_.t[ws_.pieces[c0][0]], kk_[c0], [rk_])
            RES[(ws_.name, c0)] = (t_, rk_)
        resident[0] = RES
        print("resident pieces", len(RES))
        P.op("pool", lambda e: e.iota(dtmp, pattern=[[1, 128]], base=0, channel_multiplier=-1,
                                      allow_small_or_imprecise_dtypes=True), (), ["dtmp"])
        P.op("pool", lambda e: e.iota(tq64, pattern=[[1, 128]], base=-64, channel_multiplier=0,
                                      allow_small_or_imprecise_dtypes=True), (), ["tq64"])
        P.op("pool", lambda e: e.iota(itab, pattern=[[-128, 16]], base=-64, channel_multiplier=1,
                                      allow_small_or_imprecise_dtypes=True), (), ["itab"])
        P.op("pool", lambda e: e.iota(ndtmp, pattern=[[-1, 128]], base=0, channel_multiplier=1,
                                      allow_small_or_imprecise_dtypes=True), (), ["ndtmp"])
        P.tt("dve", dtmp, dtmp, ndtmp, ALU.max, ["dtmp", "ndtmp"], ["dtmp"])
        for h in range(6):
            P.ts("dve", Dh[:, h, :], dtmp, -SLOPES[h], None, ALU.mult, None, ["dtmp"], ["Dh"])
            P.stt(Dh[:, h, :], tq64, SLOPES[h], Dh[:, h, :], ALU.mult, ALU.add, ["tq64", "Dh"], ["Dh"])
            P.ts("dve", btab[:, h, :], itab, SLOPES[h], None, ALU.mult, None, ["itab"], ["btab"])
        P.memset("pool", Dh[64:128, :, 0:64], -30000.0, ["Dh"])
        P.tt("dve", lamw[:, 0, :], lamv[:, 0, :], lamv[:, 1, :], ALU.mult, ["lamv"], ["lamw"])
        P.tt("dve", lamw[:, 1, :], lamv[:, 2, :], lamv[:, 3, :], ALU.mult, ["lamv", "lamw"], ["lamw"])
        P.op("dve", lambda e: e.tensor_reduce(out=lamc[:, 0:2].unsqueeze(2), in_=lamw[:, 0:2, :], axis=AX.X, op=ALU.add),
             ["lamw"], ["lamc"])
        P.act(lamc[:, 2:4], lamc[:, 0:2], AF.Exp, ["lamc"], ["lamc2"])
        P.tt("dve", lamc[:, 4:5], lamc[:, 3:4], lamc[:, 2:3], ALU.subtract, ["lamc2"], ["lamc4"])
        P.ts("dve", lamc[:, 5:6], lamc[:, 4:5], -LAM_INIT1, None, ALU.add, None, ["lamc4"], ["neglam"])
        neglam = lamc[:, 5:6]
        P.ts("dve", subln_b, subln_b, 0.5 * (1.0 - LAM_INIT1), None, ALU.mult, None, ["subln_b"], ["subln_b"])
        P.barrier()
        for q in range(2):
            P.memset("pool", QT2[q], 0.0, ["QT%d" % q])

        def proj1(i):
            xk = "x%d" % i
            tp = i % 2
            t3_ = i % 3
            QT, sg, qmT = QT2[tp], sg2b[t3_], qmT2b[t3_]
            qtk, sk, qk = "QT%d" % tp, "sgb%d" % t3_, "qmTb%d" % t3_
            rmsnorm_T(x_res[:, i, :], [xk], [(4, kvT, "kvT"), (1, hT1, "hT1")], A)
            xdead = ["x%d" % (i - 3)] if i >= 3 else []
            P.memset("pool", Vat[i][:, :, 128:129], 1.0, ["Vaug%d" % i] + xdead)
            yield
            for hh in range(0, 6, 2):
                wt, wkey = stage(ws_kv, k_kv, hh * 128, 256)
                bk, bkey = P.bank()
                for q in range(2):
                    for kc in range(8):
                        P.mm(bk[:, q * 128:(q + 1) * 128], wt[:, kc, q * 128:(q + 1) * 128], kvT[:, kc, :], kc == 0, kc == 7,
                             ["kvT", wkey], [bkey])
                P.copy("dve", KTt[i][:, hh:hh + 2, :], bk[:, 0:256].rearrange("p (a b) -> p a b", a=2),
                       [bkey], ["KT%d" % i] + xdead)
                yield
            for q in range(3):
                (vb1, vb1k), = tm_job(ws_kv, k_kv, 768 + q * 256, 256, [(kvT, "kvT")], None)
                P.copy("dve", Vat[i][:, 2 * q:2 * q + 2, 0:128], vb1[:, 0:256].rearrange("p (a b) -> p a b", a=2),
                       [vb1k], ["Vaug%d" % i] + xdead)
                yield
            for hh in range(0, 6, 2):
                wt, wkey = stage(ws_bin, k_bin, hh * 128, 256)
                bk, bkey = P.bank(nr=2)
                for q in range(2):
                    for kc in range(8):
                        P.mm(bk[:, q * 128:(q + 1) * 128], wt[:, kc, q * 128:(q + 1) * 128], hT1[:, kc, :], kc == 0, kc == 7,
                             ["hT1", wkey], [bkey])
                for c in range(2):
                    P.copy("dve", QT[c * 64:(c + 1) * 64, hh:hh + 2, c, :],
                           bk[c * 64:(c + 1) * 64, 0:256].rearrange("p (a b) -> p a b", a=2), [bkey, qtk], [qtk])
                yield
            for q in range(4):
                c0 = 768 + q * 256 if q < 3 else 1792
                (g1, g1k), = tm_job(ws_bin, k_bin, c0, 256, [(hT1, "hT1")], [2])
                P.act(sg[:, q * 256:(q + 1) * 256], g1[:, 0:256], AF.Tanh, [g1k], [sk], scale=0.5)
                P.stt(sg[:, q * 256:(q + 1) * 256], sg[:, q * 256:(q + 1) * 256], 1.0, g1[:, 0:256], ALU.add, ALU.mult,
                      [sk, g1k], [sk])
                yield
            qb, qbk = fm64_job(ws_bin, k_bin, 1536, hT1, "hT1")
            P.copy("dve", qmT, qb[0:64, :].rearrange("p (a b) -> p a b", a=4), [qbk], [qk])
            yield

        def attn1(i):
            xk = "x%d" % i
            tp = i % 2
            QT = QT2[tp]
            qtk = "QT%d" % tp

            def scores(h):
                PT = PT4[h % 2]
                ptk = "PT%d" % (h % 2)
                for j in range(i + 1):
                    bk, bkey = P.bank()
                    P.mm(bk[:, 0:256], KTt[j][:, h, :], QT[:, h, :, :].rearrange("p c q -> p (c q)"),
                         True, True, ["KT%d" % j, qtk], [bkey])
                    sv = bk[:, 0:256].rearrange("p (c q) -> p c q", c=2)
                    if j == i:
                        P.stt(stmp, sv, 0.125, bc(Dh[:, h, :].unsqueeze(1), [128, 2, 128]), ALU.mult, ALU.add,
                              [bkey, "Dh"], ["stmp"])
                        P.act(PT[:, :, j, :], stmp, AF.Exp, ["stmp"], [ptk])
                    else:
                        P.act(PT[:, :, j, :], sv, AF.Exp, [bkey, "btab"], [ptk],
                              bias=btab[:, h, (i - j):(i - j) + 1], scale=0.125)
                    if j % 2 == 1:
                        yield

            def pv(h):
                PT = PT4[h % 2]
                ptk = "PT%d" % (h % 2)
                bk, bkey = P.bank(6, 8, nr=3)
                ov = bk[:, 0:258].rearrange("p (a b) -> p a b", a=2)
                for c in range(2):
                    for j in range(i + 1):
                        P.mm(ov[:, c, :], PT[:, c, j, :], Vat[j][:, h, :], j == 0, j == i, [ptk, "Vaug%d" % j], [bkey])
                        if j % 4 == 3:
                            yield
                P.op("dve", lambda e, ov=ov: e.reciprocal(out=rzz, in_=ov[:, :, 128:129]), [bkey], ["rzz"])
                P.tt("dve", w1, rzz[:, 1, :], neglam, ALU.mult, ["rzz", "neglam"], ["w1"])
                P.ts("dve", t1, ov[:, 1, 0:128], w1, None, ALU.mult, None, [bkey, "w1"], ["t1"])
                P.stt(AT[:, h, :], ov[:, 0, 0:128], rzz[:, 0, :], t1, ALU.mult, ALU.add, [bkey, "rzz", "t1"], ["AT"])
                yield

            yield from scores(0)
            for h in range(6):
                if h + 1 < 6:
                    yield from scores(h + 1)
                yield from pv(h)

        def tail1(i):
            t3_ = i % 3
            sg, qmT = sg2b[t3_], qmT2b[t3_]
            sk, qk = "sgb%d" % t3_, "qmTb%d" % t3_
            P.tt("pool", ATs, AT, AT, ALU.mult, ["AT"], ["ATs"])
            P.op("dve", lambda e: e.tensor_reduce(out=s6, in_=ATs, axis=AX.X, op=ALU.add), ["ATs"], ["s6"])
            P.ts("dve", r6, s6, 1.0 / 128, NORM_EPS, ALU.mult, ALU.add, ["s6"], ["r6"])
            rsqrt_op(P, s6, r6, mhalf[:, 0:6].unsqueeze(2), ["r6"], ["s6"])
            yield
            P.tt("pool", ATs, AT, bc(s6, [128, 6, 128]), ALU.mult, ["AT", "s6", "ATs"], ["ATs"])
            P.tt("pool", ATs, ATs, bc(subln_b.unsqueeze(1), [128, 6, 128]), ALU.mult, ["ATs", "subln_b"], ["ATs"])
            yield
            P.tt("dve", YG[:, 0:768], ATs.rearrange("p a b -> p (a b)"), sg[:, 0:768], ALU.mult, ["ATs", sk], ["YG1"])
            yield
            mem_attn(1, A, qmT, sg, YG, qk=qk, sk=sk, yk="YG1")
            yield
            final_ops.append(out_proj(1, i, A, YG, final=True, yk="YG1"))
            yield

        def interleave2(ga, gb, ra=1, rb=1):
            alive = [True, True]
            gens = [ga, gb]
            rr = [ra, rb]
            while alive[0] or alive[1]:
                for q in range(2):
                    for _ in range(rr[q]):
                        if alive[q]:
                            try:
                                next(gens[q])
                            except StopIteration:
                                alive[q] = False

        P.bank_rng = (0, 6)
        for _ in proj1(0):
            pass
        for i in range(NT):
            na = max(1, (6 * ((i + 2) // 2 + (i + 1) // 4 + 2) + 6) // 18)
            gens, rates = [attn1(i)], [na]
            if i + 1 < NT:
                gens.append(proj1(i + 1))
                rates.append(1)
            if i > 0:
                gens.append(tail1(i - 1))
                rates.append(1)
            interleave_n(gens, rates)
        for _ in tail1(NT - 1):
            pass

    print("arena use", arena.off, "cst use", cst.off)
    P.emit(final_wait_ops=final_ops)
    return nc, P


_CACHE = {}


def prep_inputs(inp, b):
    f = lambda a: np.ascontiguousarray(np.asarray(a, dtype=np.float32))
    mu = f(inp["a_shift_mu"])[0]
    cols = []
    for g in range(3):
        for j in range(4):
            cols.append(np.arange((g * 4 + j) * 64, (g * 4 + j + 1) * 64))
        for j in range(4):
            cols.append(768 + np.arange((g * 4 + j) * 64, (g * 4 + j + 1) * 64))
    cols.append(np.arange(2304, 2368))
    cols.append(np.arange(2368, 2432))
    mu64 = np.stack([mu[c] for c in cols], axis=1)
    gains = np.stack([f(inp["pre_norm"])[0], f(inp["pre_norm"])[1], f(inp["mem_norm"])[0], f(inp["mem_norm"])[1],
                      f(inp["kv_norm"])], 0)
    gains = gains.reshape(5, 8, 128).transpose(2, 0, 1)
    hp = np.stack([f(inp["a_k_k"])[0].reshape(12, 64).T, f(inp["a_k_a"])[0].reshape(12, 64).T,
                   f(inp["a_r_k"])[0].T], 1)
    d = {
        "x": f(inp["x"][b]),
        "mem": f(inp["mem"][b]),
        "a_w_in": f(inp["a_w_in"])[0],
        "w_out": f(inp["w_out"]),
        "w_mem_kv": f(inp["w_mem_kv"]),
        "w_kv": f(inp["w_kv"]),
        "b_w_in": f(inp["b_w_in"])[0],
        "gains": f(gains),
        "mu64": f(mu64),
        "muv_b": f(np.broadcast_to(mu[1536:2304], (128, 768))),
        "w2aug": f(np.concatenate([f(inp["a_w2"])[0], f(inp["a_w0"])], 0)),
        "a2aug": f(np.concatenate([f(inp["a_a2"])[0], f(inp["a_a0"])], 0)),
        "headp": f(hp),
        "lnx_b": f(np.broadcast_to(np.stack([f(inp["a_lnx_w"])[0], f(inp["a_lnx_b"])[0]], 0), (128, 2, 768))),
        "postg_b": f(np.broadcast_to(f(inp["post_norm"]), (128, 2, 1024))),
        "subln_b": f(np.broadcast_to(f(inp["b_subln"])[0], (128, 128))),
        "lamv_b": f(np.broadcast_to(np.stack([f(inp["b_lam_q1"])[0], f(inp["b_lam_k1"])[0], f(inp["b_lam_q2"])[0],
                                              f(inp["b_lam_k2"])[0]], 0), (128, 4, 64))),
    }
    return d


def kernel(**inputs):
    if "nc" not in _CACHE:
        _CACHE["nc"] = build()[0]
    nc = _CACHE["nc"]
    in_maps = [prep_inputs(inputs, b) for b in range(8)]
    res = run_bass_kernel_spmd(nc, in_maps, core_ids=list(range(8)))
    out = np.stack([np.asarray(r["out"], dtype=np.float32) for r in res.results], 0)
    return out
```

```python
import math
import os
TOG = 'nomerge,nobank2'
SCHED = os.environ.get('KSCHED', '1') == '1'
PEF, ACTF, DVEF, POOLF, LATF = [float(v) for v in os.environ.get('KF', '0.93,1.07,0.98,0.80,0.25').split(',')]
FREE_SP = os.environ.get('KFSP', '1') == '1'
DMAX = float(os.environ.get('KDX', '0.9'))
QUANT = float(os.environ.get('KQ', '0.05'))
RATES0 = [int(v) for v in os.environ.get('KR0', '1,1,1').split(',')]
from contextlib import ExitStack

import numpy as np
import concourse.bass as bass
import concourse.mybir as mybir
from concourse.bass_utils import run_bass_kernel_spmd

F32 = mybir.dt.float32
BF16 = mybir.dt.bfloat16
AF = mybir.ActivationFunctionType
ALU = mybir.AluOpType
AX = mybir.AxisListType

S = 2048
D = 1024
NT = 16
CDEC = math.exp(-0.5)
NORM_EPS = 1e-6
LNX_EPS = 64e-5
SLOPES = [0.25, 0.0625, 0.015625, 0.00390625, 0.5, 0.125]
LAM_INIT1 = 0.8 - 0.6 * math.exp(-0.3 * 1)

NDMA_SLOTS = 8
COMPUTE = ("pe", "act", "dve", "pool")
QUEUES = ("sp", "actq", "poolq")
STREAM_OF = {"pe": "tensor", "act": "scalar", "dve": "vector", "pool": "gpsimd",
             "sp": "sync", "actq": "scalar", "poolq": "gpsimd"}
STREAMS = ["tensor", "scalar", "vector", "gpsimd", "sync"]


class Prog:
    def __init__(self, nc):
        self.nc = nc
        self.es = ExitStack()
        self.ops = []
        self.last_w = {}
        self.readers = {}
        self.ndma = {q: 0 for q in QUEUES}
        self.last_slot = {}
        self.barrier_deps = []
        self.nbank = 0
        self.nbank2 = 0
        self.bank_rng = (0, 8)
        self.pending = {}
        self.last_q = {}
        self.seg = 0
        self.banks = None

    def sb(self, name, shape, dt=F32):
        return self.es.enter_context(self.nc.sbuf_tensor(name, list(shape), dt))

    def ps(self, name, shape, dt=F32):
        return self.es.enter_context(self.nc.psum_tensor(name, list(shape), dt))

    def bank(self, lo=None, hi=None, nr=1):
        if lo is None:
            lo, hi = self.bank_rng
        n = hi - lo
        for t in range(n):
            i = lo + ((self.nbank + t) % n)
            key = "bank%d" % i
            if self.pending.get(key, 0) <= 0:
                self.nbank += t + 1
                self.pending[key] = nr
                return self.banks[i], key
        raise RuntimeError("no free PSUM bank in [%d,%d): %s" % (lo, hi, self.pending))

    def bank2(self, lo=0, hi=8):
        n = (hi - lo) // 2
        i = lo + 2 * (self.nbank2 % n)
        self.nbank2 += 1
        return self.psall[:, i:i + 2, :], ["bank%d" % i, "bank%d" % (i + 1)]

    def barrier(self):
        self.seg += 1
        last = {}
        for i, o in enumerate(self.ops):
            last[o["veng"]] = i
        self.barrier_deps = sorted(last.values())

    def op(self, eng, fn, reads=(), writes=(), dur=None):
        idx = len(self.ops)
        if eng != "pe":
            for k in reads:
                if k in self.pending and self.pending[k] > 0:
                    self.pending[k] -= 1
        raw = set(self.barrier_deps) if not SCHED else set()
        oth = set()
        for k in reads:
            if k in self.last_w:
                raw.add(self.last_w[k])
        for k in writes:
            if k in self.last_w:
                oth.add(self.last_w[k])
            for r in self.readers.get(k, ()):
                oth.add(r)
        if eng in QUEUES:
            n = self.ndma[eng]
            self.ndma[eng] += 1
            veng = (eng, n % NDMA_SLOTS)
            if veng in self.last_slot:
                raw.add(self.last_slot[veng])
            self.last_slot[veng] = idx
        else:
            veng = (eng, 0)
        deps = set(raw)
        odeps = set()
        for d in oth:
            if self.ops[d]["veng"] == veng and eng in COMPUTE:
                odeps.add(d)
                continue
            deps.add(d)
        if eng in QUEUES:
            if eng in self.last_q and not (eng == "sp" and FREE_SP):
                odeps.add(self.last_q[eng])
            self.last_q[eng] = idx
        deps.discard(idx)
        odeps.discard(idx)
        if dur is None:
            dur = {"pe": 0.11, "act": 0.5, "dve": 0.6, "pool": 1.0}.get(eng, 3.0)
        self.ops.append(dict(eng=eng, veng=veng, fn=fn, deps=sorted(deps), odeps=sorted(odeps - deps),
                             stream=STREAM_OF[eng], seg=self.seg, dur=dur))
        for k in reads:
            self.readers.setdefault(k, []).append(idx)
        for k in writes:
            self.last_w[k] = idx
            self.readers[k] = []
        return idx

    @staticmethod
    def _n(ap):
        n = 1
        for d in ap.shape[1:]:
            n *= d
        return n

    def _dur(self, eng, out):
        n = self._n(out)
        if eng == "pe":
            return PEF * (0.06 + 0.0004 * n)
        if eng == "act":
            return ACTF * (0.22 + 0.00075 * n)
        if eng == "dve":
            return DVEF * (0.2 + 0.00105 * n)
        return POOLF * (0.3 + 0.0021 * n)

    def tt(self, eng, out, in0, in1, op, r, w):
        d = self._dur(eng, out) if op != ALU.pow else 2.3
        return self.op(eng, lambda e: e.tensor_tensor(out=out, in0=in0, in1=in1, op=op), r, w, dur=d)

    def ts(self, eng, out, in0, s1, s2, op0, op1, r, w):
        if op1 is None:
            return self.op(eng, lambda e: e.tensor_scalar(out=out, in0=in0, scalar1=s1, scalar2=None, op0=op0), r, w,
                           dur=self._dur(eng, out))
        return self.op(eng, lambda e: e.tensor_scalar(out=out, in0=in0, scalar1=s1, scalar2=s2, op0=op0, op1=op1), r, w,
                       dur=self._dur(eng, out))

    def stt(self, out, in0, scalar, in1, op0, op1, r, w):
        return self.op("dve", lambda e: e.scalar_tensor_tensor(out=out, in0=in0, scalar=scalar, in1=in1, op0=op0, op1=op1), r, w,
                       dur=self._dur("dve", out))

    def act(self, out, in_, func, r, w, bias=None, scale=None, accum_out=None):
        kw = {}
        if bias is not None:
            kw["bias"] = bias
        if scale is not None:
            kw["scale"] = scale
        if accum_out is not None:
            kw["accum_out"] = accum_out
        return self.op("act", lambda e: e.activation(out=out, in_=in_, func=func, **kw), r, w,
                       dur=self._dur("act", out) + (0.1 if accum_out is not None else 0.0))

    def copy(self, eng, out, in_, r, w):
        if eng == "act":
            return self.op("act", lambda e: e.activation(out=out, in_=in_, func=AF.Copy), r, w, dur=self._dur("act", out))
        return self.op(eng, lambda e: e.tensor_copy(out=out, in_=in_), r, w, dur=self._dur(eng, out))

    def mm(self, out, lhsT, rhs, start, stop, r, w):
        return self.op("pe", lambda e: e.matmul(out, lhsT=lhsT, rhs=rhs, start=start, stop=stop), r, w,
                       dur=self._dur("pe", out))

    def tr(self, out, in_, ident, r, w):
        return self.op("pe", lambda e: e.transpose(out=out, in_=in_, identity=ident), r, w, dur=0.1)

    def dma(self, q, out, in_, r, w, **kw):
        return self.op(q, lambda e: e.dma_start(out=out, in_=in_, **kw), r, w)

    def memset(self, eng, ap, val, w):
        return self.op(eng, lambda e: e.memset(ap, val), (), w, dur=self._dur(eng, ap))

    def schedule(self):
        ops = self.ops
        n = len(ops)
        succ = [[] for _ in range(n)]
        npred = [0] * n
        for i, o in enumerate(ops):
            preds = set(o["deps"]) | set(o["odeps"])
            o["preds"] = preds
            for d in preds:
                succ[d].append(i)
            npred[i] = len(preds)
        prio = [0.0] * n
        for i in range(n - 1, -1, -1):
            m = 0.0
            for q in succ[i]:
                if prio[q] > m:
                    m = prio[q]
            prio[i] = ops[i]["dur"] + m
        finish = [0.0] * n
        order = []
        nseg = self.seg + 1
        by_seg = [[] for _ in range(nseg)]
        for i, o in enumerate(ops):
            by_seg[o["seg"]].append(i)
        tbase = 0.0
        chan_free = [0.0]
        LAT = LATF
        for sg in range(nseg):
            stream_free = {st: tbase for st in STREAMS}
            ready = {st: [] for st in STREAMS}
            rtime = {}
            left = len(by_seg[sg])
            in_seg = set(by_seg[sg])
            for i in by_seg[sg]:
                cnt = 0
                for d in ops[i]["preds"]:
                    if d in in_seg:
                        cnt += 1
                npred[i] = cnt
                if cnt == 0:
                    rtime[i] = tbase
                    ready[ops[i]["stream"]].append(i)
            while left:
                best = None
                for st in STREAMS:
                    rl = ready[st]
                    if not rl:
                        continue
                    sf = stream_free[st]
                    bi = None
                    bk = None
                    for i in rl:
                        t0 = rtime[i]
                        if t0 < sf:
                            t0 = sf
                        k = (int(t0 / QUANT), -prio[i])
                        if bk is None or k < bk:
                            bk = k
                            bi = i
                    if best is None or bk < best[0]:
                        best = (bk, bi, st)
                _, i, st = best
                t0 = max(rtime[i], stream_free[st])
                ready[st].remove(i)
                o = ops[i]
                if o["eng"] in QUEUES:
                    stream_free[st] = t0 + 0.08
                    if FREE_SP:
                        tx = max(t0 + 0.3, chan_free[0])
                        finish[i] = tx + o["dur"] * DMAX
                        chan_free[0] = finish[i]
                    else:
                        finish[i] = t0 + o["dur"]
                else:
                    stream_free[st] = t0 + o["dur"]
                    finish[i] = t0 + o["dur"]
                order.append(i)
                left -= 1
                for q in succ[i]:
                    if q not in in_seg:
                        continue
                    npred[q] -= 1
                    if npred[q] == 0:
                        rt = tbase
                        oq = ops[q]
                        for d in oq["preds"]:
                            if d in in_seg:
                                lat = 0.0 if ops[d]["stream"] == oq["stream"] and d in oq["odeps"] else LAT
                                if ops[d]["stream"] == oq["stream"] and d not in oq["odeps"]:
                                    lat = 0.2
                                f = finish[d] + lat
                                if f > rt:
                                    rt = f
                        rtime[q] = rt
                        ready[oq["stream"]].append(q)
            tbase = max([tbase] + [finish[i] for i in by_seg[sg]]) + 2.0
        self.sim_time = tbase
        return order

    def emit(self, final_wait_ops=()):
        nc = self.nc
        ops = self.ops
        if SCHED:
            order = self.schedule()
        else:
            order = list(range(len(ops)))
        cnt = {}
        for i in order:
            o = ops[i]
            v = o["veng"]
            cnt[v] = cnt.get(v, 0) + 1
            o["cnt"] = cnt[v]
        vengs = sorted(cnt.keys())
        sems = {}
        for v in vengs:
            sems[v] = self.es.enter_context(nc.semaphore("s_%s_%d" % v))
        mult = {v: (16 if v[0] in QUEUES else 1) for v in vengs}
        seen = {s: {} for s in STREAMS}
        know = [None] * len(ops)
        plan = {s: [] for s in STREAMS}
        nwaits = 0
        cur_seg = 0
        last_of_veng = {}
        seg_barrier = {}
        for i in order:
            o = ops[i]
            s = o["stream"]
            sd = seen[s]
            deps = list(o["deps"])
            if SCHED:
                if o["seg"] != cur_seg:
                    cur_seg = o["seg"]
                    seg_barrier = dict(last_of_veng)
                if o["seg"] > 0:
                    deps += [d for d in seg_barrier.values()]
            waits = {}
            for d in sorted(set(deps), reverse=True):
                od = ops[d]
                v = od["veng"]
                if sd.get(v, 0) >= od["cnt"]:
                    continue
                waits[v] = max(waits.get(v, 0), od["cnt"])
                for kv, kc in know[d].items():
                    if sd.get(kv, 0) < kc:
                        sd[kv] = kc
                if sd.get(v, 0) < od["cnt"]:
                    sd[v] = od["cnt"]
            o["waits"] = waits
            nwaits += len(waits)
            know[i] = dict(sd)
            plan[s].append(i)
            last_of_veng[o["veng"]] = i
        self.nwaits = nwaits
        blk = self.es.enter_context(nc.Block())

        def make(sname):
            def body(eng):
                for i in plan[sname]:
                    o = ops[i]
                    for v, c in o["waits"].items():
                        eng.wait_ge(sems[v], c * mult[v])
                    ins = o["fn"](eng)
                    ins.then_inc(sems[o["veng"]], mult[o["veng"]])
                if sname == "sync":
                    for i in final_wait_ops:
                        o = ops[i]
                        eng.wait_ge(sems[o["veng"]], o["cnt"] * mult[o["veng"]])
            return body

        for sname in STREAMS:
            if not plan[sname] and sname != "sync":
                continue
            getattr(blk, sname)(make(sname))


class Arena:
    def __init__(self, P, name, nbytes):
        self.cap = nbytes
        self.t = P.sb(name, [128, nbytes // 4], F32)
        self.off = 0

    def reset(self):
        self.off = 0

    def alloc(self, shape, dt=F32, parts=128):
        n = 1
        for s in shape:
            n *= s
        nb = n * (2 if dt == BF16 else 4)
        nb32 = (nb + 31) // 32 * 32
        a = self.off
        self.off += nb32
        self.peak = max(getattr(self, "peak", 0), self.off)
        if self.off > self.cap:
            print("ARENA OVERFLOW", self.off, self.cap)
        v = self.t[0:parts, a // 4:(a + nb32) // 4]
        if dt == BF16:
            v = v.bitcast(BF16)
        v = v[:, 0:n]
        if len(shape) == 2:
            v = v.rearrange("p (a b) -> p a b", a=shape[0], b=shape[1])
        elif len(shape) == 3:
            v = v.rearrange("p (a b c) -> p a b c", a=shape[0], b=shape[1], c=shape[2])
        return v


def rsqrt_op(P, out, in_, mh, rk, wk, tmp=None):
    if 'nopow' in TOG:
        P.act(out, in_, AF.Sqrt, rk, wk)
        P.op("dve", lambda e: e.reciprocal(out=out, in_=out), wk, wk)
    else:
        P.tt("pool", out, in_, mh, ALU.pow, list(rk) + ["mhalf"], wk)


def bc(ap, shape):
    return ap.broadcast_to(list(shape))


def build(dbg=False, nlayers=2):
    nc = bass.Bass("TRN2", target_bir_lowering=False)
    P = Prog(nc)

    def din(name, shape):
        return nc.dram_tensor(name, list(shape), F32, kind="ExternalInput").ap()

    x_d = din("x", [S, D])
    mem_d = din("mem", [256, D])
    win_d = din("a_w_in", [D, 3712])
    wout_d = din("w_out", [2, D, D])
    wmkv_d = din("w_mem_kv", [2, D, 512])
    wkv_d = din("w_kv", [D, 1536])
    bwin_d = din("b_w_in", [D, 2048])
    gains_d = din("gains", [128, 5, 8])
    mu64_d = din("mu64", [64, 26])
    muv_d = din("muv_b", [128, 768])
    w2a_d = din("w2aug", [65, 768])
    a2a_d = din("a2aug", [65, 768])
    hp_d = din("headp", [64, 3, 12])
    lnx_d = din("lnx_b", [128, 2, 768])
    postg_d = din("postg_b", [128, 2, 1024])
    subln_d = din("subln_b", [128, 128])
    lamv_d = din("lamv_b", [128, 4, 64])
    out_d = nc.dram_tensor("out", [S, D], F32, kind="ExternalOutput").ap()
    if dbg:
        dbg_d = nc.dram_tensor("dbg", [S, D], F32, kind="ExternalOutput").ap()
    class WS:
        def __init__(self, name, groups):
            self.groups = groups
            self.pieces = {}
            idx = 0
            for (ca, cb, w) in groups:
                for c0 in range(ca, cb, w):
                    self.pieces[c0] = (idx, w)
                    idx += 1
            self.t = nc.dram_tensor(name, [idx, 128, 8, 256], BF16).ap()
            self.name = name

        def cast(self, src, order=None):
            order = order if order is not None else list(range(len(self.groups)))
            for gi in order:
                ca, cb, w = self.groups[gi]
                J = (cb - ca) // w
                j0 = self.pieces[ca][0]
                for kc in range(8):
                    dst = self.t[j0:j0 + J, :, kc, 0:w].rearrange("j p c -> p j c")
                    srcv = src[kc * 128:(kc + 1) * 128, ca:cb].rearrange("p (j c) -> p j c", c=w)
                    P.dma("poolq", dst, srcv, (), ["%s_%d_%d" % (self.name, kc, gi)])
            return self.keymap()

        def keymap(self):
            km = {}
            for gi, (ca, cb, w) in enumerate(self.groups):
                for c0 in range(ca, cb, w):
                    km[c0] = ["%s_%d_%d" % (self.name, kc, gi) for kc in range(8)]
            return km

    ws_in = WS("ws_in", [(0, 768, 256), (768, 1536, 256), (1536, 2304, 256), (2304, 2432, 128), (2432, 3712, 256)])
    ws_out = [WS("ws_out%d" % l, [(0, 1024, 256)]) for l in range(2)]
    ws_mkv = [WS("ws_mkv%d" % l, [(0, 512, 256)]) for l in range(2)]
    ws_kv = WS("ws_kv", [(0, 1536, 256)])
    ws_bin = WS("ws_bin", [(0, 2048, 256)])

    psall = P.ps("psall", [128, 8, 512], F32)
    P.psall = psall
    P.banks = [psall[:, i, :] for i in range(8)]

    x_res = P.sb("x_res", [128, NT, D], F32)
    NWST = 4
    Wst = [P.sb("wst%d" % i, [128, 8, 256], BF16) for i in range(NWST)]
    wst_i = [0]
    cst = Arena(P, "cst", 22016)
    identf = cst.alloc([128], F32)
    ident = cst.alloc([128], BF16)
    maskSU = cst.alloc([128], BF16)
    maskU = cst.alloc([128], BF16)
    maskSL = cst.alloc([128], BF16)
    mtmp = cst.alloc([128], F32)
    ones64 = cst.alloc([64], F32, parts=64)
    onescol = cst.alloc([2], BF16, parts=64)
    cmask = cst.alloc([512], F32, parts=64)
    epsn = cst.alloc([1], F32)
    epsl = cst.alloc([1], F32)
    mhalf = cst.alloc([12], F32)
    rkc = cst.alloc([12, 2], BF16, parts=64)
    gains = cst.alloc([5, 8], F32)
    mu64 = cst.alloc([26], F32, parts=64)
    muv_b = cst.alloc([768], F32)
    w2aug = cst.alloc([768], BF16, parts=65)
    a2aug = cst.alloc([768], BF16, parts=65)
    headp = cst.alloc([3, 12], F32, parts=64)
    postg_b = cst.alloc([1024], F32)
    subln_b = cst.alloc([128], F32)
    lamc = cst.alloc([8], F32)
    KmT = [cst.alloc([4, 256], BF16, parts=64) for _ in range(2)]
    Vma = [cst.alloc([2, 4, 65], BF16) for _ in range(2)]

    k_in = ws_in.cast(win_d, order=[2, 4, 3, 0, 1])
    k_mkv = [ws_mkv[l].cast(wmkv_d[l]) for l in range(2)]
    k_out = [ws_out[0].cast(wout_d[0]), ws_out[1].keymap()]
    k_kv = ws_kv.keymap()
    k_bin = ws_bin.keymap()

    def late_precasts():
        ws_out[1].cast(wout_d[1])
        ws_kv.cast(wkv_d)
        ws_bin.cast(bwin_d)

    wst_list = [list(Wst)]

    resident = [{}]

    def stage(scr, keys, c0, n):
        if (scr.name, c0) in resident[0]:
            return resident[0][(scr.name, c0)]
        Wst_ = wst_list[0]
        NW_ = len(Wst_)
        return _stage(scr, keys, c0, n, Wst_, NW_)

    def _stage(scr, keys, c0, n, Wst, NWST):
        i = wst_i[0] % NWST
        assert n <= 256
        wst_i[0] += 1
        t = Wst[i]
        pi, w_ = scr.pieces[c0]
        assert n <= w_
        P.dma("sp", t[:, :, 0:n], scr.t[pi][:, :, 0:n], keys[c0], ["wst%d" % i])
        return t, "wst%d" % i

    P.memset("pool", identf, 1.0, ["identf"])
    P.op("pool", lambda e: e.affine_select(out=identf, in_=identf, pattern=[[1, 128]], compare_op=ALU.is_equal,
                                           fill=0.0, base=0, channel_multiplier=-1), ["identf"], ["identf"])
    P.copy("dve", ident, identf, ["identf"], ["ident"])
    for mk, name, pat, cm, cop in ((maskSU, "maskSU", 1, -1, ALU.is_gt), (maskU, "maskU", 1, -1, ALU.is_ge),
                                   (maskSL, "maskSL", -1, 1, ALU.is_gt)):
        P.memset("pool", mtmp, 1.0, ["mtmp"])
        P.op("pool", lambda e, pat=pat, cm=cm, cop=cop: e.affine_select(
            out=mtmp, in_=mtmp, pattern=[[pat, 128]], compare_op=cop, fill=0.0, base=0, channel_multiplier=cm),
            ["mtmp"], ["mtmp"])
        P.copy("pool", mk, mtmp, ["mtmp"], [name])
    P.memset("pool", ones64, 1.0, ["ones64"])
    P.memset("pool", onescol, 1.0, ["onescol"])
    P.memset("pool", cmask, 1.0, ["cmask"])
    P.memset("pool", cmask.rearrange("p (a b) -> p a b", a=4)[:, :, 0:1], 0.0, ["cmask"])
    P.memset("pool", epsn, NORM_EPS, ["epsn"])
    P.memset("pool", epsl, LNX_EPS, ["epsl"])
    P.memset("pool", mhalf, -0.5, ["mhalf"])
    P.dma("sp", gains, gains_d, (), ["gains"])
    P.dma("sp", mu64, mu64_d, (), ["mu64"])
    P.dma("sp", muv_b, muv_d, (), ["muv_b"])
    P.dma("poolq", w2aug, w2a_d, (), ["w2aug"])
    P.dma("poolq", a2aug, a2a_d, (), ["a2aug"])
    P.dma("sp", headp, hp_d, (), ["headp"])
    P.dma("sp", subln_b, subln_d, (), ["subln_b"])
    def load_x(i):
        P.dma("sp", x_res[:, i, :], x_d[i * 128:(i + 1) * 128, :], (), ["x%d" % i])
    for i in range(2):
        load_x(i)

    arena = Arena(P, "arena", 108864)

    def rmsnorm_T(src, src_keys, outs, A):
        xb, junk, ss, rt, rstd = A["xb"], A["junk"], A["ss"], A["rt"], A["rstd"]
        P.act(xb, src, AF.Square, src_keys, ["xb", "ss"], accum_out=ss)
        P.ts("dve", rt, ss, 1.0 / D, NORM_EPS, ALU.mult, ALU.add, ["ss"], ["rt"])
        rsqrt_op(P, rstd, rt, mhalf[:, 0:1], ["rt"], ["rstd"])
        P.act(xb, src, AF.Copy, list(src_keys) + ["rstd"], ["xb"], scale=rstd)
        bk, bkey = P.bank(nr=len(outs))
        bv = bk[:, 0:512].bitcast(BF16).rearrange("p (a b) -> p a b", a=8)
        for kc in range(8):
            P.tr(bv[:, kc, :], xb[:, kc * 128:(kc + 1) * 128], ident, ["xb", "ident"], [bkey])
        for gi, oap, okey in outs:
            P.tt("dve", oap, bv, bc(gains[:, gi, :].unsqueeze(2), [128, 8, 128]), ALU.mult,
                 [bkey, "gains"], [okey])

    def mem_kv(l, A):
        memt, memT = A["memt"], A["memT"]
        wt, wkey = stage(ws_mkv[l], k_mkv[l], 0, 256)
        wt2, wkey2 = stage(ws_mkv[l], k_mkv[l], 256, 256)
        for mt in range(2):
            P.dma("sp", memt, mem_d[mt * 128:(mt + 1) * 128, :], (), ["memt"])
            rmsnorm_T(memt, ["memt"], [(2 + l, memT[:, :, mt * 128:(mt + 1) * 128], "memT")], A)
        for hp in range(2):
            bk, bkey = P.bank()
            for hh in range(2):
                h = hp * 2 + hh
                for kc in range(8):
                    P.mm(bk[0:64, hh * 256:(hh + 1) * 256], wt[:, kc, h * 64:(h + 1) * 64], memT[:, kc, :],
                         kc == 0, kc == 7, [wkey, "memT"], [bkey])
            P.copy("act", KmT[l][:, hp * 2:(hp + 1) * 2, :], bk[0:64, :].rearrange("p (a b) -> p a b", a=2),
                   [bkey], ["KmT%d" % l])
        P.memset("pool", Vma[l], 2.0, ["Vma%d" % l])
        for mt in range(2):
            bk, bkey = P.bank()
            for kc in range(8):
                P.mm(bk[:, 0:256], memT[:, kc, mt * 128:(mt + 1) * 128], wt2[:, kc, 0:256], kc == 0, kc == 7,
                     [wkey2, "memT"], [bkey])
            P.copy("act", Vma[l][:, mt, :, 0:64], bk[:, 0:256].rearrange("p (a b) -> p a b", a=4),
                   [bkey, "Vma%d" % l], ["Vma%d" % l])

    def mem_attn(l, A, qmT, sg, YG, qk="qmT", sk="sg", yk="xb"):
        PmT, ym, rz = A["PmT"], A["ym"], A["rz"]
        for mt in range(2):
            bk, bkey = P.bank()
            for h in range(4):
                P.mm(bk[:, h * 128:(h + 1) * 128], KmT[l][:, h, mt * 128:(mt + 1) * 128], qmT[:, h, :], True, True,
                     ["KmT%d" % l, qk], [bkey])
            P.act(PmT[mt], bk[:, :].rearrange("p (a b) -> p a b", a=4), AF.Exp, [bkey], ["PmT%d" % mt], scale=0.125)
        bk, bkey = P.bank(nr=2)
        ov = bk[:, 0:260].rearrange("p (a b) -> p a b", a=4)
        for h in range(4):
            for mt in range(2):
                P.mm(ov[:, h, :], PmT[mt][:, h, :], Vma[l][:, mt, h, :], mt == 0, mt == 1,
                     ["PmT%d" % mt, "Vma%d" % l], [bkey])
        P.op("dve", lambda e: e.reciprocal(out=rz, in_=ov[:, :, 64:65]), [bkey], ["rz"])
        P.tt("dve", ym, ov[:, :, 0:64], bc(rz, [128, 4, 64]), ALU.mult, [bkey, "rz"], ["ym"])
        P.tt("dve", YG[:, 768:1024], ym.rearrange("p a b -> p (a b)"), sg[:, 768:1024], ALU.mult, ["ym", sk], [yk])

    def out_proj(l, i, A, YG, final, yk="xb"):
        YGT, otmp, ssa, rt, rstd = A["YGT"], A["otmp"], A["ssa"], A["rt2"], A["rstd2"]
        junk = A["junk"]
        bk, bkey = P.bank()
        bv = bk[:, 0:512].bitcast(BF16).rearrange("p (a b) -> p a b", a=8)
        for kc in range(8):
            P.tr(bv[:, kc, :], YG[:, kc * 128:(kc + 1) * 128], ident, [yk, "ident"], [bkey])
        P.copy("act", YGT, bv, [bkey], ["YGT"])
        obk = []
        for cg in range(2):
            bk, bkey = P.bank(nr=2)
            for hf in range(2):
                wt, wkey = stage(ws_out[l], k_out[l], cg * 512 + hf * 256, 256)
                for kc in range(8):
                    P.mm(bk[:, hf * 256:(hf + 1) * 256], YGT[:, kc, :], wt[:, kc, :], kc == 0, kc == 7, ["YGT", wkey], [bkey])
            P.act(junk[:, 0:512], bk[:, :], AF.Square, [bkey], ["otmp0", "ssa%d" % cg], accum_out=ssa[:, cg:cg + 1])
            obk.append((bk, bkey))
        P.tt("dve", ssa[:, 2:3], ssa[:, 0:1], ssa[:, 1:2], ALU.add, ["ssa0", "ssa1"], ["ssa2"])
        P.ts("dve", rt, ssa[:, 2:3], 1.0 / D, NORM_EPS, ALU.mult, ALU.add, ["ssa2"], ["rt2"])
        rsqrt_op(P, rstd, rt, mhalf[:, 0:1], ["rt2"], ["rstd2"])
        xk = "x%d" % i
        for cg in range(2):
            bk, bkey = obk[cg]
            P.stt(otmp[cg], bk[:, :], rstd, postg_b[:, cg * 512:(cg + 1) * 512], ALU.mult, ALU.mult,
                  [bkey, "rstd2", "postg_b"], ["otmp%d" % cg])
            P.tt("pool", x_res[:, i, cg * 512:(cg + 1) * 512], x_res[:, i, cg * 512:(cg + 1) * 512], otmp[cg],
                 ALU.add, [xk, "otmp%d" % cg], [xk])
        if final:
            return P.dma("sp", out_d[i * 128:(i + 1) * 128, :], x_res[:, i, :], [xk], [])
        return None

    def tm_job(scr, keys, c0, n, lhs_list, nrs):
        wt, wkey = stage(scr, keys, c0, n)
        res = []
        for li, (hv, hkey) in enumerate(lhs_list):
            bk, bkey = P.bank(nr=(nrs[li] if nrs else 1))
            for kc in range(8):
                P.mm(bk[:, 0:n], hv[:, kc, :], wt[:, kc, 0:n], kc == 0, kc == 7, [hkey, wkey], [bkey])
            res.append((bk, bkey))
        return res

    def fm64_job(scr, keys, c0, hv, hkey):
        wt, wkey = stage(scr, keys, c0, 256)
        bk, bkey = P.bank()
        for g in range(4):
            for kc in range(8):
                P.mm(bk[0:64, g * 128:(g + 1) * 128], wt[:, kc, g * 64:(g + 1) * 64], hv[:, kc, :], kc == 0, kc == 7,
                     [hkey, wkey], [bkey])
        return bk, bkey

    A = {}
    A["xb"] = arena.alloc([1024], BF16)
    for nm in ("ss", "rt", "rstd", "rt2", "rstd2"):
        A[nm] = arena.alloc([1], F32)
    A["ssa"] = arena.alloc([4], F32)
    A["PmT"] = [arena.alloc([4, 128], BF16) for _ in range(2)]
    A["ym"] = arena.alloc([4, 64], F32)
    A["rz"] = arena.alloc([4, 1], F32)
    A["YGT"] = arena.alloc([8, 128], BF16)
    A["otmp"] = [arena.alloc([512], F32) for _ in range(2)]
    A["junk"] = A["otmp"][0].bitcast(BF16)
    common_off = arena.off
    A["memt"] = arena.alloc([1024], F32)
    A["memT"] = arena.alloc([8, 256], BF16)
    for l in range(2):
        mem_kv(l, A)
    P.barrier()
    arena.off = common_off

    lnx_b = arena.alloc([2, 768], F32)
    P.dma("sp", lnx_b, lnx_d, (), ["lnx_b"])
    hT = arena.alloc([8, 129], BF16)
    carry = arena.alloc([26], F32, parts=64)
    Zg = arena.alloc([8, 129], F32, parts=64)
    Zd = arena.alloc([8, 128], F32, parts=64)
    Zw = arena.alloc([2, 129], F32, parts=64)
    Zwd = arena.alloc([2, 128], F32, parts=64)
    twd = arena.alloc([128], BF16, parts=65)
    adb = arena.alloc([128], BF16, parts=65)
    sgw, av, Lp, kk, t2, t3 = [arena.alloc([4, 128], F32, parts=64) for _ in range(6)]
    eP, ePi, ePm = [arena.alloc([4, 128], BF16, parts=64) for _ in range(3)]
    DS = []
    for sl in range(3):
        d = {}
        for nm in ("aT", "rT", "bT", "kT", "rkT"):
            d[nm] = arena.alloc([4, 128], BF16, parts=64)
        d["PCt"] = arena.alloc([4, 1], F32, parts=64)
        DS.append(d)
    IO = []
    for sl in range(2):
        d = {}
        for nm in ("Tm0", "AakT", "ArbT", "ArkT"):
            d[nm] = arena.alloc([4, 128], BF16)
        d["BK"] = arena.alloc([8, 64], BF16)
        IO.append(d)
    PQs = [arena.alloc([2, 4, 128], BF16) for _ in range(2)]
    Tm1s = arena.alloc([4, 128], BF16)
    R1s = arena.alloc([4, 64], BF16)
    Ub = arena.alloc([4, 64], BF16)
    bs2 = [arena.alloc([12, 1], F32) for _ in range(2)]
    vps = arena.alloc([768], F32)
    Vt2 = [arena.alloc([768], BF16) for _ in range(2)]
    sg2 = [arena.alloc([1024], BF16) for _ in range(2)]
    qmT2 = [arena.alloc([4, 128], BF16, parts=64) for _ in range(2)]
    Mf = arena.alloc([12, 64], F32, parts=64)
    Mb = arena.alloc([12, 64], BF16, parts=64)
    Y2 = [arena.alloc([12, 64], F32) for _ in range(2)]
    Ysq = arena.alloc([12, 64], F32)
    st = arena.alloc([8, 12], F32)
    YG0 = arena.alloc([1024], BF16)

    if nlayers >= 1:
        P.dma("sp", postg_b, postg_d[:, 0, :], (), ["postg_b"])
        P.memset("pool", carry, 0.0, ["carry"])
        P.memset("pool", Mf, 0.0, ["Mf"])
        P.memset("pool", Mb, 0.0, ["Mb"])
        P.memset("pool", hT, 0.0, ["hT"])
        P.memset("pool", twd, 1.0, ["twd"])
        P.memset("pool", adb, 1.0, ["adb"])
    kk64 = headp[:, 0, :]
    ka64 = headp[:, 1, :]
    rk64 = headp[:, 2, :]
    if nlayers >= 1:
        P.ts("dve", lnx_b, lnx_b, 0.5, None, ALU.mult, None, ["lnx_b"], ["lnx_b"])
        P.ts("dve", rkc, bc(rk64.unsqueeze(2), [64, 12, 2]), 0.5, None, ALU.mult, None, ["headp"], ["rkc"])
    flat = lambda t: t.rearrange("p a b -> p (a b)")
    v4 = lambda t, a=4: t.rearrange("p (a b) -> p a b", a=a)

    def pre(i):
        xk = "x%d" % i
        tp = i % 2
        Vt, sg, qmT = Vt2[tp], sg2[tp], qmT2[tp]
        vk, sk, qk = "Vt%d" % tp, "sg%d" % tp, "qmT%d" % tp
        P.copy("pool", hT[:, :, 0:1], hT[:, :, 128:129], ["hT"], ["hT"])
        yield
        rmsnorm_T(x_res[:, i, :], [xk], [(0, hT[:, :, 1:129], "hT")], A)
        yield
        hcur = (hT[:, :, 1:129], "hT")
        hprev = (hT[:, :, 0:128], "hT")
        for q in range(3):
            (v1, v1k), (vp1, vp1k) = tm_job(ws_in, k_in, 1536 + q * 256, 256, [hcur, hprev], [2, 1])
            yield
            cs = slice(q * 256, (q + 1) * 256)
            P.copy("act", vps[:, cs], vp1[:, 0:256], [vp1k], ["vps%d" % q])
            P.tt("dve", vps[:, cs], vps[:, cs], v1[:, 0:256], ALU.subtract, ["vps%d" % q, v1k], ["vps%d" % q])
            yield
            P.tt("pool", vps[:, cs], vps[:, cs], muv_b[:, cs], ALU.mult, ["vps%d" % q, "muv_b"], ["vps%d" % q])
            P.tt("dve", Vt[:, cs], v1[:, 0:256], vps[:, cs], ALU.add, ["vps%d" % q, v1k], [vk])
            yield
        for q in range(4):
            c0 = 2432 + q * 256 if q < 3 else 3456
            (g1, g1k), = tm_job(ws_in, k_in, c0, 256, [hcur], [2])
            P.act(sg[:, q * 256:(q + 1) * 256], g1[:, 0:256], AF.Tanh, [g1k], [sk], scale=0.5)
            P.stt(sg[:, q * 256:(q + 1) * 256], sg[:, q * 256:(q + 1) * 256], 1.0, g1[:, 0:256], ALU.add, ALU.mult,
                  [sk, g1k], [sk])
            yield
        qb, qbk = fm64_job(ws_in, k_in, 3200, hT[:, :, 1:129], "hT")
        P.copy("act", qmT, v4(qb[0:64, :]), [qbk], [qk])
        yield
        wt, wkey = stage(ws_in, k_in, 2304, 128)
        bk, bkey = P.bank()
        for g in range(2):
            for kc in range(8):
                P.mm(bk[0:64, g * 128:(g + 1) * 128], wt[:, kc, g * 64:(g + 1) * 64], hT[:, kc, 1:129], kc == 0, kc == 7,
                     ["hT", wkey], [bkey])
        P.copy("pool", Zw[:, :, 0:1], carry[:, 24:26].unsqueeze(2), ["carry"], ["Zw"])
        P.copy("act", Zw[:, :, 1:129], v4(bk[0:64, 0:256], 2), [bkey], ["Zw"])
        yield
        P.tt("pool", Zwd, Zw[:, :, 0:128], Zw[:, :, 1:129], ALU.subtract, ["Zw"], ["Zwd"])
        P.tt("pool", Zwd, Zwd, bc(mu64[:, 24:26].unsqueeze(2), [64, 2, 128]), ALU.mult, ["Zwd", "mu64"], ["Zwd"])
        P.tt("dve", Zwd, Zw[:, :, 1:129], Zwd, ALU.add, ["Zw", "Zwd"], ["Zwd"])
        P.copy("pool", carry[:, 24:26].unsqueeze(2), Zw[:, :, 128:129], ["Zw"], ["carry"])
        yield
        P.act(twd[0:64, :], Zwd[:, 0, :], AF.Tanh, ["Zwd"], ["twd"])
        P.copy("act", adb[0:64, :], Zwd[:, 1, :], ["Zwd"], ["adb"])
        yield

    def dphase(i, g, sl):
        S_ = DS[sl]
        sk_ = lambda nm: "%s_%d" % (nm, sl)
        aT, rT, bT, kT, rkT, PCt = (S_[n] for n in ("aT", "rT", "bT", "kT", "rkT", "PCt"))
        rb, rbk = fm64_job(ws_in, k_in, g * 256, hT[:, :, 1:129], "hT")
        yield
        kb, kbk = fm64_job(ws_in, k_in, 768 + g * 256, hT[:, :, 1:129], "hT")
        yield
        P.copy("pool", Zg[:, :, 0:1], carry[:, g * 8:(g + 1) * 8].unsqueeze(2), ["carry"], ["Zg"])
        P.copy("act", Zg[:, 0:4, 1:129], v4(rb[0:64, :]), [rbk], ["Zg"])
        P.copy("act", Zg[:, 4:8, 1:129], v4(kb[0:64, :]), [kbk], ["Zg"])
        yield
        P.tt("pool", Zd, Zg[:, :, 0:128], Zg[:, :, 1:129], ALU.subtract, ["Zg"], ["Zd"])
        yield
        P.tt("pool", Zd, Zd, bc(mu64[:, g * 8:(g + 1) * 8].unsqueeze(2), [64, 8, 128]), ALU.mult, ["Zd", "mu64"], ["Zd"])
        yield
        P.tt("dve", Zd, Zg[:, :, 1:129], Zd, ALU.add, ["Zg", "Zd"], ["Zd"])
        P.copy("pool", carry[:, g * 8:(g + 1) * 8].unsqueeze(2), Zg[:, :, 128:129], ["Zg"], ["carry"])
        yield
        r_ = Zd[:, 0:4, :]
        k_ = Zd[:, 4:8, :]
        bw, bwk = P.bank()
        ba, bak = P.bank()
        for j in range(4):
            h = g * 4 + j
            P.mm(bw[0:64, j * 128:(j + 1) * 128], w2aug[:, h * 64:(h + 1) * 64], twd[:, :], True, True,
                 ["w2aug", "twd"], [bwk])
        for j in range(4):
            h = g * 4 + j
            P.mm(ba[0:64, j * 128:(j + 1) * 128], a2aug[:, h * 64:(h + 1) * 64], adb[:, :], True, True,
                 ["a2aug", "adb"], [bak])
        yield
        P.act(sgw, v4(bw[0:64, :]), AF.Tanh, [bwk], ["sgw"], scale=0.5)
        P.act(av, v4(ba[0:64, :]), AF.Tanh, [bak], ["av"], scale=0.5)
        P.act(sgw, sgw, AF.Copy, ["sgw"], ["sgw"], bias=0.5, scale=0.5)
        P.act(av, av, AF.Copy, ["av"], ["av"], bias=0.5, scale=0.5)
        yield
        P.op("dve", lambda e: e.tensor_tensor_scan(out=flat(Lp), data0=cmask, data1=flat(sgw), initial=0.0,
                                                   op0=ALU.mult, op1=ALU.add), ["sgw", "cmask"], ["Lp"])
        P.tt("dve", kk, k_, bc(kk64[:, g * 4:(g + 1) * 4].unsqueeze(2), [64, 4, 128]), ALU.mult, ["Zd", "headp"], ["kk"])
        yield
        P.tt("pool", t2, kk, kk, ALU.mult, ["kk"], ["t2"])
        P.act(eP, Lp, AF.Exp, ["Lp"], ["eP"], scale=-CDEC)
        yield
        P.act(ePi, Lp, AF.Exp, ["Lp"], ["ePi"], scale=CDEC)
        P.tt("pool", sgw, Lp, sgw, ALU.subtract, ["Lp", "sgw"], ["sgw"])
        yield
        P.act(PCt, Lp[:, :, 127:128], AF.Exp, ["Lp"], [sk_("PCt")], scale=-CDEC)
        yield
        bsq, bsqk = P.bank()
        P.mm(bsq[0:64, :], ones64, flat(t2), True, True, ["ones64", "t2"], [bsqk])
        P.act(ePm, sgw, AF.Exp, ["sgw"], ["ePm"], scale=-CDEC)
        yield
        P.ts("dve", flat(t2), bsq[0:64, :], 1e-12, None, ALU.max, None, [bsqk], ["t2"])
        yield
        P.act(t2, t2, AF.Ln, ["t2"], ["t2"])
        P.stt(t3, av, -1.0, bc(ka64[:, g * 4:(g + 1) * 4].unsqueeze(2), [64, 4, 128]), ALU.add, ALU.mult,
              ["av", "headp"], ["t3"])
        yield
        P.act(t2, t2, AF.Exp, ["t2"], ["t2"], scale=-0.5)
        P.stt(t3, t3, 1.0, k_, ALU.add, ALU.mult, ["t3", "Zd"], ["t3"])
        yield
        P.tt("dve", rT, r_, eP, ALU.mult, ["Zd", "eP"], [sk_("rT")])
        P.tt("pool", kT, t3, ePi, ALU.mult, ["t3", "ePi"], [sk_("kT")])
        yield
        P.tt("dve", kk, kk, t2, ALU.mult, ["kk", "t2"], ["kk"])
        yield
        P.stt(aT, kk, -1.0, ePm, ALU.mult, ALU.mult, ["kk", "ePm"], [sk_("aT")])
        P.tt("pool", av, kk, av, ALU.mult, ["kk", "av"], ["av"])
        yield
        P.tt("dve", bT, av, ePi, ALU.mult, ["av", "ePi"], [sk_("bT")])
        P.tt("pool", rkT, r_, t3, ALU.mult, ["Zd", "t3"], [sk_("rkT")])
        yield

    def istage(i, g, n):
        dsl, isl = n % 3, n % 2
        S_ = DS[dsl]
        O_ = IO[isl]
        dk_ = lambda nm: "%s_%d" % (nm, dsl)
        ik_ = lambda nm: "%s_i%d" % (nm, isl)

        def sk_(nm):
            if nm in ("aT", "rT", "bT", "kT", "rkT", "PCt"):
                return dk_(nm)
            if nm in ("Tm0", "AakT", "ArbT", "ArkT", "BK"):
                return ik_(nm)
            return nm + "_s"
        aT, rT, bT, kT, rkT, PCt = (S_[q] for q in ("aT", "rT", "bT", "kT", "rkT", "PCt"))
        BK, AakT, ArbT, ArkT = (O_[q] for q in ("BK", "AakT", "ArbT", "ArkT"))
        Qm = [PQs[q][:, 0, :, :] for q in range(2)]
        Pm = [PQs[q][:, 1, :, :] for q in range(2)]
        Tm = [O_["Tm0"], Tm1s]
        tp = i % 2

        def amat(lh, lkey, rh, rkey, mask, mkey, eng, oap, okey):
            bk, bkey = P.bank()
            for j in range(4):
                P.mm(bk[:, j * 128:(j + 1) * 128], lh[:, j, :], rh[:, j, :], True, True, [lkey, rkey], [bkey])
            P.tt(eng, oap, v4(bk[:, :]), bc(mask.unsqueeze(1), [128, 4, 128]), ALU.mult, [bkey, mkey], [okey])

        amat(bT, sk_("bT"), aT, sk_("aT"), maskSU, "maskSU", "dve", Pm[0], sk_("Pm0"))
        yield
        amat(aT, sk_("aT"), bT, sk_("bT"), maskSL, "maskSL", "dve", Qm[0], sk_("Qm0"))
        P.tt("pool", Tm[0], Pm[0], bc(ident.unsqueeze(1), [128, 4, 128]), ALU.add, [sk_("Pm0"), "ident"], [sk_("Tm0")])
        yield
        cur = 0
        extra = [
            lambda: amat(kT, sk_("kT"), aT, sk_("aT"), maskSU, "maskSU", "dve", AakT, sk_("AakT")),
            lambda: amat(bT, sk_("bT"), rT, sk_("rT"), maskU, "maskU", "dve", ArbT, sk_("ArbT")),
            lambda: amat(kT, sk_("kT"), rT, sk_("rT"), maskU, "maskU", "dve", ArkT, sk_("ArkT")),
        ]

        def ephase_a():
            bk, bkey = P.bank()
            bv = bk[:, 0:256].bitcast(BF16).rearrange("p (a b) -> p a b", a=8)
            for j in range(4):
                P.tr(bv[:, j, :], bT[:, j, :], ident[0:64, 0:64], [sk_("bT"), "ident"], [bkey])
            for j in range(4):
                P.tr(bv[:, 4 + j, :], kT[:, j, :], ident[0:64, 0:64], [sk_("kT"), "ident"], [bkey])
            P.copy("act", BK, bv, [bkey], [sk_("BK")])

        def ephase_b():
            bk, bkey = P.bank()
            for j in range(4):
                P.mm(bk[:, 2 * j:2 * j + 2], rkT[:, j, :], onescol if 'norkc' in TOG else rkc[:, g * 4 + j, :], True, True, [sk_("rkT"), "rkc", "onescol"], [bkey])
            P.copy("dve", bs2[tp][:, g * 4:(g + 1) * 4, :], v4(bk[:, 0:8])[:, :, 0:1], [bkey], ["bs%d" % tp])

        extra += [ephase_a, ephase_b]
        for kr in range(1, 7):
            nxt = 1 - cur
            bA, bAk = P.bank()
            for j in range(4):
                P.mm(bA[:, j * 128:(j + 1) * 128], Pm[cur][:, j, :], Qm[cur][:, j, :], True, True,
                     [sk_("Pm%d" % cur), sk_("Qm%d" % cur)], [bAk])
            if kr <= 5:
                bB, bBk = P.bank()
                for j in range(4):
                    P.mm(bB[:, j * 128:(j + 1) * 128], Qm[cur][:, j, :], Pm[cur][:, j, :], True, True,
                         [sk_("Pm%d" % cur), sk_("Qm%d" % cur)], [bBk])
            if extra:
                extra.pop(0)()
            P.copy("act", Qm[nxt], v4(bA[:, :]), [bAk], [sk_("Qm%d" % nxt)])
            if kr <= 5:
                P.copy("act", Pm[nxt], v4(bB[:, :]), [bBk], [sk_("Pm%d" % nxt)])
            yield
            btk, btkk = P.bank()
            for j in range(4):
                P.mm(btk[:, j * 128:(j + 1) * 128], Qm[nxt][:, j, :], Tm[cur][:, j, :], True, True,
                     [sk_("Qm%d" % nxt), (sk_("Tm0") if cur == 0 else "Tm1_s")], [btkk])
            yield
            P.tt("dve", Tm[nxt], v4(btk[:, :]), Tm[cur], ALU.add, [btkk, (sk_("Tm0") if cur == 0 else "Tm1_s")], [(sk_("Tm0") if nxt == 0 else "Tm1_s")])
            yield
            cur = nxt
        while extra:
            extra.pop(0)()
            yield
        assert cur == 0
    def cstage(i, g, n):
        dsl, isl = n % 3, n % 2
        S_ = DS[dsl]
        O_ = IO[isl]
        dk_ = lambda nm: "%s_%d" % (nm, dsl)
        ik_ = lambda nm: "%s_i%d" % (nm, isl)

        def sk_(nm):
            if nm in ("aT", "rT", "bT", "kT", "rkT", "PCt"):
                return dk_(nm)
            if nm in ("Tm0", "AakT", "ArbT", "ArkT", "BK"):
                return ik_(nm)
            return nm + "_s"
        aT, rT, PCt = (S_[q] for q in ("aT", "rT", "PCt"))
        BK, AakT, ArbT, ArkT = (O_[q] for q in ("BK", "AakT", "ArbT", "ArkT"))
        tp = i % 2
        Vt, vk = Vt2[tp], "Vt%d" % tp
        Y = Y2[tp]
        Tt = O_["Tm0"]
        Ttk = ik_("Tm0")
        Mbk = "Mb%d" % g
        Mfk = "Mf%d" % g
        b1, b1k = P.bank()
        for j in range(4):
            h = g * 4 + j
            P.mm(b1[:, j * 64:(j + 1) * 64], aT[:, j, :], Mb[:, h, :], True, False, [sk_("aT"), Mbk, "Mb"], [b1k])
            P.mm(b1[:, j * 64:(j + 1) * 64], AakT[:, j, :], Vt[:, h * 64:(h + 1) * 64], False, True, [sk_("AakT"), vk], [b1k])
        P.copy("act", R1s, v4(b1[:, 0:256]), [b1k], [sk_("R1s")])
        yield
        b2, b2k = P.bank()
        for j in range(4):
            P.mm(b2[:, j * 64:(j + 1) * 64], Tt[:, j, :], R1s[:, j, :], True, True, [Ttk, sk_("R1s")], [b2k])
        P.copy("act", Ub, v4(b2[:, 0:256]), [b2k], [sk_("Ub")])
        yield
        b4, b4k = P.bank()
        for j in range(4):
            h = g * 4 + j
            P.mm(b4[0:64, j * 64:(j + 1) * 64], BK[:, j, :], Ub[:, j, :], True, False, [sk_("BK"), sk_("Ub")], [b4k])
            P.mm(b4[0:64, j * 64:(j + 1) * 64], BK[:, 4 + j, :], Vt[:, h * 64:(h + 1) * 64], False, True, [sk_("BK"), vk], [b4k])
        b3, b3k = P.bank()
        for j in range(4):
            h = g * 4 + j
            P.mm(b3[:, j * 64:(j + 1) * 64], rT[:, j, :], Mb[:, h, :], True, False, [sk_("rT"), Mbk, "Mb"], [b3k])
            P.mm(b3[:, j * 64:(j + 1) * 64], ArbT[:, j, :], Ub[:, j, :], False, False, [sk_("ArbT"), sk_("Ub")], [b3k])
            P.mm(b3[:, j * 64:(j + 1) * 64], ArkT[:, j, :], Vt[:, h * 64:(h + 1) * 64], False, True, [sk_("ArkT"), vk], [b3k])
        yield
        Mfg = Mf[:, g * 4:(g + 1) * 4, :]
        P.tt("dve", Mfg, Mfg, v4(b4[0:64, 0:256]), ALU.add, [Mfk, "Mf", b4k], [Mfk])
        P.tt("pool", Mfg, Mfg, bc(PCt, [64, 4, 64]), ALU.mult, [Mfk, sk_("PCt")], [Mfk])
        P.copy("act", Mb[:, g * 4:(g + 1) * 4, :], Mfg, [Mfk], [Mbk])
        yield
        P.copy("dve", Y[:, g * 4:(g + 1) * 4, :], v4(b3[:, 0:256]), [b3k], ["Y%d" % tp])
        yield

    def post(i):
        tp = i % 2
        Y = Y2[tp]
        Vt, sg, qmT = Vt2[tp], sg2[tp], qmT2[tp]
        vk, sk, qk = "Vt%d" % tp, "sg%d" % tp, "qmT%d" % tp
        s1, s2, mean, m2, var, sd, rs = [st[:, q, :].unsqueeze(2) for q in range(7)]
        P.op("dve", lambda e: e.tensor_reduce(out=s1, in_=Y, axis=AX.X, op=ALU.add), ["Y%d" % tp], ["s1"])
        P.tt("pool", Ysq, Y, Y, ALU.mult, ["Y%d" % tp], ["Ysq"])
        yield
        P.op("dve", lambda e: e.tensor_reduce(out=s2, in_=Ysq, axis=AX.X, op=ALU.add), ["Ysq"], ["s2"])
        P.ts("dve", mean, s1, 1.0 / 64, None, ALU.mult, None, ["s1"], ["mean"])
        P.tt("dve", m2, mean, mean, ALU.mult, ["mean"], ["m2"])
        P.stt(var, s2, 1.0 / 64, m2, ALU.mult, ALU.subtract, ["s2", "m2"], ["var"])
        yield
        P.ts("dve", sd, var, LNX_EPS, None, ALU.add, None, ["var"], ["sd"])
        rsqrt_op(P, rs, sd, mhalf.unsqueeze(2), ["sd"], ["rs"])
        yield
        P.tt("dve", Ysq, Y, bc(mean, [128, 12, 64]), ALU.subtract, ["Y%d" % tp, "mean", "Ysq"], ["Ysq"])
        yield
        P.tt("pool", Ysq, Ysq, bc(rs, [128, 12, 64]), ALU.mult, ["Ysq", "rs"], ["Ysq"])
        yield
        Yf = Ysq.rearrange("p a b -> p (a b)")
        P.tt("pool", Yf, Yf, lnx_b[:, 0, :], ALU.mult, ["Ysq", "lnx_b"], ["Ysq"])
        yield
        P.tt("pool", Yf, Yf, lnx_b[:, 1, :], ALU.add, ["Ysq", "lnx_b"], ["Ysq"])
        yield
        for h in range(12):
            P.stt(Ysq[:, h, :], Vt[:, h * 64:(h + 1) * 64], bs2[tp][:, h, :], Ysq[:, h, :], ALU.mult, ALU.add,
                  [vk, "bs%d" % tp, "Ysq"], ["Ysq"])
            if h % 4 == 3:
                yield
        P.tt("dve", YG0[:, 0:768], Yf, sg[:, 0:768], ALU.mult, ["Ysq", sk], ["YG0"])
        yield
        mem_attn(0, A, qmT, sg, YG0, qk=qk, sk=sk, yk="YG0")
        yield
        out_proj(0, i, A, YG0, final=False, yk="YG0")
        yield

    def interleave_n(gens, rates=None):
        gens = list(gens)
        rates = rates or [1] * len(gens)
        alive = [True] * len(gens)
        while any(alive):
            for q in range(len(gens)):
                for _ in range(rates[q]):
                    if alive[q]:
                        try:
                            next(gens[q])
                        except StopIteration:
                            alive[q] = False

    def chain_gens(gs):
        for g_ in gs:
            yield from g_

    def interleave(ga, gb):
        alive = [True, True]
        gens = [ga, gb]
        while alive[0] or alive[1]:
            for q in range(2):
                if alive[q]:
                    try:
                        next(gens[q])
                    except StopIteration:
                        alive[q] = False

    if nlayers >= 1:
        units = [(i, g) for i in range(NT) for g in range(3)]

        NU = len(units)

        def dstage(n):
            i, g = units[n]
            gs = []
            if g == 0:
                if i + 2 < NT:
                    load_x(i + 2)
                if i == 1:
                    late_precasts()
                gs.append(pre(i))
            gs.append(dphase(i, g, n % 3))
            return chain_gens(gs)

        CSLOW = int(os.environ.get('KCS', '4'))

        def slow(gen, k):
            for _ in gen:
                yield
                for _q in range(k - 1):
                    yield

        for n in range(-2, NU):
            gens = []
            if n >= 0:
                gens.append(slow(cstage(units[n][0], units[n][1], n), CSLOW))
            if 0 <= n + 1 < NU:
                gens.append(istage(units[n + 1][0], units[n + 1][1], n + 1))
            if n + 2 < NU:
                gens.append(dstage(n + 2))
            if n >= 0 and units[n][1] == 0 and units[n][0] > 0:
                gens.append(post(units[n][0] - 1))
            interleave_n(gens)
        for _ in post(NT - 1):
            pass

    final_ops = []
    if dbg or nlayers < 2:
        tgt = dbg_d if dbg else out_d
        for i in range(NT):
            final_ops.append(P.dma("sp", tgt[i * 128:(i + 1) * 128, :], x_res[:, i, :], ["x%d" % i], []))

    if nlayers >= 2:
        P.barrier()
        arena.off = common_off
        P.dma("sp", postg_b, postg_d[:, 1, :], (), ["postg_b"])
        mark1 = arena.off
        dtmp = arena.alloc([128], F32)
        tq64 = arena.alloc([128], F32)
        itab = arena.alloc([16], F32)
        ndtmp = arena.alloc([128], F32)
        lamv = arena.alloc([4, 64], F32)
        lamw = arena.alloc([4, 64], F32)
        arena.off = mark1
        arena.off = arena.off + 2048 + 2048 + 512 + 512 + 64
        KTt, Vat = [], []
        for i_ in range(NT):
            if i_ < 3:
                KTt.append(arena.alloc([6, 128], BF16))
                Vat.append(arena.alloc([6, 129], BF16))
            else:
                xv = x_res[:, i_ - 3, :].bitcast(BF16)
                KTt.append(xv[:, 0:768].rearrange("p (h t) -> p h t", h=6))
                Vat.append(xv[:, 768:768 + 774].rearrange("p (h e) -> p h e", h=6))
        PT4 = [arena.alloc([2, NT, 128], BF16) for _ in range(2)]
        P.dma("sp", lamv, lamv_d, (), ["lamv"])
        hT1 = arena.alloc([8, 128], BF16)
        kvT = arena.alloc([8, 128], BF16)
        QT2 = [arena.alloc([6, 2, 128], BF16) for _ in range(2)]
        sg2b = [arena.alloc([1024], BF16) for _ in range(3)]
        qmT2b = [arena.alloc([4, 128], BF16, parts=64) for _ in range(3)]
        AT = arena.alloc([6, 128], F32)
        ATs = arena.alloc([6, 128], F32)
        YG = arena.alloc([1024], BF16)
        Dh = arena.alloc([6, 128], BF16)
        btab = arena.alloc([6, 16], F32)
        rzz = arena.alloc([2, 1], F32)
        w1 = arena.alloc([1], F32)
        s6 = arena.alloc([6, 1], F32)
        r6 = arena.alloc([6, 1], F32)
        stmp = arena.alloc([2, 128], F32)
        t1 = arena.alloc([128], F32)
        RES = {}
        cand = ([(ws_kv, k_kv, c0) for c0 in range(0, 1536, 256)] + [(ws_out[1], k_out[1], c0) for c0 in range(0, 1024, 256)]
                + [(ws_bin, k_bin, c0) for c0 in range(0, 768, 256)])
        for (ws_, kk_, c0) in cand:
            if arena.off + 4096 > arena.cap:
                break
            t_ = arena.alloc([8, 256], BF16)
            rk_ = "res_%s_%d" % (ws_.name, c0)
            P.dma("sp", t_, ws_.t[ws_.pieces[c0][0]], kk_[c0], [rk_])
            RES[(ws_.name, c0)] = (t_, rk_)
        resident[0] = RES
        print("resident pieces", len(RES))
        P.op("pool", lambda e: e.iota(dtmp, pattern=[[1, 128]], base=0, channel_multiplier=-1,
                                      allow_small_or_imprecise_dtypes=True), (), ["dtmp"])
        P.op("pool", lambda e: e.iota(tq64, pattern=[[1, 128]], base=-64, channel_multiplier=0,
                                      allow_small_or_imprecise_dtypes=True), (), ["tq64"])
        P.op("pool", lambda e: e.iota(itab, pattern=[[-128, 16]], base=-64, channel_multiplier=1,
                                      allow_small_or_imprecise_dtypes=True), (), ["itab"])
        P.op("pool", lambda e: e.iota(ndtmp, pattern=[[-1, 128]], base=0, channel_multiplier=1,
                                      allow_small_or_imprecise_dtypes=True), (), ["ndtmp"])
        P.tt("dve", dtmp, dtmp, ndtmp, ALU.max, ["dtmp", "ndtmp"], ["dtmp"])
        for h in range(6):
            P.ts("dve", Dh[:, h, :], dtmp, -SLOPES[h], None, ALU.mult, None, ["dtmp"], ["Dh"])
            P.stt(Dh[:, h, :], tq64, SLOPES[h], Dh[:, h, :], ALU.mult, ALU.add, ["tq64", "Dh"], ["Dh"])
            P.ts("dve", btab[:, h, :], itab, SLOPES[h], None, ALU.mult, None, ["itab"], ["btab"])
        P.memset("pool", Dh[64:128, :, 0:64], -30000.0, ["Dh"])
        P.tt("dve", lamw[:, 0, :], lamv[:, 0, :], lamv[:, 1, :], ALU.mult, ["lamv"], ["lamw"])
        P.tt("dve", lamw[:, 1, :], lamv[:, 2, :], lamv[:, 3, :], ALU.mult, ["lamv", "lamw"], ["lamw"])
        P.op("dve", lambda e: e.tensor_reduce(out=lamc[:, 0:2].unsqueeze(2), in_=lamw[:, 0:2, :], axis=AX.X, op=ALU.add),
             ["lamw"], ["lamc"])
        P.act(lamc[:, 2:4], lamc[:, 0:2], AF.Exp, ["lamc"], ["lamc2"])
        P.tt("dve", lamc[:, 4:5], lamc[:, 3:4], lamc[:, 2:3], ALU.subtract, ["lamc2"], ["lamc4"])
        P.ts("dve", lamc[:, 5:6], lamc[:, 4:5], -LAM_INIT1, None, ALU.add, None, ["lamc4"], ["neglam"])
        neglam = lamc[:, 5:6]
        P.ts("dve", subln_b, subln_b, 0.5 * (1.0 - LAM_INIT1), None, ALU.mult, None, ["subln_b"], ["subln_b"])
        for q in range(2):
            P.memset("pool", QT2[q], 0.0, ["QT%d" % q])

        def proj1(i):
            xk = "x%d" % i
            tp = i % 2
            t3_ = i % 3
            QT, sg, qmT = QT2[tp], sg2b[t3_], qmT2b[t3_]
            qtk, sk, qk = "QT%d" % tp, "sgb%d" % t3_, "qmTb%d" % t3_
            rmsnorm_T(x_res[:, i, :], [xk], [(4, kvT, "kvT"), (1, hT1, "hT1")], A)
            xdead = ["x%d" % (i - 3)] if i >= 3 else []
            P.memset("pool", Vat[i][:, :, 128:129], 1.0, ["Vaug%d" % i] + xdead)
            yield
            for hh in range(0, 6, 2):
                wt, wkey = stage(ws_kv, k_kv, hh * 128, 256)
                bk, bkey = P.bank()
                for q in range(2):
                    for kc in range(8):
                        P.mm(bk[:, q * 128:(q + 1) * 128], wt[:, kc, q * 128:(q + 1) * 128], kvT[:, kc, :], kc == 0, kc == 7,
                             ["kvT", wkey], [bkey])
                P.copy("dve", KTt[i][:, hh:hh + 2, :], bk[:, 0:256].rearrange("p (a b) -> p a b", a=2),
                       [bkey], ["KT%d" % i] + xdead)
                yield
            for q in range(3):
                (vb1, vb1k), = tm_job(ws_kv, k_kv, 768 + q * 256, 256, [(kvT, "kvT")], None)
                P.copy("dve", Vat[i][:, 2 * q:2 * q + 2, 0:128], vb1[:, 0:256].rearrange("p (a b) -> p a b", a=2),
                       [vb1k], ["Vaug%d" % i] + xdead)
                yield
            for hh in range(0, 6, 2):
                wt, wkey = stage(ws_bin, k_bin, hh * 128, 256)
                bk, bkey = P.bank(nr=2)
                for q in range(2):
                    for kc in range(8):
                        P.mm(bk[:, q * 128:(q + 1) * 128], wt[:, kc, q * 128:(q + 1) * 128], hT1[:, kc, :], kc == 0, kc == 7,
                             ["hT1", wkey], [bkey])
                for c in range(2):
                    P.copy("dve", QT[c * 64:(c + 1) * 64, hh:hh + 2, c, :],
                           bk[c * 64:(c + 1) * 64, 0:256].rearrange("p (a b) -> p a b", a=2), [bkey, qtk], [qtk])
                yield
            for q in range(4):
                c0 = 768 + q * 256 if q < 3 else 1792
                (g1, g1k), = tm_job(ws_bin, k_bin, c0, 256, [(hT1, "hT1")], [2])
                P.act(sg[:, q * 256:(q + 1) * 256], g1[:, 0:256], AF.Tanh, [g1k], [sk], scale=0.5)
                P.stt(sg[:, q * 256:(q + 1) * 256], sg[:, q * 256:(q + 1) * 256], 1.0, g1[:, 0:256], ALU.add, ALU.mult,
                      [sk, g1k], [sk])
                yield
            qb, qbk = fm64_job(ws_bin, k_bin, 1536, hT1, "hT1")
            P.copy("dve", qmT, qb[0:64, :].rearrange("p (a b) -> p a b", a=4), [qbk], [qk])
            yield

        def attn1(i):
            xk = "x%d" % i
            tp = i % 2
            QT = QT2[tp]
            qtk = "QT%d" % tp

            def scores(h):
                PT = PT4[h % 2]
                ptk = "PT%d" % (h % 2)
                for j in range(i + 1):
                    bk, bkey = P.bank()
                    P.mm(bk[:, 0:256], KTt[j][:, h, :], QT[:, h, :, :].rearrange("p c q -> p (c q)"),
                         True, True, ["KT%d" % j, qtk], [bkey])
                    sv = bk[:, 0:256].rearrange("p (c q) -> p c q", c=2)
                    if j == i:
                        P.stt(stmp, sv, 0.125, bc(Dh[:, h, :].unsqueeze(1), [128, 2, 128]), ALU.mult, ALU.add,
                              [bkey, "Dh"], ["stmp"])
                        P.act(PT[:, :, j, :], stmp, AF.Exp, ["stmp"], [ptk])
                    else:
                        P.act(PT[:, :, j, :], sv, AF.Exp, [bkey, "btab"], [ptk],
                              bias=btab[:, h, (i - j):(i - j) + 1], scale=0.125)
                    if j % 2 == 1:
                        yield

            def pv(h):
                PT = PT4[h % 2]
                ptk = "PT%d" % (h % 2)
                bk, bkey = P.bank(6, 8, nr=3)
                ov = bk[:, 0:258].rearrange("p (a b) -> p a b", a=2)
                for c in range(2):
                    for j in range(i + 1):
                        P.mm(ov[:, c, :], PT[:, c, j, :], Vat[j][:, h, :], j == 0, j == i, [ptk, "Vaug%d" % j], [bkey])
                        if j % 4 == 3:
                            yield
                P.op("dve", lambda e, ov=ov: e.reciprocal(out=rzz, in_=ov[:, :, 128:129]), [bkey], ["rzz"])
                P.tt("dve", w1, rzz[:, 1, :], neglam, ALU.mult, ["rzz", "neglam"], ["w1"])
                P.ts("dve", t1, ov[:, 1, 0:128], w1, None, ALU.mult, None, [bkey, "w1"], ["t1"])
                P.stt(AT[:, h, :], ov[:, 0, 0:128], rzz[:, 0, :], t1, ALU.mult, ALU.add, [bkey, "rzz", "t1"], ["AT"])
                yield

            yield from scores(0)
            for h in range(6):
                if h + 1 < 6:
                    yield from scores(h + 1)
                yield from pv(h)

        def tail1(i):
            t3_ = i % 3
            sg, qmT = sg2b[t3_], qmT2b[t3_]
            sk, qk = "sgb%d" % t3_, "qmTb%d" % t3_
            P.tt("pool", ATs, AT, AT, ALU.mult, ["AT"], ["ATs"])
            P.op("dve", lambda e: e.tensor_reduce(out=s6, in_=ATs, axis=AX.X, op=ALU.add), ["ATs"], ["s6"])
            P.ts("dve", r6, s6, 1.0 / 128, NORM_EPS, ALU.mult, ALU.add, ["s6"], ["r6"])
            rsqrt_op(P, s6, r6, mhalf[:, 0:6].unsqueeze(2), ["r6"], ["s6"])
            yield
            P.tt("pool", ATs, AT, bc(s6, [128, 6, 128]), ALU.mult, ["AT", "s6", "ATs"], ["ATs"])
            P.tt("pool", ATs, ATs, bc(subln_b.unsqueeze(1), [128, 6, 128]), ALU.mult, ["ATs", "subln_b"], ["ATs"])
            yield
            P.tt("dve", YG[:, 0:768], ATs.rearrange("p a b -> p (a b)"), sg[:, 0:768], ALU.mult, ["ATs", sk], ["YG1"])
            yield
            mem_attn(1, A, qmT, sg, YG, qk=qk, sk=sk, yk="YG1")
            yield
            final_ops.append(out_proj(1, i, A, YG, final=True, yk="YG1"))
            yield

        def interleave2(ga, gb, ra=1, rb=1):
            alive = [True, True]
            gens = [ga, gb]
            rr = [ra, rb]
            while alive[0] or alive[1]:
                for q in range(2):
                    for _ in range(rr[q]):
                        if alive[q]:
                            try:
                                next(gens[q])
                            except StopIteration:
                                alive[q] = False

        P.bank_rng = (0, 6)
        for _ in proj1(0):
            pass
        for i in range(NT):
            na = max(1, (6 * ((i + 2) // 2 + (i + 1) // 4 + 2) + 6) // 18)
            gens, rates = [attn1(i)], [na]
            if i + 1 < NT:
                gens.append(proj1(i + 1))
                rates.append(1)
            if i > 0:
                gens.append(tail1(i - 1))
                rates.append(1)
            interleave_n(gens, rates)
        for _ in tail1(NT - 1):
            pass

    print("arena use", arena.off, "cst use", cst.off)
    P.emit(final_wait_ops=final_ops)
    return nc, P


_CACHE = {}


def prep_inputs(inp, b):
    f = lambda a: np.ascontiguousarray(np.asarray(a, dtype=np.float32))
    mu = f(inp["a_shift_mu"])[0]
    cols = []
    for g in range(3):
        for j in range(4):
            cols.append(np.arange((g * 4 + j) * 64, (g * 4 + j + 1) * 64))
        for j in range(4):
            cols.append(768 + np.arange((g * 4 + j) * 64, (g * 4 + j + 1) * 64))
    cols.append(np.arange(2304, 2368))
    cols.append(np.arange(2368, 2432))
    mu64 = np.stack([mu[c] for c in cols], axis=1)
    gains = np.stack([f(inp["pre_norm"])[0], f(inp["pre_norm"])[1], f(inp["mem_norm"])[0], f(inp["mem_norm"])[1],
                      f(inp["kv_norm"])], 0)
    gains = gains.reshape(5, 8, 128).transpose(2, 0, 1)
    hp = np.stack([f(inp["a_k_k"])[0].reshape(12, 64).T, f(inp["a_k_a"])[0].reshape(12, 64).T,
                   f(inp["a_r_k"])[0].T], 1)
    d = {
        "x": f(inp["x"][b]),
        "mem": f(inp["mem"][b]),
        "a_w_in": f(inp["a_w_in"])[0],
        "w_out": f(inp["w_out"]),
        "w_mem_kv": f(inp["w_mem_kv"]),
        "w_kv": f(inp["w_kv"]),
        "b_w_in": f(inp["b_w_in"])[0],
        "gains": f(gains),
        "mu64": f(mu64),
        "muv_b": f(np.broadcast_to(mu[1536:2304], (128, 768))),
        "w2aug": f(np.concatenate([f(inp["a_w2"])[0], f(inp["a_w0"])], 0)),
        "a2aug": f(np.concatenate([f(inp["a_a2"])[0], f(inp["a_a0"])], 0)),
        "headp": f(hp),
        "lnx_b": f(np.broadcast_to(np.stack([f(inp["a_lnx_w"])[0], f(inp["a_lnx_b"])[0]], 0), (128, 2, 768))),
        "postg_b": f(np.broadcast_to(f(inp["post_norm"]), (128, 2, 1024))),
        "subln_b": f(np.broadcast_to(f(inp["b_subln"])[0], (128, 128))),
        "lamv_b": f(np.broadcast_to(np.stack([f(inp["b_lam_q1"])[0], f(inp["b_lam_k1"])[0], f(inp["b_lam_q2"])[0],
                                              f(inp["b_lam_k2"])[0]], 0), (128, 4, 64))),
    }
    return d


def kernel(**inputs):
    if "nc" not in _CACHE:
        _CACHE["nc"] = build()[0]
    nc = _CACHE["nc"]
    in_maps = [prep_inputs(inputs, b) for b in range(8)]
    res = run_bass_kernel_spmd(nc, in_maps, core_ids=list(range(8)))
    out = np.stack([np.asarray(r["out"], dtype=np.float32) for r in res.results], 0)
    return out
```

```python
import math
import os
TOG = 'nomerge,nobank2'
SCHED = os.environ.get('KSCHED', '1') == '1'
PEF, ACTF, DVEF, POOLF, LATF = [float(v) for v in os.environ.get('KF', '0.93,1.07,0.98,0.80,0.25').split(',')]
FREE_SP = os.environ.get('KFSP', '1') == '1'
DMAX = float(os.environ.get('KDX', '0.9'))
QUANT = float(os.environ.get('KQ', '0.05'))
RATES0 = [int(v) for v in os.environ.get('KR0', '1,1,1').split(',')]
from contextlib import ExitStack

import numpy as np
import concourse.bass as bass
import concourse.mybir as mybir
from concourse.bass_utils import run_bass_kernel_spmd

F32 = mybir.dt.float32
BF16 = mybir.dt.bfloat16
AF = mybir.ActivationFunctionType
ALU = mybir.AluOpType
AX = mybir.AxisListType

S = 2048
D = 1024
NT = 16
CDEC = math.exp(-0.5)
NORM_EPS = 1e-6
LNX_EPS = 64e-5
SLOPES = [0.25, 0.0625, 0.015625, 0.00390625, 0.5, 0.125]
LAM_INIT1 = 0.8 - 0.6 * math.exp(-0.3 * 1)

NDMA_SLOTS = 8
COMPUTE = ("pe", "act", "dve", "pool")
QUEUES = ("sp", "actq", "poolq")
STREAM_OF = {"pe": "tensor", "act": "scalar", "dve": "vector", "pool": "gpsimd",
             "sp": "sync", "actq": "scalar", "poolq": "gpsimd"}
STREAMS = ["tensor", "scalar", "vector", "gpsimd", "sync"]


class Prog:
    def __init__(self, nc):
        self.nc = nc
        self.es = ExitStack()
        self.ops = []
        self.last_w = {}
        self.readers = {}
        self.ndma = {q: 0 for q in QUEUES}
        self.last_slot = {}
        self.barrier_deps = []
        self.nbank = 0
        self.nbank2 = 0
        self.bank_rng = (0, 8)
        self.pending = {}
        self.last_q = {}
        self.seg = 0
        self.banks = None

    def sb(self, name, shape, dt=F32):
        return self.es.enter_context(self.nc.sbuf_tensor(name, list(shape), dt))

    def ps(self, name, shape, dt=F32):
        return self.es.enter_context(self.nc.psum_tensor(name, list(shape), dt))

    def bank(self, lo=None, hi=None, nr=1):
        if lo is None:
            lo, hi = self.bank_rng
        n = hi - lo
        for t in range(n):
            i = lo + ((self.nbank + t) % n)
            key = "bank%d" % i
            if self.pending.get(key, 0) <= 0:
                self.nbank += t + 1
                self.pending[key] = nr
                return self.banks[i], key
        raise RuntimeError("no free PSUM bank in [%d,%d): %s" % (lo, hi, self.pending))

    def bank2(self, lo=0, hi=8):
        n = (hi - lo) // 2
        i = lo + 2 * (self.nbank2 % n)
        self.nbank2 += 1
        return self.psall[:, i:i + 2, :], ["bank%d" % i, "bank%d" % (i + 1)]

    def barrier(self):
        self.seg += 1
        last = {}
        for i, o in enumerate(self.ops):
            last[o["veng"]] = i
        self.barrier_deps = sorted(last.values())

    def op(self, eng, fn, reads=(), writes=(), dur=None):
        idx = len(self.ops)
        if eng != "pe":
            for k in reads:
                if k in self.pending and self.pending[k] > 0:
                    self.pending[k] -= 1
        raw = set(self.barrier_deps) if not SCHED else set()
        oth = set()
        for k in reads:
            if k in self.last_w:
                raw.add(self.last_w[k])
        for k in writes:
            if k in self.last_w:
                oth.add(self.last_w[k])
            for r in self.readers.get(k, ()):
                oth.add(r)
        if eng in QUEUES:
            n = self.ndma[eng]
            self.ndma[eng] += 1
            veng = (eng, n % NDMA_SLOTS)
            if veng in self.last_slot:
                raw.add(self.last_slot[veng])
            self.last_slot[veng] = idx
        else:
            veng = (eng, 0)
        deps = set(raw)
        odeps = set()
        for d in oth:
            if self.ops[d]["veng"] == veng and eng in COMPUTE:
                odeps.add(d)
                continue
            deps.add(d)
        if eng in QUEUES:
            if eng in self.last_q and not (eng == "sp" and FREE_SP):
                odeps.add(self.last_q[eng])
            self.last_q[eng] = idx
        deps.discard(idx)
        odeps.discard(idx)
        if dur is None:
            dur = {"pe": 0.11, "act": 0.5, "dve": 0.6, "pool": 1.0}.get(eng, 3.0)
        self.ops.append(dict(eng=eng, veng=veng, fn=fn, deps=sorted(deps), odeps=sorted(odeps - deps),
                             stream=STREAM_OF[eng], seg=self.seg, dur=dur))
        for k in reads:
            self.readers.setdefault(k, []).append(idx)
        for k in writes:
            self.last_w[k] = idx
            self.readers[k] = []
        return idx

    @staticmethod
    def _n(ap):
        n = 1
        for d in ap.shape[1:]:
            n *= d
        return n

    def _dur(self, eng, out):
        n = self._n(out)
        if eng == "pe":
            return PEF * (0.06 + 0.0004 * n)
        if eng == "act":
            return ACTF * (0.22 + 0.00075 * n)
        if eng == "dve":
            return DVEF * (0.2 + 0.00105 * n)
        return POOLF * (0.3 + 0.0021 * n)

    def tt(self, eng, out, in0, in1, op, r, w):
        d = self._dur(eng, out) if op != ALU.pow else 2.3
        return self.op(eng, lambda e: e.tensor_tensor(out=out, in0=in0, in1=in1, op=op), r, w, dur=d)

    def ts(self, eng, out, in0, s1, s2, op0, op1, r, w):
        if op1 is None:
            return self.op(eng, lambda e: e.tensor_scalar(out=out, in0=in0, scalar1=s1, scalar2=None, op0=op0), r, w,
                           dur=self._dur(eng, out))
        return self.op(eng, lambda e: e.tensor_scalar(out=out, in0=in0, scalar1=s1, scalar2=s2, op0=op0, op1=op1), r, w,
                       dur=self._dur(eng, out))

    def stt(self, out, in0, scalar, in1, op0, op1, r, w):
        return self.op("dve", lambda e: e.scalar_tensor_tensor(out=out, in0=in0, scalar=scalar, in1=in1, op0=op0, op1=op1), r, w,
                       dur=self._dur("dve", out))

    def act(self, out, in_, func, r, w, bias=None, scale=None, accum_out=None):
        kw = {}
        if bias is not None:
            kw["bias"] = bias
        if scale is not None:
            kw["scale"] = scale
        if accum_out is not None:
            kw["accum_out"] = accum_out
        return self.op("act", lambda e: e.activation(out=out, in_=in_, func=func, **kw), r, w,
                       dur=self._dur("act", out) + (0.1 if accum_out is not None else 0.0))

    def copy(self, eng, out, in_, r, w):
        if eng == "act":
            return self.op("act", lambda e: e.activation(out=out, in_=in_, func=AF.Copy), r, w, dur=self._dur("act", out))
        return self.op(eng, lambda e: e.tensor_copy(out=out, in_=in_), r, w, dur=self._dur(eng, out))

    def mm(self, out, lhsT, rhs, start, stop, r, w):
        return self.op("pe", lambda e: e.matmul(out, lhsT=lhsT, rhs=rhs, start=start, stop=stop), r, w,
                       dur=self._dur("pe", out))

    def tr(self, out, in_, ident, r, w):
        return self.op("pe", lambda e: e.transpose(out=out, in_=in_, identity=ident), r, w, dur=0.1)

    def dma(self, q, out, in_, r, w, **kw):
        return self.op(q, lambda e: e.dma_start(out=out, in_=in_, **kw), r, w)

    def memset(self, eng, ap, val, w):
        return self.op(eng, lambda e: e.memset(ap, val), (), w, dur=self._dur(eng, ap))

    def schedule(self):
        ops = self.ops
        n = len(ops)
        succ = [[] for _ in range(n)]
        npred = [0] * n
        for i, o in enumerate(ops):
            preds = set(o["deps"]) | set(o["odeps"])
            o["preds"] = preds
            for d in preds:
                succ[d].append(i)
            npred[i] = len(preds)
        prio = [0.0] * n
        for i in range(n - 1, -1, -1):
            m = 0.0
            for q in succ[i]:
                if prio[q] > m:
                    m = prio[q]
            prio[i] = ops[i]["dur"] + m
        finish = [0.0] * n
        order = []
        nseg = self.seg + 1
        by_seg = [[] for _ in range(nseg)]
        for i, o in enumerate(ops):
            by_seg[o["seg"]].append(i)
        tbase = 0.0
        chan_free = [0.0]
        LAT = LATF
        for sg in range(nseg):
            stream_free = {st: tbase for st in STREAMS}
            ready = {st: [] for st in STREAMS}
            rtime = {}
            left = len(by_seg[sg])
            in_seg = set(by_seg[sg])
            for i in by_seg[sg]:
                cnt = 0
                for d in ops[i]["preds"]:
                    if d in in_seg:
                        cnt += 1
                npred[i] = cnt
                if cnt == 0:
                    rtime[i] = tbase
                    ready[ops[i]["stream"]].append(i)
            while left:
                best = None
                for st in STREAMS:
                    rl = ready[st]
                    if not rl:
                        continue
                    sf = stream_free[st]
                    bi = None
                    bk = None
                    for i in rl:
                        t0 = rtime[i]
                        if t0 < sf:
                            t0 = sf
                        k = (int(t0 / QUANT), -prio[i])
                        if bk is None or k < bk:
                            bk = k
                            bi = i
                    if best is None or bk < best[0]:
                        best = (bk, bi, st)
                _, i, st = best
                t0 = max(rtime[i], stream_free[st])
                ready[st].remove(i)
                o = ops[i]
                if o["eng"] in QUEUES:
                    stream_free[st] = t0 + 0.08
                    if FREE_SP:
                        tx = max(t0 + 0.3, chan_free[0])
                        finish[i] = tx + o["dur"] * DMAX
                        chan_free[0] = finish[i]
                    else:
                        finish[i] = t0 + o["dur"]
                else:
                    stream_free[st] = t0 + o["dur"]
                    finish[i] = t0 + o["dur"]
                order.append(i)
                left -= 1
                for q in succ[i]:
                    if q not in in_seg:
                        continue
                    npred[q] -= 1
                    if npred[q] == 0:
                        rt = tbase
                        oq = ops[q]
                        for d in oq["preds"]:
                            if d in in_seg:
                                lat = 0.0 if ops[d]["stream"] == oq["stream"] and d in oq["odeps"] else LAT
                                if ops[d]["stream"] == oq["stream"] and d not in oq["odeps"]:
                                    lat = 0.2
                                f = finish[d] + lat
                                if f > rt:
                                    rt = f
                        rtime[q] = rt
                        ready[oq["stream"]].append(q)
            tbase = max([tbase] + [finish[i] for i in by_seg[sg]]) + 2.0
        self.sim_time = tbase
        return order

    def emit(self, final_wait_ops=()):
        nc = self.nc
        ops = self.ops
        if SCHED:
            order = self.schedule()
        else:
            order = list(range(len(ops)))
        cnt = {}
        for i in order:
            o = ops[i]
            v = o["veng"]
            cnt[v] = cnt.get(v, 0) + 1
            o["cnt"] = cnt[v]
        vengs = sorted(cnt.keys())
        sems = {}
        for v in vengs:
            sems[v] = self.es.enter_context(nc.semaphore("s_%s_%d" % v))
        mult = {v: (16 if v[0] in QUEUES else 1) for v in vengs}
        seen = {s: {} for s in STREAMS}
        know = [None] * len(ops)
        plan = {s: [] for s in STREAMS}
        nwaits = 0
        cur_seg = 0
        last_of_veng = {}
        seg_barrier = {}
        for i in order:
            o = ops[i]
            s = o["stream"]
            sd = seen[s]
            deps = list(o["deps"])
            if SCHED:
                if o["seg"] != cur_seg:
                    cur_seg = o["seg"]
                    seg_barrier = dict(last_of_veng)
                if o["seg"] > 0:
                    deps += [d for d in seg_barrier.values()]
            waits = {}
            for d in sorted(set(deps), reverse=True):
                od = ops[d]
                v = od["veng"]
                if sd.get(v, 0) >= od["cnt"]:
                    continue
                waits[v] = max(waits.get(v, 0), od["cnt"])
                for kv, kc in know[d].items():
                    if sd.get(kv, 0) < kc:
                        sd[kv] = kc
                if sd.get(v, 0) < od["cnt"]:
                    sd[v] = od["cnt"]
            o["waits"] = waits
            nwaits += len(waits)
            know[i] = dict(sd)
            plan[s].append(i)
            last_of_veng[o["veng"]] = i
        self.nwaits = nwaits
        blk = self.es.enter_context(nc.Block())

        def make(sname):
            def body(eng):
                for i in plan[sname]:
                    o = ops[i]
                    for v, c in o["waits"].items():
                        eng.wait_ge(sems[v], c * mult[v])
                    ins = o["fn"](eng)
                    ins.then_inc(sems[o["veng"]], mult[o["veng"]])
                if sname == "sync":
                    for i in final_wait_ops:
                        o = ops[i]
                        eng.wait_ge(sems[o["veng"]], o["cnt"] * mult[o["veng"]])
            return body

        for sname in STREAMS:
            if not plan[sname] and sname != "sync":
                continue
            getattr(blk, sname)(make(sname))


class Arena:
    def __init__(self, P, name, nbytes):
        self.cap = nbytes
        self.t = P.sb(name, [128, nbytes // 4], F32)
        self.off = 0

    def reset(self):
        self.off = 0

    def alloc(self, shape, dt=F32, parts=128):
        n = 1
        for s in shape:
            n *= s
        nb = n * (2 if dt == BF16 else 4)
        nb32 = (nb + 31) // 32 * 32
        a = self.off
        self.off += nb32
        self.peak = max(getattr(self, "peak", 0), self.off)
        if self.off > self.cap:
            print("ARENA OVERFLOW", self.off, self.cap)
        v = self.t[0:parts, a // 4:(a + nb32) // 4]
        if dt == BF16:
            v = v.bitcast(BF16)
        v = v[:, 0:n]
        if len(shape) == 2:
            v = v.rearrange("p (a b) -> p a b", a=shape[0], b=shape[1])
        elif len(shape) == 3:
            v = v.rearrange("p (a b c) -> p a b c", a=shape[0], b=shape[1], c=shape[2])
        return v


def rsqrt_op(P, out, in_, mh, rk, wk, tmp=None):
    if 'nopow' in TOG:
        P.act(out, in_, AF.Sqrt, rk, wk)
        P.op("dve", lambda e: e.reciprocal(out=out, in_=out), wk, wk)
    else:
        P.tt("pool", out, in_, mh, ALU.pow, list(rk) + ["mhalf"], wk)


def bc(ap, shape):
    return ap.broadcast_to(list(shape))


def build(dbg=False, nlayers=2):
    nc = bass.Bass("TRN2", target_bir_lowering=False)
    P = Prog(nc)

    def din(name, shape):
        return nc.dram_tensor(name, list(shape), F32, kind="ExternalInput").ap()

    x_d = din("x", [S, D])
    mem_d = din("mem", [256, D])
    win_d = din("a_w_in", [D, 3712])
    wout_d = din("w_out", [2, D, D])
    wmkv_d = din("w_mem_kv", [2, D, 512])
    wkv_d = din("w_kv", [D, 1536])
    bwin_d = din("b_w_in", [D, 2048])
    gains_d = din("gains", [128, 5, 8])
    mu64_d = din("mu64", [64, 26])
    muv_d = din("muv_b", [128, 768])
    w2a_d = din("w2aug", [65, 768])
    a2a_d = din("a2aug", [65, 768])
    hp_d = din("headp", [64, 3, 12])
    lnx_d = din("lnx_b", [128, 2, 768])
    postg_d = din("postg_b", [128, 2, 1024])
    subln_d = din("subln_b", [128, 128])
    lamv_d = din("lamv_b", [128, 4, 64])
    out_d = nc.dram_tensor("out", [S, D], F32, kind="ExternalOutput").ap()
    if dbg:
        dbg_d = nc.dram_tensor("dbg", [S, D], F32, kind="ExternalOutput").ap()
    class WS:
        def __init__(self, name, groups):
            self.groups = groups
            self.pieces = {}
            idx = 0
            for (ca, cb, w) in groups:
                for c0 in range(ca, cb, w):
                    self.pieces[c0] = (idx, w)
                    idx += 1
            self.t = nc.dram_tensor(name, [idx, 128, 8, 256], BF16).ap()
            self.name = name

        def cast(self, src, order=None):
            order = order if order is not None else list(range(len(self.groups)))
            for gi in order:
                ca, cb, w = self.groups[gi]
                J = (cb - ca) // w
                j0 = self.pieces[ca][0]
                for kc in range(8):
                    dst = self.t[j0:j0 + J, :, kc, 0:w].rearrange("j p c -> p j c")
                    srcv = src[kc * 128:(kc + 1) * 128, ca:cb].rearrange("p (j c) -> p j c", c=w)
                    P.dma("poolq", dst, srcv, (), ["%s_%d_%d" % (self.name, kc, gi)])
            return self.keymap()

        def keymap(self):
            km = {}
            for gi, (ca, cb, w) in enumerate(self.groups):
                for c0 in range(ca, cb, w):
                    km[c0] = ["%s_%d_%d" % (self.name, kc, gi) for kc in range(8)]
            return km

    ws_in = WS("ws_in", [(0, 768, 256), (768, 1536, 256), (1536, 2304, 256), (2304, 2432, 128), (2432, 3712, 256)])
    ws_out = [WS("ws_out%d" % l, [(0, 1024, 256)]) for l in range(2)]
    ws_mkv = [WS("ws_mkv%d" % l, [(0, 512, 256)]) for l in range(2)]
    ws_kv = WS("ws_kv", [(0, 1536, 256)])
    ws_bin = WS("ws_bin", [(0, 2048, 256)])

    psall = P.ps("psall", [128, 8, 512], F32)
    P.psall = psall
    P.banks = [psall[:, i, :] for i in range(8)]

    x_res = P.sb("x_res", [128, NT, D], F32)
    NWST = 4
    Wst = [P.sb("wst%d" % i, [128, 8, 256], BF16) for i in range(NWST)]
    wst_i = [0]
    cst = Arena(P, "cst", 22528)
    identf = cst.alloc([128], F32)
    ident = cst.alloc([128], BF16)
    maskSU = cst.alloc([128], BF16)
    maskU = cst.alloc([128], BF16)
    maskSL = cst.alloc([128], BF16)
    ShA = cst.alloc([128], BF16)
    ShB = cst.alloc([128], BF16)
    mtmp = cst.alloc([128], F32)
    ones64 = cst.alloc([64], F32, parts=64)
    onescol = cst.alloc([2], BF16, parts=64)
    cmask = cst.alloc([512], F32, parts=64)
    epsn = cst.alloc([1], F32)
    epsl = cst.alloc([1], F32)
    mhalf = cst.alloc([12], F32)
    rkc = cst.alloc([12, 2], BF16, parts=64)
    gains = cst.alloc([5, 8], F32)
    mu64 = cst.alloc([26], F32, parts=64)
    muv_b = cst.alloc([768], F32)
    w2aug = cst.alloc([768], BF16, parts=65)
    a2aug = cst.alloc([768], BF16, parts=65)
    headp = cst.alloc([3, 12], F32, parts=64)
    postg_b = cst.alloc([1024], F32)
    subln_b = cst.alloc([128], F32)
    lamc = cst.alloc([8], F32)
    KmT = [cst.alloc([4, 256], BF16, parts=64) for _ in range(2)]
    Vma = [cst.alloc([2, 4, 65], BF16) for _ in range(2)]

    k_in = ws_in.cast(win_d, order=[2, 4, 3, 0, 1])
    k_mkv = [ws_mkv[l].cast(wmkv_d[l]) for l in range(2)]
    k_out = [ws_out[0].cast(wout_d[0]), ws_out[1].keymap()]
    k_kv = ws_kv.keymap()
    k_bin = ws_bin.keymap()

    def late_precasts():
        ws_out[1].cast(wout_d[1])
        ws_kv.cast(wkv_d)
        ws_bin.cast(bwin_d)

    wst_list = [list(Wst)]

    resident = [{}]

    def stage(scr, keys, c0, n):
        if (scr.name, c0) in resident[0]:
            return resident[0][(scr.name, c0)]
        Wst_ = wst_list[0]
        NW_ = len(Wst_)
        return _stage(scr, keys, c0, n, Wst_, NW_)

    def _stage(scr, keys, c0, n, Wst, NWST):
        i = wst_i[0] % NWST
        assert n <= 256
        wst_i[0] += 1
        t = Wst[i]
        pi, w_ = scr.pieces[c0]
        assert n <= w_
        P.dma("sp", t[:, :, 0:n], scr.t[pi][:, :, 0:n], keys[c0], ["wst%d" % i])
        return t, "wst%d" % i

    P.memset("pool", identf, 1.0, ["identf"])
    P.op("pool", lambda e: e.affine_select(out=identf, in_=identf, pattern=[[1, 128]], compare_op=ALU.is_equal,
                                           fill=0.0, base=0, channel_multiplier=-1), ["identf"], ["identf"])
    P.copy("dve", ident, identf, ["identf"], ["ident"])
    for mk, name, pat, cm, cop in ((maskSU, "maskSU", 1, -1, ALU.is_gt), (maskU, "maskU", 1, -1, ALU.is_ge),
                                   (maskSL, "maskSL", -1, 1, ALU.is_gt)):
        P.memset("pool", mtmp, 1.0, ["mtmp"])
        P.op("pool", lambda e, pat=pat, cm=cm, cop=cop: e.affine_select(
            out=mtmp, in_=mtmp, pattern=[[pat, 128]], compare_op=cop, fill=0.0, base=0, channel_multiplier=cm),
            ["mtmp"], ["mtmp"])
        P.copy("pool", mk, mtmp, ["mtmp"], [name])
    for mk, name, pat, base_, cm in ((ShA, "ShA", 1, -1, -1), (ShB, "ShB", 1000, -127, 1)):
        P.memset("pool", mtmp, 1.0, ["mtmp"])
        P.op("pool", lambda e, pat=pat, base_=base_, cm=cm: e.affine_select(
            out=mtmp, in_=mtmp, pattern=[[pat, 128]], compare_op=ALU.is_equal, fill=0.0, base=base_, channel_multiplier=cm),
            ["mtmp"], ["mtmp"])
        P.copy("pool", mk, mtmp, ["mtmp"], [name])
    P.memset("pool", ones64, 1.0, ["ones64"])
    P.memset("pool", onescol, 1.0, ["onescol"])
    P.memset("pool", cmask, 1.0, ["cmask"])
    P.memset("pool", cmask.rearrange("p (a b) -> p a b", a=4)[:, :, 0:1], 0.0, ["cmask"])
    P.memset("pool", epsn, NORM_EPS, ["epsn"])
    P.memset("pool", epsl, LNX_EPS, ["epsl"])
    P.memset("pool", mhalf, -0.5, ["mhalf"])
    P.dma("sp", gains, gains_d, (), ["gains"])
    P.dma("sp", mu64, mu64_d, (), ["mu64"])
    P.dma("sp", muv_b, muv_d, (), ["muv_b"])
    P.dma("poolq", w2aug, w2a_d, (), ["w2aug"])
    P.dma("poolq", a2aug, a2a_d, (), ["a2aug"])
    P.dma("sp", headp, hp_d, (), ["headp"])
    P.dma("sp", subln_b, subln_d, (), ["subln_b"])
    def load_x(i):
        P.dma("sp", x_res[:, i, :], x_d[i * 128:(i + 1) * 128, :], (), ["x%d" % i])
    for i in range(2):
        load_x(i)

    arena = Arena(P, "arena", 108352)

    def rmsnorm_T(src, src_keys, outs, A):
        xb, junk, ss, rt, rstd = A["xb"], A["junk"], A["ss"], A["rt"], A["rstd"]
        P.act(xb, src, AF.Square, src_keys, ["xb", "ss"], accum_out=ss)
        P.ts("dve", rt, ss, 1.0 / D, NORM_EPS, ALU.mult, ALU.add, ["ss"], ["rt"])
        rsqrt_op(P, rstd, rt, mhalf[:, 0:1], ["rt"], ["rstd"])
        P.act(xb, src, AF.Copy, list(src_keys) + ["rstd"], ["xb"], scale=rstd)
        bk, bkey = P.bank(nr=len(outs))
        bv = bk[:, 0:512].bitcast(BF16).rearrange("p (a b) -> p a b", a=8)
        for kc in range(8):
            P.tr(bv[:, kc, :], xb[:, kc * 128:(kc + 1) * 128], ident, ["xb", "ident"], [bkey])
        for gi, oap, okey in outs:
            P.tt("dve", oap, bv, bc(gains[:, gi, :].unsqueeze(2), [128, 8, 128]), ALU.mult,
                 [bkey, "gains"], [okey])

    def mem_kv(l, A):
        memt, memT = A["memt"], A["memT"]
        wt, wkey = stage(ws_mkv[l], k_mkv[l], 0, 256)
        wt2, wkey2 = stage(ws_mkv[l], k_mkv[l], 256, 256)
        for mt in range(2):
            P.dma("sp", memt, mem_d[mt * 128:(mt + 1) * 128, :], (), ["memt"])
            rmsnorm_T(memt, ["memt"], [(2 + l, memT[:, :, mt * 128:(mt + 1) * 128], "memT")], A)
        for hp in range(2):
            bk, bkey = P.bank()
            for hh in range(2):
                h = hp * 2 + hh
                for kc in range(8):
                    P.mm(bk[0:64, hh * 256:(hh + 1) * 256], wt[:, kc, h * 64:(h + 1) * 64], memT[:, kc, :],
                         kc == 0, kc == 7, [wkey, "memT"], [bkey])
            P.copy("act", KmT[l][:, hp * 2:(hp + 1) * 2, :], bk[0:64, :].rearrange("p (a b) -> p a b", a=2),
                   [bkey], ["KmT%d" % l])
        P.memset("pool", Vma[l], 2.0, ["Vma%d" % l])
        for mt in range(2):
            bk, bkey = P.bank()
            for kc in range(8):
                P.mm(bk[:, 0:256], memT[:, kc, mt * 128:(mt + 1) * 128], wt2[:, kc, 0:256], kc == 0, kc == 7,
                     [wkey2, "memT"], [bkey])
            P.copy("act", Vma[l][:, mt, :, 0:64], bk[:, 0:256].rearrange("p (a b) -> p a b", a=4),
                   [bkey, "Vma%d" % l], ["Vma%d" % l])

    def mem_attn(l, A, qmT, sg, YG, qk="qmT", sk="sg", yk="xb"):
        PmT, ym, rz = A["PmT"], A["ym"], A["rz"]
        for mt in range(2):
            bk, bkey = P.bank()
            for h in range(4):
                P.mm(bk[:, h * 128:(h + 1) * 128], KmT[l][:, h, mt * 128:(mt + 1) * 128], qmT[:, h, :], True, True,
                     ["KmT%d" % l, qk], [bkey])
            P.act(PmT[mt], bk[:, :].rearrange("p (a b) -> p a b", a=4), AF.Exp, [bkey], ["PmT%d" % mt], scale=0.125)
        bk, bkey = P.bank(nr=2)
        ov = bk[:, 0:260].rearrange("p (a b) -> p a b", a=4)
        for h in range(4):
            for mt in range(2):
                P.mm(ov[:, h, :], PmT[mt][:, h, :], Vma[l][:, mt, h, :], mt == 0, mt == 1,
                     ["PmT%d" % mt, "Vma%d" % l], [bkey])
        P.op("dve", lambda e: e.reciprocal(out=rz, in_=ov[:, :, 64:65]), [bkey], ["rz"])
        P.tt("dve", ym, ov[:, :, 0:64], bc(rz, [128, 4, 64]), ALU.mult, [bkey, "rz"], ["ym"])
        P.tt("dve", YG[:, 768:1024], ym.rearrange("p a b -> p (a b)"), sg[:, 768:1024], ALU.mult, ["ym", sk], [yk])

    def out_proj(l, i, A, YG, final, yk="xb"):
        YGT, otmp, ssa, rt, rstd = A["YGT"], A["otmp"], A["ssa"], A["rt2"], A["rstd2"]
        junk = A["junk"]
        bk, bkey = P.bank()
        bv = bk[:, 0:512].bitcast(BF16).rearrange("p (a b) -> p a b", a=8)
        for kc in range(8):
            P.tr(bv[:, kc, :], YG[:, kc * 128:(kc + 1) * 128], ident, [yk, "ident"], [bkey])
        P.copy("act", YGT, bv, [bkey], ["YGT"])
        obk = []
        for cg in range(2):
            bk, bkey = P.bank(nr=2)
            for hf in range(2):
                wt, wkey = stage(ws_out[l], k_out[l], cg * 512 + hf * 256, 256)
                for kc in range(8):
                    P.mm(bk[:, hf * 256:(hf + 1) * 256], YGT[:, kc, :], wt[:, kc, :], kc == 0, kc == 7, ["YGT", wkey], [bkey])
            P.act(junk[:, 0:512], bk[:, :], AF.Square, [bkey], ["otmp0", "ssa%d" % cg], accum_out=ssa[:, cg:cg + 1])
            obk.append((bk, bkey))
        P.tt("dve", ssa[:, 2:3], ssa[:, 0:1], ssa[:, 1:2], ALU.add, ["ssa0", "ssa1"], ["ssa2"])
        P.ts("dve", rt, ssa[:, 2:3], 1.0 / D, NORM_EPS, ALU.mult, ALU.add, ["ssa2"], ["rt2"])
        rsqrt_op(P, rstd, rt, mhalf[:, 0:1], ["rt2"], ["rstd2"])
        xk = "x%d" % i
        for cg in range(2):
            bk, bkey = obk[cg]
            P.stt(otmp[cg], bk[:, :], rstd, postg_b[:, cg * 512:(cg + 1) * 512], ALU.mult, ALU.mult,
                  [bkey, "rstd2", "postg_b"], ["otmp%d" % cg])
            P.tt("pool", x_res[:, i, cg * 512:(cg + 1) * 512], x_res[:, i, cg * 512:(cg + 1) * 512], otmp[cg],
                 ALU.add, [xk, "otmp%d" % cg], [xk])
        if final:
            return P.dma("sp", out_d[i * 128:(i + 1) * 128, :], x_res[:, i, :], [xk], [])
        return None

    def tm_job(scr, keys, c0, n, lhs_list, nrs):
        wt, wkey = stage(scr, keys, c0, n)
        res = []
        for li, (hv, hkey) in enumerate(lhs_list):
            bk, bkey = P.bank(nr=(nrs[li] if nrs else 1))
            for kc in range(8):
                P.mm(bk[:, 0:n], hv[:, kc, :], wt[:, kc, 0:n], kc == 0, kc == 7, [hkey, wkey], [bkey])
            res.append((bk, bkey))
        return res

    def fm64_job(scr, keys, c0, hv, hkey):
        wt, wkey = stage(scr, keys, c0, 256)
        bk, bkey = P.bank()
        for g in range(4):
            for kc in range(8):
                P.mm(bk[0:64, g * 128:(g + 1) * 128], wt[:, kc, g * 64:(g + 1) * 64], hv[:, kc, :], kc == 0, kc == 7,
                     [hkey, wkey], [bkey])
        return bk, bkey

    A = {}
    A["xb"] = arena.alloc([1024], BF16)
    for nm in ("ss", "rt", "rstd", "rt2", "rstd2"):
        A[nm] = arena.alloc([1], F32)
    A["ssa"] = arena.alloc([4], F32)
    A["PmT"] = [arena.alloc([4, 128], BF16) for _ in range(2)]
    A["ym"] = arena.alloc([4, 64], F32)
    A["rz"] = arena.alloc([4, 1], F32)
    A["YGT"] = arena.alloc([8, 128], BF16)
    A["otmp"] = [arena.alloc([512], F32) for _ in range(2)]
    A["junk"] = A["otmp"][0].bitcast(BF16)
    common_off = arena.off
    A["memt"] = arena.alloc([1024], F32)
    A["memT"] = arena.alloc([8, 256], BF16)
    for l in range(2):
        mem_kv(l, A)
    P.barrier()
    arena.off = common_off

    lnx_b = arena.alloc([2, 768], F32)
    P.dma("sp", lnx_b, lnx_d, (), ["lnx_b"])
    hT = arena.alloc([8, 129], BF16)
    carry = arena.alloc([26], F32, parts=64)
    Zg = arena.alloc([8, 129], F32, parts=64)
    Zd = arena.alloc([8, 128], F32, parts=64)
    Zw = arena.alloc([2, 129], F32, parts=64)
    Zwd = arena.alloc([2, 128], F32, parts=64)
    twd = arena.alloc([128], BF16, parts=65)
    adb = arena.alloc([128], BF16, parts=65)
    sgw, av, Lp, kk, t2, t3 = [arena.alloc([4, 128], F32, parts=64) for _ in range(6)]
    eP, ePi, ePm = [arena.alloc([4, 128], BF16, parts=64) for _ in range(3)]
    DS = []
    for sl in range(3):
        d = {}
        for nm in ("aT", "rT", "bT", "kT", "rkT"):
            d[nm] = arena.alloc([4, 128], BF16, parts=64)
        d["PCt"] = arena.alloc([4, 1], F32, parts=64)
        DS.append(d)
    IO = []
    for sl in range(2):
        d = {}
        for nm in ("Tm0", "AakT", "ArbT", "ArkT"):
            d[nm] = arena.alloc([4, 128], BF16)
        d["BK"] = arena.alloc([8, 64], BF16)
        IO.append(d)
    PQs = [arena.alloc([2, 4, 128], BF16) for _ in range(2)]
    Tm1s = arena.alloc([4, 128], BF16)
    R1s = arena.alloc([4, 64], BF16)
    Ub = arena.alloc([4, 64], BF16)
    bs2 = [arena.alloc([12, 1], F32) for _ in range(2)]
    vps = arena.alloc([768], BF16)
    vraw2 = [arena.alloc([768], BF16) for _ in range(2)]
    Vt2 = [arena.alloc([768], BF16) for _ in range(2)]
    sg2 = [arena.alloc([1024], BF16) for _ in range(2)]
    qmT2 = [arena.alloc([4, 128], BF16, parts=64) for _ in range(2)]
    Mf = arena.alloc([12, 64], F32, parts=64)
    Mb = arena.alloc([12, 64], BF16, parts=64)
    Y2 = [arena.alloc([12, 64], F32) for _ in range(2)]
    Ysq = arena.alloc([12, 64], F32)
    st = arena.alloc([8, 12], F32)
    YG0 = arena.alloc([1024], BF16)

    if nlayers >= 1:
        P.dma("sp", postg_b, postg_d[:, 0, :], (), ["postg_b"])
        P.memset("pool", carry, 0.0, ["carry"])
        P.memset("pool", Mf, 0.0, ["Mf"])
        P.memset("pool", Mb, 0.0, ["Mb"])
        P.memset("pool", hT, 0.0, ["hT"])
        P.memset("pool", vraw2[1], 0.0, ["vraw1"])
        P.memset("pool", twd, 1.0, ["twd"])
        P.memset("pool", adb, 1.0, ["adb"])
    kk64 = headp[:, 0, :]
    ka64 = headp[:, 1, :]
    rk64 = headp[:, 2, :]
    if nlayers >= 1:
        P.ts("dve", lnx_b, lnx_b, 0.5, None, ALU.mult, None, ["lnx_b"], ["lnx_b"])
        P.ts("dve", rkc, bc(rk64.unsqueeze(2), [64, 12, 2]), 0.5, None, ALU.mult, None, ["headp"], ["rkc"])
    flat = lambda t: t.rearrange("p a b -> p (a b)")
    v4 = lambda t, a=4: t.rearrange("p (a b) -> p a b", a=a)

    def pre(i):
        xk = "x%d" % i
        tp = i % 2
        Vt, sg, qmT = Vt2[tp], sg2[tp], qmT2[tp]
        vk, sk, qk = "Vt%d" % tp, "sg%d" % tp, "qmT%d" % tp
        P.copy("pool", hT[:, :, 0:1], hT[:, :, 128:129], ["hT"], ["hT"])
        yield
        rmsnorm_T(x_res[:, i, :], [xk], [(0, hT[:, :, 1:129], "hT")], A)
        yield
        hcur = (hT[:, :, 1:129], "hT")
        hprev = (hT[:, :, 0:128], "hT")
        vraw, vrk = vraw2[tp], "vraw%d" % tp
        vprv, vpk = vraw2[1 - tp], "vraw%d" % (1 - tp)
        for q in range(3):
            (v1, v1k), = tm_job(ws_in, k_in, 1536 + q * 256, 256, [hcur], [2])
            cs = slice(q * 256, (q + 1) * 256)
            P.copy("act", vraw[:, cs], v1[:, 0:256], [v1k], [vrk])
            yield
            b2_, b2k_ = P.bank()
            P.mm(b2_[:, 0:256], ShA, vraw[:, cs], True, False, ["ShA", vrk], [b2k_])
            P.mm(b2_[:, 0:256], ShB, vprv[:, cs], False, True, ["ShB", vpk], [b2k_])
            P.tt("dve", vps[:, cs], b2_[:, 0:256], vraw[:, cs], ALU.subtract, [b2k_, vrk], ["vps%d" % q])
            yield
            P.tt("pool", vps[:, cs], vps[:, cs], muv_b[:, cs], ALU.mult, ["vps%d" % q, "muv_b"], ["vps%d" % q])
            P.tt("dve", Vt[:, cs], v1[:, 0:256], vps[:, cs], ALU.add, ["vps%d" % q, v1k], [vk])
            yield
        for q in range(4):
            c0 = 2432 + q * 256 if q < 3 else 3456
            (g1, g1k), = tm_job(ws_in, k_in, c0, 256, [hcur], [2])
            P.act(sg[:, q * 256:(q + 1) * 256], g1[:, 0:256], AF.Tanh, [g1k], [sk], scale=0.5)
            P.stt(sg[:, q * 256:(q + 1) * 256], sg[:, q * 256:(q + 1) * 256], 1.0, g1[:, 0:256], ALU.add, ALU.mult,
                  [sk, g1k], [sk])
            yield
        qb, qbk = fm64_job(ws_in, k_in, 3200, hT[:, :, 1:129], "hT")
        P.copy("act", qmT, v4(qb[0:64, :]), [qbk], [qk])
        yield
        wt, wkey = stage(ws_in, k_in, 2304, 128)
        bk, bkey = P.bank()
        for g in range(2):
            for kc in range(8):
                P.mm(bk[0:64, g * 128:(g + 1) * 128], wt[:, kc, g * 64:(g + 1) * 64], hT[:, kc, 1:129], kc == 0, kc == 7,
                     ["hT", wkey], [bkey])
        P.copy("pool", Zw[:, :, 0:1], carry[:, 24:26].unsqueeze(2), ["carry"], ["Zw"])
        P.copy("act", Zw[:, :, 1:129], v4(bk[0:64, 0:256], 2), [bkey], ["Zw"])
        yield
        P.tt("pool", Zwd, Zw[:, :, 0:128], Zw[:, :, 1:129], ALU.subtract, ["Zw"], ["Zwd"])
        P.tt("pool", Zwd, Zwd, bc(mu64[:, 24:26].unsqueeze(2), [64, 2, 128]), ALU.mult, ["Zwd", "mu64"], ["Zwd"])
        P.tt("dve", Zwd, Zw[:, :, 1:129], Zwd, ALU.add, ["Zw", "Zwd"], ["Zwd"])
        P.copy("pool", carry[:, 24:26].unsqueeze(2), Zw[:, :, 128:129], ["Zw"], ["carry"])
        yield
        P.act(twd[0:64, :], Zwd[:, 0, :], AF.Tanh, ["Zwd"], ["twd"])
        P.copy("act", adb[0:64, :], Zwd[:, 1, :], ["Zwd"], ["adb"])
        yield

    def dphase(i, g, sl):
        S_ = DS[sl]
        sk_ = lambda nm: "%s_%d" % (nm, sl)
        aT, rT, bT, kT, rkT, PCt = (S_[n] for n in ("aT", "rT", "bT", "kT", "rkT", "PCt"))
        rb, rbk = fm64_job(ws_in, k_in, g * 256, hT[:, :, 1:129], "hT")
        yield
        kb, kbk = fm64_job(ws_in, k_in, 768 + g * 256, hT[:, :, 1:129], "hT")
        yield
        P.copy("pool", Zg[:, :, 0:1], carry[:, g * 8:(g + 1) * 8].unsqueeze(2), ["carry"], ["Zg"])
        P.copy("act", Zg[:, 0:4, 1:129], v4(rb[0:64, :]), [rbk], ["Zg"])
        P.copy("act", Zg[:, 4:8, 1:129], v4(kb[0:64, :]), [kbk], ["Zg"])
        yield
        P.tt("pool", Zd, Zg[:, :, 0:128], Zg[:, :, 1:129], ALU.subtract, ["Zg"], ["Zd"])
        yield
        P.tt("pool", Zd, Zd, bc(mu64[:, g * 8:(g + 1) * 8].unsqueeze(2), [64, 8, 128]), ALU.mult, ["Zd", "mu64"], ["Zd"])
        yield
        P.tt("dve", Zd, Zg[:, :, 1:129], Zd, ALU.add, ["Zg", "Zd"], ["Zd"])
        P.copy("pool", carry[:, g * 8:(g + 1) * 8].unsqueeze(2), Zg[:, :, 128:129], ["Zg"], ["carry"])
        yield
        r_ = Zd[:, 0:4, :]
        k_ = Zd[:, 4:8, :]
        bw, bwk = P.bank()
        ba, bak = P.bank()
        for j in range(4):
            h = g * 4 + j
            P.mm(bw[0:64, j * 128:(j + 1) * 128], w2aug[:, h * 64:(h + 1) * 64], twd[:, :], True, True,
                 ["w2aug", "twd"], [bwk])
        for j in range(4):
            h = g * 4 + j
            P.mm(ba[0:64, j * 128:(j + 1) * 128], a2aug[:, h * 64:(h + 1) * 64], adb[:, :], True, True,
                 ["a2aug", "adb"], [bak])
        yield
        P.act(sgw, v4(bw[0:64, :]), AF.Tanh, [bwk], ["sgw"], scale=0.5)
        P.act(av, v4(ba[0:64, :]), AF.Tanh, [bak], ["av"], scale=0.5)
        P.act(sgw, sgw, AF.Copy, ["sgw"], ["sgw"], bias=0.5, scale=0.5)
        P.act(av, av, AF.Copy, ["av"], ["av"], bias=0.5, scale=0.5)
        yield
        P.op("dve", lambda e: e.tensor_tensor_scan(out=flat(Lp), data0=cmask, data1=flat(sgw), initial=0.0,
                                                   op0=ALU.mult, op1=ALU.add), ["sgw", "cmask"], ["Lp"])
        P.tt("dve", kk, k_, bc(kk64[:, g * 4:(g + 1) * 4].unsqueeze(2), [64, 4, 128]), ALU.mult, ["Zd", "headp"], ["kk"])
        yield
        P.tt("pool", t2, kk, kk, ALU.mult, ["kk"], ["t2"])
        P.act(eP, Lp, AF.Exp, ["Lp"], ["eP"], scale=-CDEC)
        yield
        P.act(ePi, Lp, AF.Exp, ["Lp"], ["ePi"], scale=CDEC)
        P.tt("pool", sgw, Lp, sgw, ALU.subtract, ["Lp", "sgw"], ["sgw"])
        yield
        P.act(PCt, Lp[:, :, 127:128], AF.Exp, ["Lp"], [sk_("PCt")], scale=-CDEC)
        yield
        bsq, bsqk = P.bank()
        P.mm(bsq[0:64, :], ones64, flat(t2), True, True, ["ones64", "t2"], [bsqk])
        P.act(ePm, sgw, AF.Exp, ["sgw"], ["ePm"], scale=-CDEC)
        yield
        P.ts("dve", flat(t2), bsq[0:64, :], 1e-12, None, ALU.max, None, [bsqk], ["t2"])
        yield
        P.act(t2, t2, AF.Ln, ["t2"], ["t2"])
        P.stt(t3, av, -1.0, bc(ka64[:, g * 4:(g + 1) * 4].unsqueeze(2), [64, 4, 128]), ALU.add, ALU.mult,
              ["av", "headp"], ["t3"])
        yield
        P.act(t2, t2, AF.Exp, ["t2"], ["t2"], scale=-0.5)
        P.stt(t3, t3, 1.0, k_, ALU.add, ALU.mult, ["t3", "Zd"], ["t3"])
        yield
        P.tt("dve", rT, r_, eP, ALU.mult, ["Zd", "eP"], [sk_("rT")])
        P.tt("pool", kT, t3, ePi, ALU.mult, ["t3", "ePi"], [sk_("kT")])
        yield
        P.tt("dve", kk, kk, t2, ALU.mult, ["kk", "t2"], ["kk"])
        yield
        P.stt(aT, kk, -1.0, ePm, ALU.mult, ALU.mult, ["kk", "ePm"], [sk_("aT")])
        P.tt("pool", av, kk, av, ALU.mult, ["kk", "av"], ["av"])
        yield
        P.tt("dve", bT, av, ePi, ALU.mult, ["av", "ePi"], [sk_("bT")])
        P.tt("pool", rkT, r_, t3, ALU.mult, ["Zd", "t3"], [sk_("rkT")])
        yield

    def istage(i, g, n):
        dsl, isl = n % 3, n % 2
        S_ = DS[dsl]
        O_ = IO[isl]
        dk_ = lambda nm: "%s_%d" % (nm, dsl)
        ik_ = lambda nm: "%s_i%d" % (nm, isl)

        def sk_(nm):
            if nm in ("aT", "rT", "bT", "kT", "rkT", "PCt"):
                return dk_(nm)
            if nm in ("Tm0", "AakT", "ArbT", "ArkT", "BK"):
                return ik_(nm)
            return nm + "_s"
        aT, rT, bT, kT, rkT, PCt = (S_[q] for q in ("aT", "rT", "bT", "kT", "rkT", "PCt"))
        BK, AakT, ArbT, ArkT = (O_[q] for q in ("BK", "AakT", "ArbT", "ArkT"))
        Qm = [PQs[q][:, 0, :, :] for q in range(2)]
        Pm = [PQs[q][:, 1, :, :] for q in range(2)]
        Tm = [O_["Tm0"], Tm1s]
        tp = i % 2

        def amat(lh, lkey, rh, rkey, mask, mkey, eng, oap, okey):
            bk, bkey = P.bank()
            for j in range(4):
                P.mm(bk[:, j * 128:(j + 1) * 128], lh[:, j, :], rh[:, j, :], True, True, [lkey, rkey], [bkey])
            P.tt(eng, oap, v4(bk[:, :]), bc(mask.unsqueeze(1), [128, 4, 128]), ALU.mult, [bkey, mkey], [okey])

        amat(bT, sk_("bT"), aT, sk_("aT"), maskSU, "maskSU", "dve", Pm[0], sk_("Pm0"))
        yield
        amat(aT, sk_("aT"), bT, sk_("bT"), maskSL, "maskSL", "dve", Qm[0], sk_("Qm0"))
        P.tt("pool", Tm[0], Pm[0], bc(ident.unsqueeze(1), [128, 4, 128]), ALU.add, [sk_("Pm0"), "ident"], [sk_("Tm0")])
        yield
        cur = 0
        extra = [
            lambda: amat(kT, sk_("kT"), aT, sk_("aT"), maskSU, "maskSU", "dve", AakT, sk_("AakT")),
            lambda: amat(bT, sk_("bT"), rT, sk_("rT"), maskU, "maskU", "dve", ArbT, sk_("ArbT")),
            lambda: amat(kT, sk_("kT"), rT, sk_("rT"), maskU, "maskU", "dve", ArkT, sk_("ArkT")),
        ]

        def ephase_a():
            bk, bkey = P.bank()
            bv = bk[:, 0:256].bitcast(BF16).rearrange("p (a b) -> p a b", a=8)
            for j in range(4):
                P.tr(bv[:, j, :], bT[:, j, :], ident[0:64, 0:64], [sk_("bT"), "ident"], [bkey])
            for j in range(4):
                P.tr(bv[:, 4 + j, :], kT[:, j, :], ident[0:64, 0:64], [sk_("kT"), "ident"], [bkey])
            P.copy("act", BK, bv, [bkey], [sk_("BK")])

        def ephase_b():
            bk, bkey = P.bank()
            for j in range(4):
                P.mm(bk[:, 2 * j:2 * j + 2], rkT[:, j, :], onescol if 'norkc' in TOG else rkc[:, g * 4 + j, :], True, True, [sk_("rkT"), "rkc", "onescol"], [bkey])
            P.copy("dve", bs2[tp][:, g * 4:(g + 1) * 4, :], v4(bk[:, 0:8])[:, :, 0:1], [bkey], ["bs%d" % tp])

        extra += [ephase_a, ephase_b]
        for kr in range(1, 7):
            nxt = 1 - cur
            bA, bAk = P.bank()
            for j in range(4):
                P.mm(bA[:, j * 128:(j + 1) * 128], Pm[cur][:, j, :], Qm[cur][:, j, :], True, True,
                     [sk_("Pm%d" % cur), sk_("Qm%d" % cur)], [bAk])
            if kr <= 5:
                bB, bBk = P.bank()
                for j in range(4):
                    P.mm(bB[:, j * 128:(j + 1) * 128], Qm[cur][:, j, :], Pm[cur][:, j, :], True, True,
                         [sk_("Pm%d" % cur), sk_("Qm%d" % cur)], [bBk])
            if extra:
                extra.pop(0)()
            P.copy("act", Qm[nxt], v4(bA[:, :]), [bAk], [sk_("Qm%d" % nxt)])
            if kr <= 5:
                P.copy("act", Pm[nxt], v4(bB[:, :]), [bBk], [sk_("Pm%d" % nxt)])
            yield
            btk, btkk = P.bank()
            for j in range(4):
                P.mm(btk[:, j * 128:(j + 1) * 128], Qm[nxt][:, j, :], Tm[cur][:, j, :], True, True,
                     [sk_("Qm%d" % nxt), (sk_("Tm0") if cur == 0 else "Tm1_s")], [btkk])
            yield
            P.tt("dve", Tm[nxt], v4(btk[:, :]), Tm[cur], ALU.add, [btkk, (sk_("Tm0") if cur == 0 else "Tm1_s")], [(sk_("Tm0") if nxt == 0 else "Tm1_s")])
            yield
            cur = nxt
        while extra:
            extra.pop(0)()
            yield
        assert cur == 0
    def cstage(i, g, n):
        dsl, isl = n % 3, n % 2
        S_ = DS[dsl]
        O_ = IO[isl]
        dk_ = lambda nm: "%s_%d" % (nm, dsl)
        ik_ = lambda nm: "%s_i%d" % (nm, isl)

        def sk_(nm):
            if nm in ("aT", "rT", "bT", "kT", "rkT", "PCt"):
                return dk_(nm)
            if nm in ("Tm0", "AakT", "ArbT", "ArkT", "BK"):
                return ik_(nm)
            return nm + "_s"
        aT, rT, PCt = (S_[q] for q in ("aT", "rT", "PCt"))
        BK, AakT, ArbT, ArkT = (O_[q] for q in ("BK", "AakT", "ArbT", "ArkT"))
        tp = i % 2
        Vt, vk = Vt2[tp], "Vt%d" % tp
        Y = Y2[tp]
        Tt = O_["Tm0"]
        Ttk = ik_("Tm0")
        Mbk = "Mb%d" % g
        Mfk = "Mf%d" % g
        b1, b1k = P.bank()
        for j in range(4):
            h = g * 4 + j
            P.mm(b1[:, j * 64:(j + 1) * 64], aT[:, j, :], Mb[:, h, :], True, False, [sk_("aT"), Mbk, "Mb"], [b1k])
            P.mm(b1[:, j * 64:(j + 1) * 64], AakT[:, j, :], Vt[:, h * 64:(h + 1) * 64], False, True, [sk_("AakT"), vk], [b1k])
        P.copy("act", R1s, v4(b1[:, 0:256]), [b1k], [sk_("R1s")])
        yield
        b2, b2k = P.bank()
        for j in range(4):
            P.mm(b2[:, j * 64:(j + 1) * 64], Tt[:, j, :], R1s[:, j, :], True, True, [Ttk, sk_("R1s")], [b2k])
        P.copy("act", Ub, v4(b2[:, 0:256]), [b2k], [sk_("Ub")])
        yield
        b4, b4k = P.bank()
        for j in range(4):
            h = g * 4 + j
            P.mm(b4[0:64, j * 64:(j + 1) * 64], BK[:, j, :], Ub[:, j, :], True, False, [sk_("BK"), sk_("Ub")], [b4k])
            P.mm(b4[0:64, j * 64:(j + 1) * 64], BK[:, 4 + j, :], Vt[:, h * 64:(h + 1) * 64], False, True, [sk_("BK"), vk], [b4k])
        b3, b3k = P.bank()
        for j in range(4):
            h = g * 4 + j
            P.mm(b3[:, j * 64:(j + 1) * 64], rT[:, j, :], Mb[:, h, :], True, False, [sk_("rT"), Mbk, "Mb"], [b3k])
            P.mm(b3[:, j * 64:(j + 1) * 64], ArbT[:, j, :], Ub[:, j, :], False, False, [sk_("ArbT"), sk_("Ub")], [b3k])
            P.mm(b3[:, j * 64:(j + 1) * 64], ArkT[:, j, :], Vt[:, h * 64:(h + 1) * 64], False, True, [sk_("ArkT"), vk], [b3k])
        yield
        Mfg = Mf[:, g * 4:(g + 1) * 4, :]
        P.tt("dve", Mfg, Mfg, v4(b4[0:64, 0:256]), ALU.add, [Mfk, "Mf", b4k], [Mfk])
        P.tt("pool", Mfg, Mfg, bc(PCt, [64, 4, 64]), ALU.mult, [Mfk, sk_("PCt")], [Mfk])
        P.copy("act", Mb[:, g * 4:(g + 1) * 4, :], Mfg, [Mfk], [Mbk])
        yield
        P.copy("dve", Y[:, g * 4:(g + 1) * 4, :], v4(b3[:, 0:256]), [b3k], ["Y%d" % tp])
        yield

    def post(i):
        tp = i % 2
        Y = Y2[tp]
        Vt, sg, qmT = Vt2[tp], sg2[tp], qmT2[tp]
        vk, sk, qk = "Vt%d" % tp, "sg%d" % tp, "qmT%d" % tp
        s1, s2, mean, m2, var, sd, rs = [st[:, q, :].unsqueeze(2) for q in range(7)]
        P.op("dve", lambda e: e.tensor_reduce(out=s1, in_=Y, axis=AX.X, op=ALU.add), ["Y%d" % tp], ["s1"])
        P.tt("pool", Ysq, Y, Y, ALU.mult, ["Y%d" % tp], ["Ysq"])
        yield
        P.op("dve", lambda e: e.tensor_reduce(out=s2, in_=Ysq, axis=AX.X, op=ALU.add), ["Ysq"], ["s2"])
        P.ts("dve", mean, s1, 1.0 / 64, None, ALU.mult, None, ["s1"], ["mean"])
        P.tt("dve", m2, mean, mean, ALU.mult, ["mean"], ["m2"])
        P.stt(var, s2, 1.0 / 64, m2, ALU.mult, ALU.subtract, ["s2", "m2"], ["var"])
        yield
        P.ts("dve", sd, var, LNX_EPS, None, ALU.add, None, ["var"], ["sd"])
        rsqrt_op(P, rs, sd, mhalf.unsqueeze(2), ["sd"], ["rs"])
        yield
        P.tt("dve", Ysq, Y, bc(mean, [128, 12, 64]), ALU.subtract, ["Y%d" % tp, "mean", "Ysq"], ["Ysq"])
        yield
        P.tt("pool", Ysq, Ysq, bc(rs, [128, 12, 64]), ALU.mult, ["Ysq", "rs"], ["Ysq"])
        yield
        Yf = Ysq.rearrange("p a b -> p (a b)")
        P.tt("pool", Yf, Yf, lnx_b[:, 0, :], ALU.mult, ["Ysq", "lnx_b"], ["Ysq"])
        yield
        P.tt("pool", Yf, Yf, lnx_b[:, 1, :], ALU.add, ["Ysq", "lnx_b"], ["Ysq"])
        yield
        for h in range(12):
            P.stt(Ysq[:, h, :], Vt[:, h * 64:(h + 1) * 64], bs2[tp][:, h, :], Ysq[:, h, :], ALU.mult, ALU.add,
                  [vk, "bs%d" % tp, "Ysq"], ["Ysq"])
            if h % 4 == 3:
                yield
        P.tt("dve", YG0[:, 0:768], Yf, sg[:, 0:768], ALU.mult, ["Ysq", sk], ["YG0"])
        yield
        mem_attn(0, A, qmT, sg, YG0, qk=qk, sk=sk, yk="YG0")
        yield
        out_proj(0, i, A, YG0, final=False, yk="YG0")
        yield

    def interleave_n(gens, rates=None):
        gens = list(gens)
        rates = rates or [1] * len(gens)
        alive = [True] * len(gens)
        while any(alive):
            for q in range(len(gens)):
                for _ in range(rates[q]):
                    if alive[q]:
                        try:
                            next(gens[q])
                        except StopIteration:
                            alive[q] = False

    def chain_gens(gs):
        for g_ in gs:
            yield from g_

    def interleave(ga, gb):
        alive = [True, True]
        gens = [ga, gb]
        while alive[0] or alive[1]:
            for q in range(2):
                if alive[q]:
                    try:
                        next(gens[q])
                    except StopIteration:
                        alive[q] = False

    if nlayers >= 1:
        units = [(i, g) for i in range(NT) for g in range(3)]

        NU = len(units)

        def dstage(n):
            i, g = units[n]
            gs = []
            if g == 0:
                if i + 2 < NT:
                    load_x(i + 2)
                if i == 1:
                    late_precasts()
                gs.append(pre(i))
            gs.append(dphase(i, g, n % 3))
            return chain_gens(gs)

        CSLOW = int(os.environ.get('KCS', '4'))

        def slow(gen, k):
            for _ in gen:
                yield
                for _q in range(k - 1):
                    yield

        for n in range(-2, NU):
            gens = []
            if n >= 0:
                gens.append(slow(cstage(units[n][0], units[n][1], n), CSLOW))
            if 0 <= n + 1 < NU:
                gens.append(istage(units[n + 1][0], units[n + 1][1], n + 1))
            if n + 2 < NU:
                gens.append(dstage(n + 2))
            if n >= 0 and units[n][1] == 0 and units[n][0] > 0:
                gens.append(post(units[n][0] - 1))
            interleave_n(gens)
        for _ in post(NT - 1):
            pass

    final_ops = []
    if dbg or nlayers < 2:
        tgt = dbg_d if dbg else out_d
        for i in range(NT):
            final_ops.append(P.dma("sp", tgt[i * 128:(i + 1) * 128, :], x_res[:, i, :], ["x%d" % i], []))

    if nlayers >= 2:
        P.barrier()
        arena.off = common_off
        P.dma("sp", postg_b, postg_d[:, 1, :], (), ["postg_b"])
        mark1 = arena.off
        dtmp = arena.alloc([128], F32)
        tq64 = arena.alloc([128], F32)
        itab = arena.alloc([16], F32)
        ndtmp = arena.alloc([128], F32)
        lamv = arena.alloc([4, 64], F32)
        lamw = arena.alloc([4, 64], F32)
        arena.off = mark1
        arena.off = arena.off + 2048 + 2048 + 512 + 512 + 64
        KTt, Vat = [], []
        for i_ in range(NT):
            if i_ < 3:
                KTt.append(arena.alloc([6, 128], BF16))
                Vat.append(arena.alloc([6, 129], BF16))
            else:
                xv = x_res[:, i_ - 3, :].bitcast(BF16)
                KTt.append(xv[:, 0:768].rearrange("p (h t) -> p h t", h=6))
                Vat.append(xv[:, 768:768 + 774].rearrange("p (h e) -> p h e", h=6))
        PT4 = [arena.alloc([2, NT, 128], BF16) for _ in range(2)]
        P.dma("sp", lamv, lamv_d, (), ["lamv"])
        hT1 = arena.alloc([8, 128], BF16)
        kvT = arena.alloc([8, 128], BF16)
        QT2 = [arena.alloc([6, 2, 128], BF16) for _ in range(2)]
        sg2b = [arena.alloc([1024], BF16) for _ in range(3)]
        qmT2b = [arena.alloc([4, 128], BF16, parts=64) for _ in range(3)]
        AT = arena.alloc([6, 128], F32)
        ATs = arena.alloc([6, 128], F32)
        YG = arena.alloc([1024], BF16)
        Dh = arena.alloc([6, 128], BF16)
        btab = arena.alloc([6, 16], F32)
        rzz = arena.alloc([2, 1], F32)
        w1 = arena.alloc([1], F32)
        s6 = arena.alloc([6, 1], F32)
        r6 = arena.alloc([6, 1], F32)
        stmp = arena.alloc([2, 128], F32)
        t1 = arena.alloc([128], F32)
        RES = {}
        cand = ([(ws_kv, k_kv, c0) for c0 in range(0, 1536, 256)] + [(ws_out[1], k_out[1], c0) for c0 in range(0, 1024, 256)]
                + [(ws_bin, k_bin, c0) for c0 in range(0, 768, 256)])
        for (ws_, kk_, c0) in cand:
            if arena.off + 4096 > arena.cap:
                break
            t_ = arena.alloc([8, 256], BF16)
            rk_ = "res_%s_%d" % (ws_.name, c0)
            P.dma("sp", t_, ws_.t[ws_.pieces[c0][0]], kk_[c0], [rk_])
            RES[(ws_.name, c0)] = (t_, rk_)
        resident[0] = RES
        print("resident pieces", len(RES))
        P.op("pool", lambda e: e.iota(dtmp, pattern=[[1, 128]], base=0, channel_multiplier=-1,
                                      allow_small_or_imprecise_dtypes=True), (), ["dtmp"])
        P.op("pool", lambda e: e.iota(tq64, pattern=[[1, 128]], base=-64, channel_multiplier=0,
                                      allow_small_or_imprecise_dtypes=True), (), ["tq64"])
        P.op("pool", lambda e: e.iota(itab, pattern=[[-128, 16]], base=-64, channel_multiplier=1,
                                      allow_small_or_imprecise_dtypes=True), (), ["itab"])
        P.op("pool", lambda e: e.iota(ndtmp, pattern=[[-1, 128]], base=0, channel_multiplier=1,
                                      allow_small_or_imprecise_dtypes=True), (), ["ndtmp"])
        P.tt("dve", dtmp, dtmp, ndtmp, ALU.max, ["dtmp", "ndtmp"], ["dtmp"])
        for h in range(6):
            P.ts("dve", Dh[:, h, :], dtmp, -SLOPES[h], None, ALU.mult, None, ["dtmp"], ["Dh"])
            P.stt(Dh[:, h, :], tq64, SLOPES[h], Dh[:, h, :], ALU.mult, ALU.add, ["tq64", "Dh"], ["Dh"])
            P.ts("dve", btab[:, h, :], itab, SLOPES[h], None, ALU.mult, None, ["itab"], ["btab"])
        P.memset("pool", Dh[64:128, :, 0:64], -30000.0, ["Dh"])
        P.tt("dve", lamw[:, 0, :], lamv[:, 0, :], lamv[:, 1, :], ALU.mult, ["lamv"], ["lamw"])
        P.tt("dve", lamw[:, 1, :], lamv[:, 2, :], lamv[:, 3, :], ALU.mult, ["lamv", "lamw"], ["lamw"])
        P.op("dve", lambda e: e.tensor_reduce(out=lamc[:, 0:2].unsqueeze(2), in_=lamw[:, 0:2, :], axis=AX.X, op=ALU.add),
             ["lamw"], ["lamc"])
        P.act(lamc[:, 2:4], lamc[:, 0:2], AF.Exp, ["lamc"], ["lamc2"])
        P.tt("dve", lamc[:, 4:5], lamc[:, 3:4], lamc[:, 2:3], ALU.subtract, ["lamc2"], ["lamc4"])
        P.ts("dve", lamc[:, 5:6], lamc[:, 4:5], -LAM_INIT1, None, ALU.add, None, ["lamc4"], ["neglam"])
        neglam = lamc[:, 5:6]
        P.ts("dve", subln_b, subln_b, 0.5 * (1.0 - LAM_INIT1), None, ALU.mult, None, ["subln_b"], ["subln_b"])
        for q in range(2):
            P.memset("pool", QT2[q], 0.0, ["QT%d" % q])

        def proj1(i):
            xk = "x%d" % i
            tp = i % 2
            t3_ = i % 3
            QT, sg, qmT = QT2[tp], sg2b[t3_], qmT2b[t3_]
            qtk, sk, qk = "QT%d" % tp, "sgb%d" % t3_, "qmTb%d" % t3_
            rmsnorm_T(x_res[:, i, :], [xk], [(4, kvT, "kvT"), (1, hT1, "hT1")], A)
            xdead = ["x%d" % (i - 3)] if i >= 3 else []
            P.memset("pool", Vat[i][:, :, 128:129], 1.0, ["Vaug%d" % i] + xdead)
            yield
            for hh in range(0, 6, 2):
                wt, wkey = stage(ws_kv, k_kv, hh * 128, 256)
                bk, bkey = P.bank()
                for q in range(2):
                    for kc in range(8):
                        P.mm(bk[:, q * 128:(q + 1) * 128], wt[:, kc, q * 128:(q + 1) * 128], kvT[:, kc, :], kc == 0, kc == 7,
                             ["kvT", wkey], [bkey])
                P.copy("dve", KTt[i][:, hh:hh + 2, :], bk[:, 0:256].rearrange("p (a b) -> p a b", a=2),
                       [bkey], ["KT%d" % i] + xdead)
                yield
            for q in range(3):
                (vb1, vb1k), = tm_job(ws_kv, k_kv, 768 + q * 256, 256, [(kvT, "kvT")], None)
                P.copy("dve", Vat[i][:, 2 * q:2 * q + 2, 0:128], vb1[:, 0:256].rearrange("p (a b) -> p a b", a=2),
                       [vb1k], ["Vaug%d" % i] + xdead)
                yield
            for hh in range(0, 6, 2):
                wt, wkey = stage(ws_bin, k_bin, hh * 128, 256)
                bk, bkey = P.bank(nr=2)
                for q in range(2):
                    for kc in range(8):
                        P.mm(bk[:, q * 128:(q + 1) * 128], wt[:, kc, q * 128:(q + 1) * 128], hT1[:, kc, :], kc == 0, kc == 7,
                             ["hT1", wkey], [bkey])
                for c in range(2):
                    P.copy("dve", QT[c * 64:(c + 1) * 64, hh:hh + 2, c, :],
                           bk[c * 64:(c + 1) * 64, 0:256].rearrange("p (a b) -> p a b", a=2), [bkey, qtk], [qtk])
                yield
            for q in range(4):
                c0 = 768 + q * 256 if q < 3 else 1792
                (g1, g1k), = tm_job(ws_bin, k_bin, c0, 256, [(hT1, "hT1")], [2])
                P.act(sg[:, q * 256:(q + 1) * 256], g1[:, 0:256], AF.Tanh, [g1k], [sk], scale=0.5)
                P.stt(sg[:, q * 256:(q + 1) * 256], sg[:, q * 256:(q + 1) * 256], 1.0, g1[:, 0:256], ALU.add, ALU.mult,
                      [sk, g1k], [sk])
                yield
            qb, qbk = fm64_job(ws_bin, k_bin, 1536, hT1, "hT1")
            P.copy("dve", qmT, qb[0:64, :].rearrange("p (a b) -> p a b", a=4), [qbk], [qk])
            yield

        def attn1(i):
            xk = "x%d" % i
            tp = i % 2
            QT = QT2[tp]
            qtk = "QT%d" % tp

            def scores(h):
                PT = PT4[h % 2]
                ptk = "PT%d" % (h % 2)
                for j in range(i + 1):
                    bk, bkey = P.bank()
                    P.mm(bk[:, 0:256], KTt[j][:, h, :], QT[:, h, :, :].rearrange("p c q -> p (c q)"),
                         True, True, ["KT%d" % j, qtk], [bkey])
                    sv = bk[:, 0:256].rearrange("p (c q) -> p c q", c=2)
                    if j == i:
                        P.stt(stmp, sv, 0.125, bc(Dh[:, h, :].unsqueeze(1), [128, 2, 128]), ALU.mult, ALU.add,
                              [bkey, "Dh"], ["stmp"])
                        P.act(PT[:, :, j, :], stmp, AF.Exp, ["stmp"], [ptk])
                    else:
                        P.act(PT[:, :, j, :], sv, AF.Exp, [bkey, "btab"], [ptk],
                              bias=btab[:, h, (i - j):(i - j) + 1], scale=0.125)
                    if j % 2 == 1:
                        yield

            def pv(h):
                PT = PT4[h % 2]
                ptk = "PT%d" % (h % 2)
                bk, bkey = P.bank(6, 8, nr=3)
                ov = bk[:, 0:258].rearrange("p (a b) -> p a b", a=2)
                for c in range(2):
                    for j in range(i + 1):
                        P.mm(ov[:, c, :], PT[:, c, j, :], Vat[j][:, h, :], j == 0, j == i, [ptk, "Vaug%d" % j], [bkey])
                        if j % 4 == 3:
                            yield
                P.op("dve", lambda e, ov=ov: e.reciprocal(out=rzz, in_=ov[:, :, 128:129]), [bkey], ["rzz"])
                P.tt("dve", w1, rzz[:, 1, :], neglam, ALU.mult, ["rzz", "neglam"], ["w1"])
                P.ts("dve", t1, ov[:, 1, 0:128], w1, None, ALU.mult, None, [bkey, "w1"], ["t1"])
                P.stt(AT[:, h, :], ov[:, 0, 0:128], rzz[:, 0, :], t1, ALU.mult, ALU.add, [bkey, "rzz", "t1"], ["AT"])
                yield

            yield from scores(0)
            for h in range(6):
                if h + 1 < 6:
                    yield from scores(h + 1)
                yield from pv(h)

        def tail1(i):
            t3_ = i % 3
            sg, qmT = sg2b[t3_], qmT2b[t3_]
            sk, qk = "sgb%d" % t3_, "qmTb%d" % t3_
            P.tt("pool", ATs, AT, AT, ALU.mult, ["AT"], ["ATs"])
            P.op("dve", lambda e: e.tensor_reduce(out=s6, in_=ATs, axis=AX.X, op=ALU.add), ["ATs"], ["s6"])
            P.ts("dve", r6, s6, 1.0 / 128, NORM_EPS, ALU.mult, ALU.add, ["s6"], ["r6"])
            rsqrt_op(P, s6, r6, mhalf[:, 0:6].unsqueeze(2), ["r6"], ["s6"])
            yield
            P.tt("pool", ATs, AT, bc(s6, [128, 6, 128]), ALU.mult, ["AT", "s6", "ATs"], ["ATs"])
            P.tt("pool", ATs, ATs, bc(subln_b.unsqueeze(1), [128, 6, 128]), ALU.mult, ["ATs", "subln_b"], ["ATs"])
            yield
            P.tt("dve", YG[:, 0:768], ATs.rearrange("p a b -> p (a b)"), sg[:, 0:768], ALU.mult, ["ATs", sk], ["YG1"])
            yield
            mem_attn(1, A, qmT, sg, YG, qk=qk, sk=sk, yk="YG1")
            yield
            final_ops.append(out_proj(1, i, A, YG, final=True, yk="YG1"))
            yield

        def interleave2(ga, gb, ra=1, rb=1):
            alive = [True, True]
            gens = [ga, gb]
            rr = [ra, rb]
            while alive[0] or alive[1]:
                for q in range(2):
                    for _ in range(rr[q]):
                        if alive[q]:
                            try:
                                next(gens[q])
                            except StopIteration:
                                alive[q] = False

        P.bank_rng = (0, 6)
        for _ in proj1(0):
            pass
        for i in range(NT):
            na = max(1, (6 * ((i + 2) // 2 + (i + 1) // 4 + 2) + 6) // 18)
            gens, rates = [attn1(i)], [na]
            if i + 1 < NT:
                gens.append(proj1(i + 1))
                rates.append(1)
            if i > 0:
                gens.append(tail1(i - 1))
                rates.append(1)
            interleave_n(gens, rates)
        for _ in tail1(NT - 1):
            pass

    print("arena use", arena.off, "cst use", cst.off)
    P.emit(final_wait_ops=final_ops)
    return nc, P


_CACHE = {}


def prep_inputs(inp, b):
    f = lambda a: np.ascontiguousarray(np.asarray(a, dtype=np.float32))
    mu = f(inp["a_shift_mu"])[0]
    cols = []
    for g in range(3):
        for j in range(4):
            cols.append(np.arange((g * 4 + j) * 64, (g * 4 + j + 1) * 64))
        for j in range(4):
            cols.append(768 + np.arange((g * 4 + j) * 64, (g * 4 + j + 1) * 64))
    cols.append(np.arange(2304, 2368))
    cols.append(np.arange(2368, 2432))
    mu64 = np.stack([mu[c] for c in cols], axis=1)
    gains = np.stack([f(inp["pre_norm"])[0], f(inp["pre_norm"])[1], f(inp["mem_norm"])[0], f(inp["mem_norm"])[1],
                      f(inp["kv_norm"])], 0)
    gains = gains.reshape(5, 8, 128).transpose(2, 0, 1)
    hp = np.stack([f(inp["a_k_k"])[0].reshape(12, 64).T, f(inp["a_k_a"])[0].reshape(12, 64).T,
                   f(inp["a_r_k"])[0].T], 1)
    d = {
        "x": f(inp["x"][b]),
        "mem": f(inp["mem"][b]),
        "a_w_in": f(inp["a_w_in"])[0],
        "w_out": f(inp["w_out"]),
        "w_mem_kv": f(inp["w_mem_kv"]),
        "w_kv": f(inp["w_kv"]),
        "b_w_in": f(inp["b_w_in"])[0],
        "gains": f(gains),
        "mu64": f(mu64),
        "muv_b": f(np.broadcast_to(mu[1536:2304], (128, 768))),
        "w2aug": f(np.concatenate([f(inp["a_w2"])[0], f(inp["a_w0"])], 0)),
        "a2aug": f(np.concatenate([f(inp["a_a2"])[0], f(inp["a_a0"])], 0)),
        "headp": f(hp),
        "lnx_b": f(np.broadcast_to(np.stack([f(inp["a_lnx_w"])[0], f(inp["a_lnx_b"])[0]], 0), (128, 2, 768))),
        "postg_b": f(np.broadcast_to(f(inp["post_norm"]), (128, 2, 1024))),
        "subln_b": f(np.broadcast_to(f(inp["b_subln"])[0], (128, 128))),
        "lamv_b": f(np.broadcast_to(np.stack([f(inp["b_lam_q1"])[0], f(inp["b_lam_k1"])[0], f(inp["b_lam_q2"])[0],
                                              f(inp["b_lam_k2"])[0]], 0), (128, 4, 64))),
    }
    return d


def kernel(**inputs):
    if "nc" not in _CACHE:
        _CACHE["nc"] = build()[0]
    nc = _CACHE["nc"]
    in_maps = [prep_inputs(inputs, b) for b in range(8)]
    res = run_bass_kernel_spmd(nc, in_maps, core_ids=list(range(8)))
    out = np.stack([np.asarray(r["out"], dtype=np.float32) for r in res.results], 0)
    return out
```

```python
import math
import os
TOG = 'nomerge,nobank2'
SCHED = os.environ.get('KSCHED', '1') == '1'
PEF, ACTF, DVEF, POOLF, LATF = [float(v) for v in os.environ.get('KF', '0.93,1.07,0.98,0.80,0.25').split(',')]
FREE_SP = os.environ.get('KFSP', '1') == '1'
DMAX = float(os.environ.get('KDX', '0.9'))
QUANT = float(os.environ.get('KQ', '0.05'))
RATES0 = [int(v) for v in os.environ.get('KR0', '1,1,1').split(',')]
from contextlib import ExitStack

import numpy as np
import concourse.bass as bass
import concourse.mybir as mybir
from concourse.bass_utils import run_bass_kernel_spmd

F32 = mybir.dt.float32
BF16 = mybir.dt.bfloat16
AF = mybir.ActivationFunctionType
ALU = mybir.AluOpType
AX = mybir.AxisListType

S = 2048
D = 1024
NT = 16
CDEC = math.exp(-0.5)
NORM_EPS = 1e-6
LNX_EPS = 64e-5
SLOPES = [0.25, 0.0625, 0.015625, 0.00390625, 0.5, 0.125]
LAM_INIT1 = 0.8 - 0.6 * math.exp(-0.3 * 1)

NDMA_SLOTS = 16
COMPUTE = ("pe", "act", "dve", "pool")
QUEUES = ("sp", "actq", "poolq")
STREAM_OF = {"pe": "tensor", "act": "scalar", "dve": "vector", "pool": "gpsimd",
             "sp": "sync", "actq": "scalar", "poolq": "gpsimd"}
STREAMS = ["tensor", "scalar", "vector", "gpsimd", "sync"]


class Prog:
    def __init__(self, nc):
        self.nc = nc
        self.es = ExitStack()
        self.ops = []
        self.last_w = {}
        self.readers = {}
        self.ndma = {q: 0 for q in QUEUES}
        self.last_slot = {}
        self.barrier_deps = []
        self.nbank = 0
        self.nbank2 = 0
        self.bank_rng = (0, 8)
        self.pending = {}
        self.last_q = {}
        self.seg = 0
        self.banks = None

    def sb(self, name, shape, dt=F32):
        return self.es.enter_context(self.nc.sbuf_tensor(name, list(shape), dt))

    def ps(self, name, shape, dt=F32):
        return self.es.enter_context(self.nc.psum_tensor(name, list(shape), dt))

    def bank(self, lo=None, hi=None, nr=1):
        if lo is None:
            lo, hi = self.bank_rng
        n = hi - lo
        for t in range(n):
            i = lo + ((self.nbank + t) % n)
            key = "bank%d" % i
            if self.pending.get(key, 0) <= 0:
                self.nbank += t + 1
                self.pending[key] = nr
                return self.banks[i], key
        raise RuntimeError("no free PSUM bank in [%d,%d): %s" % (lo, hi, self.pending))

    def bank2(self, lo=0, hi=8):
        n = (hi - lo) // 2
        i = lo + 2 * (self.nbank2 % n)
        self.nbank2 += 1
        return self.psall[:, i:i + 2, :], ["bank%d" % i, "bank%d" % (i + 1)]

    def barrier(self):
        self.seg += 1
        last = {}
        for i, o in enumerate(self.ops):
            last[o["veng"]] = i
        self.barrier_deps = sorted(last.values())

    def op(self, eng, fn, reads=(), writes=(), dur=None):
        idx = len(self.ops)
        if eng != "pe":
            for k in reads:
                if k in self.pending and self.pending[k] > 0:
                    self.pending[k] -= 1
        raw = set(self.barrier_deps) if not SCHED else set()
        oth = set()
        for k in reads:
            if k in self.last_w:
                raw.add(self.last_w[k])
        for k in writes:
            if k in self.last_w:
                oth.add(self.last_w[k])
            for r in self.readers.get(k, ()):
                oth.add(r)
        if eng in QUEUES:
            n = self.ndma[eng]
            self.ndma[eng] += 1
            veng = (eng, n % NDMA_SLOTS)
            if veng in self.last_slot:
                raw.add(self.last_slot[veng])
            self.last_slot[veng] = idx
        else:
            veng = (eng, 0)
        deps = set(raw)
        odeps = set()
        for d in oth:
            if self.ops[d]["veng"] == veng and eng in COMPUTE:
                odeps.add(d)
                continue
            deps.add(d)
        if eng in QUEUES:
            if eng in self.last_q and not (eng == "sp" and FREE_SP):
                odeps.add(self.last_q[eng])
            self.last_q[eng] = idx
        deps.discard(idx)
        odeps.discard(idx)
        if dur is None:
            dur = {"pe": 0.11, "act": 0.5, "dve": 0.6, "pool": 1.0}.get(eng, 3.0)
        self.ops.append(dict(eng=eng, veng=veng, fn=fn, deps=sorted(deps), odeps=sorted(odeps - deps),
                             stream=STREAM_OF[eng], seg=self.seg, dur=dur))
        for k in reads:
            self.readers.setdefault(k, []).append(idx)
        for k in writes:
            self.last_w[k] = idx
            self.readers[k] = []
        return idx

    @staticmethod
    def _n(ap):
        n = 1
        for d in ap.shape[1:]:
            n *= d
        return n

    def _dur(self, eng, out):
        n = self._n(out)
        if eng == "pe":
            return PEF * (0.06 + 0.0004 * n)
        if eng == "act":
            return ACTF * (0.22 + 0.00075 * n)
        if eng == "dve":
            return DVEF * (0.2 + 0.00105 * n)
        return POOLF * (0.3 + 0.0021 * n)

    def tt(self, eng, out, in0, in1, op, r, w):
        d = self._dur(eng, out) if op != ALU.pow else 2.3
        return self.op(eng, lambda e: e.tensor_tensor(out=out, in0=in0, in1=in1, op=op), r, w, dur=d)

    def ts(self, eng, out, in0, s1, s2, op0, op1, r, w):
        if op1 is None:
            return self.op(eng, lambda e: e.tensor_scalar(out=out, in0=in0, scalar1=s1, scalar2=None, op0=op0), r, w,
                           dur=self._dur(eng, out))
        return self.op(eng, lambda e: e.tensor_scalar(out=out, in0=in0, scalar1=s1, scalar2=s2, op0=op0, op1=op1), r, w,
                       dur=self._dur(eng, out))

    def stt(self, out, in0, scalar, in1, op0, op1, r, w):
        return self.op("dve", lambda e: e.scalar_tensor_tensor(out=out, in0=in0, scalar=scalar, in1=in1, op0=op0, op1=op1), r, w,
                       dur=self._dur("dve", out))

    def act(self, out, in_, func, r, w, bias=None, scale=None, accum_out=None):
        kw = {}
        if bias is not None:
            kw["bias"] = bias
        if scale is not None:
            kw["scale"] = scale
        if accum_out is not None:
            kw["accum_out"] = accum_out
        return self.op("act", lambda e: e.activation(out=out, in_=in_, func=func, **kw), r, w,
                       dur=self._dur("act", out) + (0.1 if accum_out is not None else 0.0))

    def copy(self, eng, out, in_, r, w):
        if eng == "act":
            return self.op("act", lambda e: e.activation(out=out, in_=in_, func=AF.Copy), r, w, dur=self._dur("act", out))
        return self.op(eng, lambda e: e.tensor_copy(out=out, in_=in_), r, w, dur=self._dur(eng, out))

    def mm(self, out, lhsT, rhs, start, stop, r, w):
        return self.op("pe", lambda e: e.matmul(out, lhsT=lhsT, rhs=rhs, start=start, stop=stop), r, w,
                       dur=self._dur("pe", out))

    def tr(self, out, in_, ident, r, w):
        return self.op("pe", lambda e: e.transpose(out=out, in_=in_, identity=ident), r, w, dur=0.1)

    def dma(self, q, out, in_, r, w, **kw):
        return self.op(q, lambda e: e.dma_start(out=out, in_=in_, **kw), r, w)

    def memset(self, eng, ap, val, w):
        return self.op(eng, lambda e: e.memset(ap, val), (), w, dur=self._dur(eng, ap))

    def schedule(self):
        ops = self.ops
        n = len(ops)
        succ = [[] for _ in range(n)]
        npred = [0] * n
        for i, o in enumerate(ops):
            preds = set(o["deps"]) | set(o["odeps"])
            o["preds"] = preds
            for d in preds:
                succ[d].append(i)
            npred[i] = len(preds)
        prio = [0.0] * n
        for i in range(n - 1, -1, -1):
            m = 0.0
            for q in succ[i]:
                if prio[q] > m:
                    m = prio[q]
            prio[i] = ops[i]["dur"] + m
        finish = [0.0] * n
        order = []
        nseg = self.seg + 1
        by_seg = [[] for _ in range(nseg)]
        for i, o in enumerate(ops):
            by_seg[o["seg"]].append(i)
        tbase = 0.0
        chan_free = [0.0]
        LAT = LATF
        for sg in range(nseg):
            stream_free = {st: tbase for st in STREAMS}
            ready = {st: [] for st in STREAMS}
            rtime = {}
            left = len(by_seg[sg])
            in_seg = set(by_seg[sg])
            for i in by_seg[sg]:
                cnt = 0
                for d in ops[i]["preds"]:
                    if d in in_seg:
                        cnt += 1
                npred[i] = cnt
                if cnt == 0:
                    rtime[i] = tbase
                    ready[ops[i]["stream"]].append(i)
            while left:
                best = None
                for st in STREAMS:
                    rl = ready[st]
                    if not rl:
                        continue
                    sf = stream_free[st]
                    bi = None
                    bk = None
                    for i in rl:
                        t0 = rtime[i]
                        if t0 < sf:
                            t0 = sf
                        k = (int(t0 / QUANT), -prio[i])
                        if bk is None or k < bk:
                            bk = k
                            bi = i
                    if best is None or bk < best[0]:
                        best = (bk, bi, st)
                _, i, st = best
                t0 = max(rtime[i], stream_free[st])
                ready[st].remove(i)
                o = ops[i]
                if o["eng"] in QUEUES:
                    stream_free[st] = t0 + 0.08
                    if FREE_SP:
                        tx = max(t0 + 0.3, chan_free[0])
                        finish[i] = tx + o["dur"] * DMAX
                        chan_free[0] = finish[i]
                    else:
                        finish[i] = t0 + o["dur"]
                else:
                    stream_free[st] = t0 + o["dur"]
                    finish[i] = t0 + o["dur"]
                order.append(i)
                left -= 1
                for q in succ[i]:
                    if q not in in_seg:
                        continue
                    npred[q] -= 1
                    if npred[q] == 0:
                        rt = tbase
                        oq = ops[q]
                        for d in oq["preds"]:
                            if d in in_seg:
                                lat = 0.0 if ops[d]["stream"] == oq["stream"] and d in oq["odeps"] else LAT
                                if ops[d]["stream"] == oq["stream"] and d not in oq["odeps"]:
                                    lat = 0.2
                                f = finish[d] + lat
                                if f > rt:
                                    rt = f
                        rtime[q] = rt
                        ready[oq["stream"]].append(q)
            tbase = max([tbase] + [finish[i] for i in by_seg[sg]]) + 2.0
        self.sim_time = tbase
        return order

    def emit(self, final_wait_ops=()):
        nc = self.nc
        ops = self.ops
        if SCHED:
            order = self.schedule()
        else:
            order = list(range(len(ops)))
        cnt = {}
        for i in order:
            o = ops[i]
            v = o["veng"]
            cnt[v] = cnt.get(v, 0) + 1
            o["cnt"] = cnt[v]
        vengs = sorted(cnt.keys())
        sems = {}
        for v in vengs:
            sems[v] = self.es.enter_context(nc.semaphore("s_%s_%d" % v))
        mult = {v: (16 if v[0] in QUEUES else 1) for v in vengs}
        seen = {s: {} for s in STREAMS}
        know = [None] * len(ops)
        plan = {s: [] for s in STREAMS}
        nwaits = 0
        cur_seg = 0
        last_of_veng = {}
        seg_barrier = {}
        for i in order:
            o = ops[i]
            s = o["stream"]
            sd = seen[s]
            deps = list(o["deps"])
            if SCHED:
                if o["seg"] != cur_seg:
                    cur_seg = o["seg"]
                    seg_barrier = dict(last_of_veng)
                if o["seg"] > 0:
                    deps += [d for d in seg_barrier.values()]
            waits = {}
            for d in sorted(set(deps), reverse=True):
                od = ops[d]
                v = od["veng"]
                if sd.get(v, 0) >= od["cnt"]:
                    continue
                waits[v] = max(waits.get(v, 0), od["cnt"])
                for kv, kc in know[d].items():
                    if sd.get(kv, 0) < kc:
                        sd[kv] = kc
                if sd.get(v, 0) < od["cnt"]:
                    sd[v] = od["cnt"]
            o["waits"] = waits
            nwaits += len(waits)
            know[i] = dict(sd)
            plan[s].append(i)
            last_of_veng[o["veng"]] = i
        self.nwaits = nwaits
        blk = self.es.enter_context(nc.Block())

        def make(sname):
            def body(eng):
                for i in plan[sname]:
                    o = ops[i]
                    for v, c in o["waits"].items():
                        eng.wait_ge(sems[v], c * mult[v])
                    ins = o["fn"](eng)
                    ins.then_inc(sems[o["veng"]], mult[o["veng"]])
                if sname == "sync":
                    for i in final_wait_ops:
                        o = ops[i]
                        eng.wait_ge(sems[o["veng"]], o["cnt"] * mult[o["veng"]])
            return body

        for sname in STREAMS:
            if not plan[sname] and sname != "sync":
                continue
            getattr(blk, sname)(make(sname))


class Arena:
    def __init__(self, P, name, nbytes):
        self.cap = nbytes
        self.t = P.sb(name, [128, nbytes // 4], F32)
        self.off = 0

    def reset(self):
        self.off = 0

    def alloc(self, shape, dt=F32, parts=128):
        n = 1
        for s in shape:
            n *= s
        nb = n * (2 if dt == BF16 else 4)
        nb32 = (nb + 31) // 32 * 32
        a = self.off
        self.off += nb32
        self.peak = max(getattr(self, "peak", 0), self.off)
        if self.off > self.cap:
            print("ARENA OVERFLOW", self.off, self.cap)
        v = self.t[0:parts, a // 4:(a + nb32) // 4]
        if dt == BF16:
            v = v.bitcast(BF16)
        v = v[:, 0:n]
        if len(shape) == 2:
            v = v.rearrange("p (a b) -> p a b", a=shape[0], b=shape[1])
        elif len(shape) == 3:
            v = v.rearrange("p (a b c) -> p a b c", a=shape[0], b=shape[1], c=shape[2])
        return v


def rsqrt_op(P, out, in_, mh, rk, wk, tmp=None):
    if 'nopow' in TOG:
        P.act(out, in_, AF.Sqrt, rk, wk)
        P.op("dve", lambda e: e.reciprocal(out=out, in_=out), wk, wk)
    else:
        P.tt("pool", out, in_, mh, ALU.pow, list(rk) + ["mhalf"], wk)


def bc(ap, shape):
    return ap.broadcast_to(list(shape))


def build(dbg=False, nlayers=2):
    nc = bass.Bass("TRN2", target_bir_lowering=False)
    P = Prog(nc)

    def din(name, shape):
        return nc.dram_tensor(name, list(shape), F32, kind="ExternalInput").ap()

    x_d = din("x", [S, D])
    mem_d = din("mem", [256, D])
    win_d = din("a_w_in", [D, 3712])
    wout_d = din("w_out", [2, D, D])
    wmkv_d = din("w_mem_kv", [2, D, 512])
    wkv_d = din("w_kv", [D, 1536])
    bwin_d = din("b_w_in", [D, 2048])
    gains_d = din("gains", [128, 5, 8])
    mu64_d = din("mu64", [64, 26])
    muv_d = din("muv_b", [128, 768])
    w2a_d = din("w2aug", [65, 768])
    a2a_d = din("a2aug", [65, 768])
    hp_d = din("headp", [64, 3, 12])
    lnx_d = din("lnx_b", [128, 2, 768])
    postg_d = din("postg_b", [128, 2, 1024])
    subln_d = din("subln_b", [128, 128])
    lamv_d = din("lamv_b", [128, 4, 64])
    out_d = nc.dram_tensor("out", [S, D], F32, kind="ExternalOutput").ap()
    if dbg:
        dbg_d = nc.dram_tensor("dbg", [S, D], F32, kind="ExternalOutput").ap()
    class WS:
        def __init__(self, name, groups):
            self.groups = groups
            self.pieces = {}
            idx = 0
            for (ca, cb, w) in groups:
                for c0 in range(ca, cb, w):
                    self.pieces[c0] = (idx, w)
                    idx += 1
            self.t = nc.dram_tensor(name, [idx, 128, 8, 256], BF16).ap()
            self.name = name

        def cast(self, src, order=None):
            order = order if order is not None else list(range(len(self.groups)))
            for gi in order:
                ca, cb, w = self.groups[gi]
                J = (cb - ca) // w
                j0 = self.pieces[ca][0]
                for kc in range(8):
                    dst = self.t[j0:j0 + J, :, kc, 0:w].rearrange("j p c -> p j c")
                    srcv = src[kc * 128:(kc + 1) * 128, ca:cb].rearrange("p (j c) -> p j c", c=w)
                    P.dma("poolq", dst, srcv, (), ["%s_%d_%d" % (self.name, kc, gi)])
            return self.keymap()

        def keymap(self):
            km = {}
            for gi, (ca, cb, w) in enumerate(self.groups):
                for c0 in range(ca, cb, w):
                    km[c0] = ["%s_%d_%d" % (self.name, kc, gi) for kc in range(8)]
            return km

    ws_in = WS("ws_in", [(0, 768, 256), (768, 1536, 256), (1536, 2304, 256), (2304, 2432, 128), (2432, 3712, 256)])
    ws_out = [WS("ws_out%d" % l, [(0, 1024, 256)]) for l in range(2)]
    ws_mkv = [WS("ws_mkv%d" % l, [(0, 512, 256)]) for l in range(2)]
    ws_kv = WS("ws_kv", [(0, 1536, 256)])
    ws_bin = WS("ws_bin", [(0, 2048, 256)])

    psall = P.ps("psall", [128, 8, 512], F32)
    P.psall = psall
    P.banks = [psall[:, i, :] for i in range(8)]

    x_res = P.sb("x_res", [128, NT, D], F32)
    NWST = 4
    Wst = [P.sb("wst%d" % i, [128, 8, 256], BF16) for i in range(NWST)]
    wst_i = [0]
    cst = Arena(P, "cst", 22016)
    identf = cst.alloc([128], F32)
    ident = cst.alloc([128], BF16)
    maskSU = cst.alloc([128], BF16)
    maskU = cst.alloc([128], BF16)
    maskSL = cst.alloc([128], BF16)
    mtmp = cst.alloc([128], F32)
    ones64 = cst.alloc([64], F32, parts=64)
    onescol = cst.alloc([2], BF16, parts=64)
    cmask = cst.alloc([512], F32, parts=64)
    epsn = cst.alloc([1], F32)
    epsl = cst.alloc([1], F32)
    mhalf = cst.alloc([12], F32)
    rkc = cst.alloc([12, 2], BF16, parts=64)
    gains = cst.alloc([5, 8], F32)
    mu64 = cst.alloc([26], F32, parts=64)
    muv_b = cst.alloc([768], F32)
    w2aug = cst.alloc([768], BF16, parts=65)
    a2aug = cst.alloc([768], BF16, parts=65)
    headp = cst.alloc([3, 12], F32, parts=64)
    postg_b = cst.alloc([1024], F32)
    subln_b = cst.alloc([128], F32)
    lamc = cst.alloc([8], F32)
    KmT = [cst.alloc([4, 256], BF16, parts=64) for _ in range(2)]
    Vma = [cst.alloc([2, 4, 65], BF16) for _ in range(2)]

    k_in = ws_in.cast(win_d, order=[2, 4, 3, 0, 1])
    k_mkv = [ws_mkv[l].cast(wmkv_d[l]) for l in range(2)]
    k_out = [ws_out[0].cast(wout_d[0]), ws_out[1].keymap()]
    k_kv = ws_kv.keymap()
    k_bin = ws_bin.keymap()

    def late_precasts():
        ws_out[1].cast(wout_d[1])
        ws_kv.cast(wkv_d)
        ws_bin.cast(bwin_d)

    wst_list = [list(Wst)]

    resident = [{}]

    def stage(scr, keys, c0, n):
        if (scr.name, c0) in resident[0]:
            return resident[0][(scr.name, c0)]
        Wst_ = wst_list[0]
        NW_ = len(Wst_)
        return _stage(scr, keys, c0, n, Wst_, NW_)

    def _stage(scr, keys, c0, n, Wst, NWST):
        i = wst_i[0] % NWST
        assert n <= 256
        wst_i[0] += 1
        t = Wst[i]
        pi, w_ = scr.pieces[c0]
        assert n <= w_
        P.dma("sp", t[:, :, 0:n], scr.t[pi][:, :, 0:n], keys[c0], ["wst%d" % i])
        return t, "wst%d" % i

    P.memset("pool", identf, 1.0, ["identf"])
    P.op("pool", lambda e: e.affine_select(out=identf, in_=identf, pattern=[[1, 128]], compare_op=ALU.is_equal,
                                           fill=0.0, base=0, channel_multiplier=-1), ["identf"], ["identf"])
    P.copy("dve", ident, identf, ["identf"], ["ident"])
    for mk, name, pat, cm, cop in ((maskSU, "maskSU", 1, -1, ALU.is_gt), (maskU, "maskU", 1, -1, ALU.is_ge),
                                   (maskSL, "maskSL", -1, 1, ALU.is_gt)):
        P.memset("pool", mtmp, 1.0, ["mtmp"])
        P.op("pool", lambda e, pat=pat, cm=cm, cop=cop: e.affine_select(
            out=mtmp, in_=mtmp, pattern=[[pat, 128]], compare_op=cop, fill=0.0, base=0, channel_multiplier=cm),
            ["mtmp"], ["mtmp"])
        P.copy("pool", mk, mtmp, ["mtmp"], [name])
    P.memset("pool", ones64, 1.0, ["ones64"])
    P.memset("pool", onescol, 1.0, ["onescol"])
    P.memset("pool", cmask, 1.0, ["cmask"])
    P.memset("pool", cmask.rearrange("p (a b) -> p a b", a=4)[:, :, 0:1], 0.0, ["cmask"])
    P.memset("pool", epsn, NORM_EPS, ["epsn"])
    P.memset("pool", epsl, LNX_EPS, ["epsl"])
    P.memset("pool", mhalf, -0.5, ["mhalf"])
    P.dma("sp", gains, gains_d, (), ["gains"])
    P.dma("sp", mu64, mu64_d, (), ["mu64"])
    P.dma("sp", muv_b, muv_d, (), ["muv_b"])
    P.dma("poolq", w2aug, w2a_d, (), ["w2aug"])
    P.dma("poolq", a2aug, a2a_d, (), ["a2aug"])
    P.dma("sp", headp, hp_d, (), ["headp"])
    P.dma("sp", subln_b, subln_d, (), ["subln_b"])
    def load_x(i):
        P.dma("sp", x_res[:, i, :], x_d[i * 128:(i + 1) * 128, :], (), ["x%d" % i])
    for i in range(2):
        load_x(i)

    arena = Arena(P, "arena", 108864)

    def rmsnorm_T(src, src_keys, outs, A):
        xb, junk, ss, rt, rstd = A["xb"], A["junk"], A["ss"], A["rt"], A["rstd"]
        P.act(xb, src, AF.Square, src_keys, ["xb", "ss"], accum_out=ss)
        P.ts("dve", rt, ss, 1.0 / D, NORM_EPS, ALU.mult, ALU.add, ["ss"], ["rt"])
        rsqrt_op(P, rstd, rt, mhalf[:, 0:1], ["rt"], ["rstd"])
        P.act(xb, src, AF.Copy, list(src_keys) + ["rstd"], ["xb"], scale=rstd)
        bk, bkey = P.bank(nr=len(outs))
        bv = bk[:, 0:512].bitcast(BF16).rearrange("p (a b) -> p a b", a=8)
        for kc in range(8):
            P.tr(bv[:, kc, :], xb[:, kc * 128:(kc + 1) * 128], ident, ["xb", "ident"], [bkey])
        for gi, oap, okey in outs:
            P.tt("dve", oap, bv, bc(gains[:, gi, :].unsqueeze(2), [128, 8, 128]), ALU.mult,
                 [bkey, "gains"], [okey])

    def mem_kv(l, A):
        memt, memT = A["memt"], A["memT"]
        wt, wkey = stage(ws_mkv[l], k_mkv[l], 0, 256)
        wt2, wkey2 = stage(ws_mkv[l], k_mkv[l], 256, 256)
        for mt in range(2):
            P.dma("sp", memt, mem_d[mt * 128:(mt + 1) * 128, :], (), ["memt"])
            rmsnorm_T(memt, ["memt"], [(2 + l, memT[:, :, mt * 128:(mt + 1) * 128], "memT")], A)
        for hp in range(2):
            bk, bkey = P.bank()
            for hh in range(2):
                h = hp * 2 + hh
                for kc in range(8):
                    P.mm(bk[0:64, hh * 256:(hh + 1) * 256], wt[:, kc, h * 64:(h + 1) * 64], memT[:, kc, :],
                         kc == 0, kc == 7, [wkey, "memT"], [bkey])
            P.copy("act", KmT[l][:, hp * 2:(hp + 1) * 2, :], bk[0:64, :].rearrange("p (a b) -> p a b", a=2),
                   [bkey], ["KmT%d" % l])
        P.memset("pool", Vma[l], 2.0, ["Vma%d" % l])
        for mt in range(2):
            bk, bkey = P.bank()
            for kc in range(8):
                P.mm(bk[:, 0:256], memT[:, kc, mt * 128:(mt + 1) * 128], wt2[:, kc, 0:256], kc == 0, kc == 7,
                     [wkey2, "memT"], [bkey])
            P.copy("act", Vma[l][:, mt, :, 0:64], bk[:, 0:256].rearrange("p (a b) -> p a b", a=4),
                   [bkey, "Vma%d" % l], ["Vma%d" % l])

    def mem_attn(l, A, qmT, sg, YG, qk="qmT", sk="sg", yk="xb"):
        PmT, ym, rz = A["PmT"], A["ym"], A["rz"]
        for mt in range(2):
            bk, bkey = P.bank()
            for h in range(4):
                P.mm(bk[:, h * 128:(h + 1) * 128], KmT[l][:, h, mt * 128:(mt + 1) * 128], qmT[:, h, :], True, True,
                     ["KmT%d" % l, qk], [bkey])
            P.act(PmT[mt], bk[:, :].rearrange("p (a b) -> p a b", a=4), AF.Exp, [bkey], ["PmT%d" % mt], scale=0.125)
        bk, bkey = P.bank(nr=2)
        ov = bk[:, 0:260].rearrange("p (a b) -> p a b", a=4)
        for h in range(4):
            for mt in range(2):
                P.mm(ov[:, h, :], PmT[mt][:, h, :], Vma[l][:, mt, h, :], mt == 0, mt == 1,
                     ["PmT%d" % mt, "Vma%d" % l], [bkey])
        P.op("dve", lambda e: e.reciprocal(out=rz, in_=ov[:, :, 64:65]), [bkey], ["rz"])
        P.tt("dve", ym, ov[:, :, 0:64], bc(rz, [128, 4, 64]), ALU.mult, [bkey, "rz"], ["ym"])
        P.tt("dve", YG[:, 768:1024], ym.rearrange("p a b -> p (a b)"), sg[:, 768:1024], ALU.mult, ["ym", sk], [yk])

    def out_proj(l, i, A, YG, final, yk="xb"):
        YGT, otmp, ssa, rt, rstd = A["YGT"], A["otmp"], A["ssa"], A["rt2"], A["rstd2"]
        junk = A["junk"]
        bk, bkey = P.bank()
        bv = bk[:, 0:512].bitcast(BF16).rearrange("p (a b) -> p a b", a=8)
        for kc in range(8):
            P.tr(bv[:, kc, :], YG[:, kc * 128:(kc + 1) * 128], ident, [yk, "ident"], [bkey])
        P.copy("act", YGT, bv, [bkey], ["YGT"])
        obk = []
        for cg in range(2):
            bk, bkey = P.bank(nr=2)
            for hf in range(2):
                wt, wkey = stage(ws_out[l], k_out[l], cg * 512 + hf * 256, 256)
                for kc in range(8):
                    P.mm(bk[:, hf * 256:(hf + 1) * 256], YGT[:, kc, :], wt[:, kc, :], kc == 0, kc == 7, ["YGT", wkey], [bkey])
            P.act(junk[:, 0:512], bk[:, :], AF.Square, [bkey], ["otmp0", "ssa%d" % cg], accum_out=ssa[:, cg:cg + 1])
            obk.append((bk, bkey))
        P.tt("dve", ssa[:, 2:3], ssa[:, 0:1], ssa[:, 1:2], ALU.add, ["ssa0", "ssa1"], ["ssa2"])
        P.ts("dve", rt, ssa[:, 2:3], 1.0 / D, NORM_EPS, ALU.mult, ALU.add, ["ssa2"], ["rt2"])
        rsqrt_op(P, rstd, rt, mhalf[:, 0:1], ["rt2"], ["rstd2"])
        xk = "x%d" % i
        for cg in range(2):
            bk, bkey = obk[cg]
            P.stt(otmp[cg], bk[:, :], rstd, postg_b[:, cg * 512:(cg + 1) * 512], ALU.mult, ALU.mult,
                  [bkey, "rstd2", "postg_b"], ["otmp%d" % cg])
            P.tt("pool", x_res[:, i, cg * 512:(cg + 1) * 512], x_res[:, i, cg * 512:(cg + 1) * 512], otmp[cg],
                 ALU.add, [xk, "otmp%d" % cg], [xk])
        if final:
            return P.dma("sp", out_d[i * 128:(i + 1) * 128, :], x_res[:, i, :], [xk], [])
        return None

    def tm_job(scr, keys, c0, n, lhs_list, nrs):
        wt, wkey = stage(scr, keys, c0, n)
        res = []
        for li, (hv, hkey) in enumerate(lhs_list):
            bk, bkey = P.bank(nr=(nrs[li] if nrs else 1))
            for kc in range(8):
                P.mm(bk[:, 0:n], hv[:, kc, :], wt[:, kc, 0:n], kc == 0, kc == 7, [hkey, wkey], [bkey])
            res.append((bk, bkey))
        return res

    def fm64_job(scr, keys, c0, hv, hkey):
        wt, wkey = stage(scr, keys, c0, 256)
        bk, bkey = P.bank()
        for g in range(4):
            for kc in range(8):
                P.mm(bk[0:64, g * 128:(g + 1) * 128], wt[:, kc, g * 64:(g + 1) * 64], hv[:, kc, :], kc == 0, kc == 7,
                     [hkey, wkey], [bkey])
        return bk, bkey

    A = {}
    A["xb"] = arena.alloc([1024], BF16)
    for nm in ("ss", "rt", "rstd", "rt2", "rstd2"):
        A[nm] = arena.alloc([1], F32)
    A["ssa"] = arena.alloc([4], F32)
    A["PmT"] = [arena.alloc([4, 128], BF16) for _ in range(2)]
    A["ym"] = arena.alloc([4, 64], F32)
    A["rz"] = arena.alloc([4, 1], F32)
    A["YGT"] = arena.alloc([8, 128], BF16)
    A["otmp"] = [arena.alloc([512], F32) for _ in range(2)]
    A["junk"] = A["otmp"][0].bitcast(BF16)
    common_off = arena.off
    A["memt"] = arena.alloc([1024], F32)
    A["memT"] = arena.alloc([8, 256], BF16)
    for l in range(2):
        mem_kv(l, A)
    P.barrier()
    arena.off = common_off

    lnx_b = arena.alloc([2, 768], F32)
    P.dma("sp", lnx_b, lnx_d, (), ["lnx_b"])
    hT = arena.alloc([8, 129], BF16)
    carry = arena.alloc([26], F32, parts=64)
    Zg = arena.alloc([8, 129], F32, parts=64)
    Zd = arena.alloc([8, 128], F32, parts=64)
    Zw = arena.alloc([2, 129], F32, parts=64)
    Zwd = arena.alloc([2, 128], F32, parts=64)
    twd = arena.alloc([128], BF16, parts=65)
    adb = arena.alloc([128], BF16, parts=65)
    sgw, av, Lp, kk, t2, t3 = [arena.alloc([4, 128], F32, parts=64) for _ in range(6)]
    eP, ePi, ePm = [arena.alloc([4, 128], BF16, parts=64) for _ in range(3)]
    DS = []
    for sl in range(3):
        d = {}
        for nm in ("aT", "rT", "bT", "kT", "rkT"):
            d[nm] = arena.alloc([4, 128], BF16, parts=64)
        d["PCt"] = arena.alloc([4, 1], F32, parts=64)
        DS.append(d)
    IO = []
    for sl in range(2):
        d = {}
        for nm in ("Tm0", "AakT", "ArbT", "ArkT"):
            d[nm] = arena.alloc([4, 128], BF16)
        d["BK"] = arena.alloc([8, 64], BF16)
        IO.append(d)
    PQs = [arena.alloc([2, 4, 128], BF16) for _ in range(2)]
    Tm1s = arena.alloc([4, 128], BF16)
    R1s = arena.alloc([4, 64], BF16)
    Ub = arena.alloc([4, 64], BF16)
    bs2 = [arena.alloc([12, 1], F32) for _ in range(2)]
    vps = arena.alloc([768], F32)
    Vt2 = [arena.alloc([768], BF16) for _ in range(2)]
    sg2 = [arena.alloc([1024], BF16) for _ in range(2)]
    qmT2 = [arena.alloc([4, 128], BF16, parts=64) for _ in range(2)]
    Mf = arena.alloc([12, 64], F32, parts=64)
    Mb = arena.alloc([12, 64], BF16, parts=64)
    Y2 = [arena.alloc([12, 64], F32) for _ in range(2)]
    Ysq = arena.alloc([12, 64], F32)
    st = arena.alloc([8, 12], F32)
    YG0 = arena.alloc([1024], BF16)

    if nlayers >= 1:
        P.dma("sp", postg_b, postg_d[:, 0, :], (), ["postg_b"])
        P.memset("pool", carry, 0.0, ["carry"])
        P.memset("pool", Mf, 0.0, ["Mf"])
        P.memset("pool", Mb, 0.0, ["Mb"])
        P.memset("pool", hT, 0.0, ["hT"])
        P.memset("pool", twd, 1.0, ["twd"])
        P.memset("pool", adb, 1.0, ["adb"])
    kk64 = headp[:, 0, :]
    ka64 = headp[:, 1, :]
    rk64 = headp[:, 2, :]
    if nlayers >= 1:
        P.ts("dve", lnx_b, lnx_b, 0.5, None, ALU.mult, None, ["lnx_b"], ["lnx_b"])
        P.ts("dve", rkc, bc(rk64.unsqueeze(2), [64, 12, 2]), 0.5, None, ALU.mult, None, ["headp"], ["rkc"])
    flat = lambda t: t.rearrange("p a b -> p (a b)")
    v4 = lambda t, a=4: t.rearrange("p (a b) -> p a b", a=a)

    def pre(i):
        xk = "x%d" % i
        tp = i % 2
        Vt, sg, qmT = Vt2[tp], sg2[tp], qmT2[tp]
        vk, sk, qk = "Vt%d" % tp, "sg%d" % tp, "qmT%d" % tp
        P.copy("pool", hT[:, :, 0:1], hT[:, :, 128:129], ["hT"], ["hT"])
        yield
        rmsnorm_T(x_res[:, i, :], [xk], [(0, hT[:, :, 1:129], "hT")], A)
        yield
        hcur = (hT[:, :, 1:129], "hT")
        hprev = (hT[:, :, 0:128], "hT")
        for q in range(3):
            (v1, v1k), (vp1, vp1k) = tm_job(ws_in, k_in, 1536 + q * 256, 256, [hcur, hprev], [2, 1])
            yield
            cs = slice(q * 256, (q + 1) * 256)
            P.copy("act", vps[:, cs], vp1[:, 0:256], [vp1k], ["vps%d" % q])
            P.tt("dve", vps[:, cs], vps[:, cs], v1[:, 0:256], ALU.subtract, ["vps%d" % q, v1k], ["vps%d" % q])
            yield
            P.tt("pool", vps[:, cs], vps[:, cs], muv_b[:, cs], ALU.mult, ["vps%d" % q, "muv_b"], ["vps%d" % q])
            P.tt("dve", Vt[:, cs], v1[:, 0:256], vps[:, cs], ALU.add, ["vps%d" % q, v1k], [vk])
            yield
        for q in range(4):
            c0 = 2432 + q * 256 if q < 3 else 3456
            (g1, g1k), = tm_job(ws_in, k_in, c0, 256, [hcur], [2])
            P.act(sg[:, q * 256:(q + 1) * 256], g1[:, 0:256], AF.Tanh, [g1k], [sk], scale=0.5)
            P.stt(sg[:, q * 256:(q + 1) * 256], sg[:, q * 256:(q + 1) * 256], 1.0, g1[:, 0:256], ALU.add, ALU.mult,
                  [sk, g1k], [sk])
            yield
        qb, qbk = fm64_job(ws_in, k_in, 3200, hT[:, :, 1:129], "hT")
        P.copy("act", qmT, v4(qb[0:64, :]), [qbk], [qk])
        yield
        wt, wkey = stage(ws_in, k_in, 2304, 128)
        bk, bkey = P.bank()
        for g in range(2):
            for kc in range(8):
                P.mm(bk[0:64, g * 128:(g + 1) * 128], wt[:, kc, g * 64:(g + 1) * 64], hT[:, kc, 1:129], kc == 0, kc == 7,
                     ["hT", wkey], [bkey])
        P.copy("pool", Zw[:, :, 0:1], carry[:, 24:26].unsqueeze(2), ["carry"], ["Zw"])
        P.copy("act", Zw[:, :, 1:129], v4(bk[0:64, 0:256], 2), [bkey], ["Zw"])
        yield
        P.tt("pool", Zwd, Zw[:, :, 0:128], Zw[:, :, 1:129], ALU.subtract, ["Zw"], ["Zwd"])
        P.tt("pool", Zwd, Zwd, bc(mu64[:, 24:26].unsqueeze(2), [64, 2, 128]), ALU.mult, ["Zwd", "mu64"], ["Zwd"])
        P.tt("dve", Zwd, Zw[:, :, 1:129], Zwd, ALU.add, ["Zw", "Zwd"], ["Zwd"])
        P.copy("pool", carry[:, 24:26].unsqueeze(2), Zw[:, :, 128:129], ["Zw"], ["carry"])
        yield
        P.act(twd[0:64, :], Zwd[:, 0, :], AF.Tanh, ["Zwd"], ["twd"])
        P.copy("act", adb[0:64, :], Zwd[:, 1, :], ["Zwd"], ["adb"])
        yield

    def dphase(i, g, sl):
        S_ = DS[sl]
        sk_ = lambda nm: "%s_%d" % (nm, sl)
        aT, rT, bT, kT, rkT, PCt = (S_[n] for n in ("aT", "rT", "bT", "kT", "rkT", "PCt"))
        rb, rbk = fm64_job(ws_in, k_in, g * 256, hT[:, :, 1:129], "hT")
        yield
        kb, kbk = fm64_job(ws_in, k_in, 768 + g * 256, hT[:, :, 1:129], "hT")
        yield
        P.copy("pool", Zg[:, :, 0:1], carry[:, g * 8:(g + 1) * 8].unsqueeze(2), ["carry"], ["Zg"])
        P.copy("act", Zg[:, 0:4, 1:129], v4(rb[0:64, :]), [rbk], ["Zg"])
        P.copy("act", Zg[:, 4:8, 1:129], v4(kb[0:64, :]), [kbk], ["Zg"])
        yield
        P.tt("pool", Zd, Zg[:, :, 0:128], Zg[:, :, 1:129], ALU.subtract, ["Zg"], ["Zd"])
        yield
        P.tt("pool", Zd, Zd, bc(mu64[:, g * 8:(g + 1) * 8].unsqueeze(2), [64, 8, 128]), ALU.mult, ["Zd", "mu64"], ["Zd"])
        yield
        P.tt("dve", Zd, Zg[:, :, 1:129], Zd, ALU.add, ["Zg", "Zd"], ["Zd"])
        P.copy("pool", carry[:, g * 8:(g + 1) * 8].unsqueeze(2), Zg[:, :, 128:129], ["Zg"], ["carry"])
        yield
        r_ = Zd[:, 0:4, :]
        k_ = Zd[:, 4:8, :]
        bw, bwk = P.bank()
        ba, bak = P.bank()
        for j in range(4):
            h = g * 4 + j
            P.mm(bw[0:64, j * 128:(j + 1) * 128], w2aug[:, h * 64:(h + 1) * 64], twd[:, :], True, True,
                 ["w2aug", "twd"], [bwk])
        for j in range(4):
            h = g * 4 + j
            P.mm(ba[0:64, j * 128:(j + 1) * 128], a2aug[:, h * 64:(h + 1) * 64], adb[:, :], True, True,
                 ["a2aug", "adb"], [bak])
        yield
        P.act(sgw, v4(bw[0:64, :]), AF.Tanh, [bwk], ["sgw"], scale=0.5)
        P.act(av, v4(ba[0:64, :]), AF.Tanh, [bak], ["av"], scale=0.5)
        P.act(sgw, sgw, AF.Copy, ["sgw"], ["sgw"], bias=0.5, scale=0.5)
        P.act(av, av, AF.Copy, ["av"], ["av"], bias=0.5, scale=0.5)
        yield
        P.op("dve", lambda e: e.tensor_tensor_scan(out=flat(Lp), data0=cmask, data1=flat(sgw), initial=0.0,
                                                   op0=ALU.mult, op1=ALU.add), ["sgw", "cmask"], ["Lp"])
        P.tt("dve", kk, k_, bc(kk64[:, g * 4:(g + 1) * 4].unsqueeze(2), [64, 4, 128]), ALU.mult, ["Zd", "headp"], ["kk"])
        yield
        P.tt("pool", t2, kk, kk, ALU.mult, ["kk"], ["t2"])
        P.act(eP, Lp, AF.Exp, ["Lp"], ["eP"], scale=-CDEC)
        yield
        P.act(ePi, Lp, AF.Exp, ["Lp"], ["ePi"], scale=CDEC)
        P.tt("pool", sgw, Lp, sgw, ALU.subtract, ["Lp", "sgw"], ["sgw"])
        yield
        P.act(PCt, Lp[:, :, 127:128], AF.Exp, ["Lp"], [sk_("PCt")], scale=-CDEC)
        yield
        bsq, bsqk = P.bank()
        P.mm(bsq[0:64, :], ones64, flat(t2), True, True, ["ones64", "t2"], [bsqk])
        P.act(ePm, sgw, AF.Exp, ["sgw"], ["ePm"], scale=-CDEC)
        yield
        P.ts("dve", flat(t2), bsq[0:64, :], 1e-12, None, ALU.max, None, [bsqk], ["t2"])
        yield
        P.act(t2, t2, AF.Ln, ["t2"], ["t2"])
        P.stt(t3, av, -1.0, bc(ka64[:, g * 4:(g + 1) * 4].unsqueeze(2), [64, 4, 128]), ALU.add, ALU.mult,
              ["av", "headp"], ["t3"])
        yield
        P.act(t2, t2, AF.Exp, ["t2"], ["t2"], scale=-0.5)
        P.stt(t3, t3, 1.0, k_, ALU.add, ALU.mult, ["t3", "Zd"], ["t3"])
        yield
        P.tt("dve", rT, r_, eP, ALU.mult, ["Zd", "eP"], [sk_("rT")])
        P.tt("pool", kT, t3, ePi, ALU.mult, ["t3", "ePi"], [sk_("kT")])
        yield
        P.tt("dve", kk, kk, t2, ALU.mult, ["kk", "t2"], ["kk"])
        yield
        P.stt(aT, kk, -1.0, ePm, ALU.mult, ALU.mult, ["kk", "ePm"], [sk_("aT")])
        P.tt("pool", av, kk, av, ALU.mult, ["kk", "av"], ["av"])
        yield
        P.tt("dve", bT, av, ePi, ALU.mult, ["av", "ePi"], [sk_("bT")])
        P.tt("pool", rkT, r_, t3, ALU.mult, ["Zd", "t3"], [sk_("rkT")])
        yield

    def istage(i, g, n):
        dsl, isl = n % 3, n % 2
        S_ = DS[dsl]
        O_ = IO[isl]
        dk_ = lambda nm: "%s_%d" % (nm, dsl)
        ik_ = lambda nm: "%s_i%d" % (nm, isl)

        def sk_(nm):
            if nm in ("aT", "rT", "bT", "kT", "rkT", "PCt"):
                return dk_(nm)
            if nm in ("Tm0", "AakT", "ArbT", "ArkT", "BK"):
                return ik_(nm)
            return nm + "_s"
        aT, rT, bT, kT, rkT, PCt = (S_[q] for q in ("aT", "rT", "bT", "kT", "rkT", "PCt"))
        BK, AakT, ArbT, ArkT = (O_[q] for q in ("BK", "AakT", "ArbT", "ArkT"))
        Qm = [PQs[q][:, 0, :, :] for q in range(2)]
        Pm = [PQs[q][:, 1, :, :] for q in range(2)]
        Tm = [O_["Tm0"], Tm1s]
        tp = i % 2

        def amat(lh, lkey, rh, rkey, mask, mkey, eng, oap, okey):
            bk, bkey = P.bank()
            for j in range(4):
                P.mm(bk[:, j * 128:(j + 1) * 128], lh[:, j, :], rh[:, j, :], True, True, [lkey, rkey], [bkey])
            P.tt(eng, oap, v4(bk[:, :]), bc(mask.unsqueeze(1), [128, 4, 128]), ALU.mult, [bkey, mkey], [okey])

        amat(bT, sk_("bT"), aT, sk_("aT"), maskSU, "maskSU", "dve", Pm[0], sk_("Pm0"))
        yield
        amat(aT, sk_("aT"), bT, sk_("bT"), maskSL, "maskSL", "dve", Qm[0], sk_("Qm0"))
        P.tt("pool", Tm[0], Pm[0], bc(ident.unsqueeze(1), [128, 4, 128]), ALU.add, [sk_("Pm0"), "ident"], [sk_("Tm0")])
        yield
        cur = 0
        extra = [
            lambda: amat(kT, sk_("kT"), aT, sk_("aT"), maskSU, "maskSU", "dve", AakT, sk_("AakT")),
            lambda: amat(bT, sk_("bT"), rT, sk_("rT"), maskU, "maskU", "dve", ArbT, sk_("ArbT")),
            lambda: amat(kT, sk_("kT"), rT, sk_("rT"), maskU, "maskU", "dve", ArkT, sk_("ArkT")),
        ]

        def ephase_a():
            bk, bkey = P.bank()
            bv = bk[:, 0:256].bitcast(BF16).rearrange("p (a b) -> p a b", a=8)
            for j in range(4):
                P.tr(bv[:, j, :], bT[:, j, :], ident[0:64, 0:64], [sk_("bT"), "ident"], [bkey])
            for j in range(4):
                P.tr(bv[:, 4 + j, :], kT[:, j, :], ident[0:64, 0:64], [sk_("kT"), "ident"], [bkey])
            P.copy("act", BK, bv, [bkey], [sk_("BK")])

        def ephase_b():
            bk, bkey = P.bank()
            for j in range(4):
                P.mm(bk[:, 2 * j:2 * j + 2], rkT[:, j, :], onescol if 'norkc' in TOG else rkc[:, g * 4 + j, :], True, True, [sk_("rkT"), "rkc", "onescol"], [bkey])
            P.copy("dve", bs2[tp][:, g * 4:(g + 1) * 4, :], v4(bk[:, 0:8])[:, :, 0:1], [bkey], ["bs%d" % tp])

        extra += [ephase_a, ephase_b]
        for kr in range(1, 7):
            nxt = 1 - cur
            bA, bAk = P.bank()
            for j in range(4):
                P.mm(bA[:, j * 128:(j + 1) * 128], Pm[cur][:, j, :], Qm[cur][:, j, :], True, True,
                     [sk_("Pm%d" % cur), sk_("Qm%d" % cur)], [bAk])
            if kr <= 5:
                bB, bBk = P.bank()
                for j in range(4):
                    P.mm(bB[:, j * 128:(j + 1) * 128], Qm[cur][:, j, :], Pm[cur][:, j, :], True, True,
                         [sk_("Pm%d" % cur), sk_("Qm%d" % cur)], [bBk])
            if extra:
                extra.pop(0)()
            P.copy("act", Qm[nxt], v4(bA[:, :]), [bAk], [sk_("Qm%d" % nxt)])
            if kr <= 5:
                P.copy("act", Pm[nxt], v4(bB[:, :]), [bBk], [sk_("Pm%d" % nxt)])
            yield
            btk, btkk = P.bank()
            for j in range(4):
                P.mm(btk[:, j * 128:(j + 1) * 128], Qm[nxt][:, j, :], Tm[cur][:, j, :], True, True,
                     [sk_("Qm%d" % nxt), (sk_("Tm0") if cur == 0 else "Tm1_s")], [btkk])
            yield
            P.tt("dve", Tm[nxt], v4(btk[:, :]), Tm[cur], ALU.add, [btkk, (sk_("Tm0") if cur == 0 else "Tm1_s")], [(sk_("Tm0") if nxt == 0 else "Tm1_s")])
            yield
            cur = nxt
        while extra:
            extra.pop(0)()
            yield
        assert cur == 0
    def cstage(i, g, n):
        dsl, isl = n % 3, n % 2
        S_ = DS[dsl]
        O_ = IO[isl]
        dk_ = lambda nm: "%s_%d" % (nm, dsl)
        ik_ = lambda nm: "%s_i%d" % (nm, isl)

        def sk_(nm):
            if nm in ("aT", "rT", "bT", "kT", "rkT", "PCt"):
                return dk_(nm)
            if nm in ("Tm0", "AakT", "ArbT", "ArkT", "BK"):
                return ik_(nm)
            return nm + "_s"
        aT, rT, PCt = (S_[q] for q in ("aT", "rT", "PCt"))
        BK, AakT, ArbT, ArkT = (O_[q] for q in ("BK", "AakT", "ArbT", "ArkT"))
        tp = i % 2
        Vt, vk = Vt2[tp], "Vt%d" % tp
        Y = Y2[tp]
        Tt = O_["Tm0"]
        Ttk = ik_("Tm0")
        Mbk = "Mb%d" % g
        Mfk = "Mf%d" % g
        b1, b1k = P.bank()
        for j in range(4):
            h = g * 4 + j
            P.mm(b1[:, j * 64:(j + 1) * 64], aT[:, j, :], Mb[:, h, :], True, False, [sk_("aT"), Mbk, "Mb"], [b1k])
            P.mm(b1[:, j * 64:(j + 1) * 64], AakT[:, j, :], Vt[:, h * 64:(h + 1) * 64], False, True, [sk_("AakT"), vk], [b1k])
        P.copy("act", R1s, v4(b1[:, 0:256]), [b1k], [sk_("R1s")])
        yield
        b2, b2k = P.bank()
        for j in range(4):
            P.mm(b2[:, j * 64:(j + 1) * 64], Tt[:, j, :], R1s[:, j, :], True, True, [Ttk, sk_("R1s")], [b2k])
        P.copy("act", Ub, v4(b2[:, 0:256]), [b2k], [sk_("Ub")])
        yield
        b4, b4k = P.bank()
        for j in range(4):
            h = g * 4 + j
            P.mm(b4[0:64, j * 64:(j + 1) * 64], BK[:, j, :], Ub[:, j, :], True, False, [sk_("BK"), sk_("Ub")], [b4k])
            P.mm(b4[0:64, j * 64:(j + 1) * 64], BK[:, 4 + j, :], Vt[:, h * 64:(h + 1) * 64], False, True, [sk_("BK"), vk], [b4k])
        b3, b3k = P.bank()
        for j in range(4):
            h = g * 4 + j
            P.mm(b3[:, j * 64:(j + 1) * 64], rT[:, j, :], Mb[:, h, :], True, False, [sk_("rT"), Mbk, "Mb"], [b3k])
            P.mm(b3[:, j * 64:(j + 1) * 64], ArbT[:, j, :], Ub[:, j, :], False, False, [sk_("ArbT"), sk_("Ub")], [b3k])
            P.mm(b3[:, j * 64:(j + 1) * 64], ArkT[:, j, :], Vt[:, h * 64:(h + 1) * 64], False, True, [sk_("ArkT"), vk], [b3k])
        yield
        Mfg = Mf[:, g * 4:(g + 1) * 4, :]
        P.tt("dve", Mfg, Mfg, v4(b4[0:64, 0:256]), ALU.add, [Mfk, "Mf", b4k], [Mfk])
        P.tt("pool", Mfg, Mfg, bc(PCt, [64, 4, 64]), ALU.mult, [Mfk, sk_("PCt")], [Mfk])
        P.copy("act", Mb[:, g * 4:(g + 1) * 4, :], Mfg, [Mfk], [Mbk])
        yield
        P.copy("dve", Y[:, g * 4:(g + 1) * 4, :], v4(b3[:, 0:256]), [b3k], ["Y%d" % tp])
        yield

    def post(i):
        tp = i % 2
        Y = Y2[tp]
        Vt, sg, qmT = Vt2[tp], sg2[tp], qmT2[tp]
        vk, sk, qk = "Vt%d" % tp, "sg%d" % tp, "qmT%d" % tp
        s1, s2, mean, m2, var, sd, rs = [st[:, q, :].unsqueeze(2) for q in range(7)]
        P.op("dve", lambda e: e.tensor_reduce(out=s1, in_=Y, axis=AX.X, op=ALU.add), ["Y%d" % tp], ["s1"])
        P.tt("pool", Ysq, Y, Y, ALU.mult, ["Y%d" % tp], ["Ysq"])
        yield
        P.op("dve", lambda e: e.tensor_reduce(out=s2, in_=Ysq, axis=AX.X, op=ALU.add), ["Ysq"], ["s2"])
        P.ts("dve", mean, s1, 1.0 / 64, None, ALU.mult, None, ["s1"], ["mean"])
        P.tt("dve", m2, mean, mean, ALU.mult, ["mean"], ["m2"])
        P.stt(var, s2, 1.0 / 64, m2, ALU.mult, ALU.subtract, ["s2", "m2"], ["var"])
        yield
        P.ts("dve", sd, var, LNX_EPS, None, ALU.add, None, ["var"], ["sd"])
        rsqrt_op(P, rs, sd, mhalf.unsqueeze(2), ["sd"], ["rs"])
        yield
        P.tt("dve", Ysq, Y, bc(mean, [128, 12, 64]), ALU.subtract, ["Y%d" % tp, "mean", "Ysq"], ["Ysq"])
        yield
        P.tt("pool", Ysq, Ysq, bc(rs, [128, 12, 64]), ALU.mult, ["Ysq", "rs"], ["Ysq"])
        yield
        Yf = Ysq.rearrange("p a b -> p (a b)")
        P.tt("pool", Yf, Yf, lnx_b[:, 0, :], ALU.mult, ["Ysq", "lnx_b"], ["Ysq"])
        yield
        P.tt("pool", Yf, Yf, lnx_b[:, 1, :], ALU.add, ["Ysq", "lnx_b"], ["Ysq"])
        yield
        for h in range(12):
            P.stt(Ysq[:, h, :], Vt[:, h * 64:(h + 1) * 64], bs2[tp][:, h, :], Ysq[:, h, :], ALU.mult, ALU.add,
                  [vk, "bs%d" % tp, "Ysq"], ["Ysq"])
            if h % 4 == 3:
                yield
        P.tt("dve", YG0[:, 0:768], Yf, sg[:, 0:768], ALU.mult, ["Ysq", sk], ["YG0"])
        yield
        mem_attn(0, A, qmT, sg, YG0, qk=qk, sk=sk, yk="YG0")
        yield
        out_proj(0, i, A, YG0, final=False, yk="YG0")
        yield

    def interleave_n(gens, rates=None):
        gens = list(gens)
        rates = rates or [1] * len(gens)
        alive = [True] * len(gens)
        while any(alive):
            for q in range(len(gens)):
                for _ in range(rates[q]):
                    if alive[q]:
                        try:
                            next(gens[q])
                        except StopIteration:
                            alive[q] = False

    def chain_gens(gs):
        for g_ in gs:
            yield from g_

    def interleave(ga, gb):
        alive = [True, True]
        gens = [ga, gb]
        while alive[0] or alive[1]:
            for q in range(2):
                if alive[q]:
                    try:
                        next(gens[q])
                    except StopIteration:
                        alive[q] = False

    if nlayers >= 1:
        units = [(i, g) for i in range(NT) for g in range(3)]

        NU = len(units)

        def dstage(n):
            i, g = units[n]
            gs = []
            if g == 0:
                if i + 2 < NT:
                    load_x(i + 2)
                if i == 1:
                    late_precasts()
                gs.append(pre(i))
            gs.append(dphase(i, g, n % 3))
            return chain_gens(gs)

        CSLOW = int(os.environ.get('KCS', '8'))

        def slow(gen, k):
            for _ in gen:
                yield
                for _q in range(k - 1):
                    yield

        for n in range(-2, NU):
            gens = []
            if n >= 0:
                gens.append(slow(cstage(units[n][0], units[n][1], n), CSLOW))
            if 0 <= n + 1 < NU:
                gens.append(istage(units[n + 1][0], units[n + 1][1], n + 1))
            if n + 2 < NU:
                gens.append(dstage(n + 2))
            if n >= 0 and units[n][1] == 0 and units[n][0] > 0:
                gens.append(post(units[n][0] - 1))
            interleave_n(gens)
        for _ in post(NT - 1):
            pass

    final_ops = []
    if dbg or nlayers < 2:
        tgt = dbg_d if dbg else out_d
        for i in range(NT):
            final_ops.append(P.dma("sp", tgt[i * 128:(i + 1) * 128, :], x_res[:, i, :], ["x%d" % i], []))

    if nlayers >= 2:
        P.barrier()
        arena.off = common_off
        P.dma("sp", postg_b, postg_d[:, 1, :], (), ["postg_b"])
        mark1 = arena.off
        dtmp = arena.alloc([128], F32)
        tq64 = arena.alloc([128], F32)
        itab = arena.alloc([16], F32)
        ndtmp = arena.alloc([128], F32)
        lamv = arena.alloc([4, 64], F32)
        lamw = arena.alloc([4, 64], F32)
        arena.off = mark1
        arena.off = arena.off + 2048 + 2048 + 512 + 512 + 64
        KTt, Vat = [], []
        for i_ in range(NT):
            if i_ < 3:
                KTt.append(arena.alloc([6, 128], BF16))
                Vat.append(arena.alloc([6, 129], BF16))
            else:
                xv = x_res[:, i_ - 3, :].bitcast(BF16)
                KTt.append(xv[:, 0:768].rearrange("p (h t) -> p h t", h=6))
                Vat.append(xv[:, 768:768 + 774].rearrange("p (h e) -> p h e", h=6))
        PT4 = [arena.alloc([2, NT, 128], BF16) for _ in range(2)]
        P.dma("sp", lamv, lamv_d, (), ["lamv"])
        hT1 = arena.alloc([8, 128], BF16)
        kvT = arena.alloc([8, 128], BF16)
        QT2 = [arena.alloc([6, 2, 128], BF16) for _ in range(2)]
        sg2b = [arena.alloc([1024], BF16) for _ in range(3)]
        qmT2b = [arena.alloc([4, 128], BF16, parts=64) for _ in range(3)]
        AT = arena.alloc([6, 128], F32)
        ATs = arena.alloc([6, 128], F32)
        YG = arena.alloc([1024], BF16)
        Dh = arena.alloc([6, 128], BF16)
        btab = arena.alloc([6, 16], F32)
        rzz = arena.alloc([2, 1], F32)
        w1 = arena.alloc([1], F32)
        s6 = arena.alloc([6, 1], F32)
        r6 = arena.alloc([6, 1], F32)
        stmp = arena.alloc([2, 128], F32)
        t1 = arena.alloc([128], F32)
        RES = {}
        cand = ([(ws_kv, k_kv, c0) for c0 in range(0, 1536, 256)] + [(ws_out[1], k_out[1], c0) for c0 in range(0, 1024, 256)]
                + [(ws_bin, k_bin, c0) for c0 in range(0, 768, 256)])
        for (ws_, kk_, c0) in cand:
            if arena.off + 4096 > arena.cap:
                break
            t_ = arena.alloc([8, 256], BF16)
            rk_ = "res_%s_%d" % (ws_.name, c0)
            P.dma("sp", t_, ws_.t[ws_.pieces[c0][0]], kk_[c0], [rk_])
            RES[(ws_.name, c0)] = (t_, rk_)
        resident[0] = RES
        print("resident pieces", len(RES))
        P.op("pool", lambda e: e.iota(dtmp, pattern=[[1, 128]], base=0, channel_multiplier=-1,
                                      allow_small_or_imprecise_dtypes=True), (), ["dtmp"])
        P.op("pool", lambda e: e.iota(tq64, pattern=[[1, 128]], base=-64, channel_multiplier=0,
                                      allow_small_or_imprecise_dtypes=True), (), ["tq64"])
        P.op("pool", lambda e: e.iota(itab, pattern=[[-128, 16]], base=-64, channel_multiplier=1,
                                      allow_small_or_imprecise_dtypes=True), (), ["itab"])
        P.op("pool", lambda e: e.iota(ndtmp, pattern=[[-1, 128]], base=0, channel_multiplier=1,
                                      allow_small_or_imprecise_dtypes=True), (), ["ndtmp"])
        P.tt("dve", dtmp, dtmp, ndtmp, ALU.max, ["dtmp", "ndtmp"], ["dtmp"])
        for h in range(6):
            P.ts("dve", Dh[:, h, :], dtmp, -SLOPES[h], None, ALU.mult, None, ["dtmp"], ["Dh"])
            P.stt(Dh[:, h, :], tq64, SLOPES[h], Dh[:, h, :], ALU.mult, ALU.add, ["tq64", "Dh"], ["Dh"])
            P.ts("dve", btab[:, h, :], itab, SLOPES[h], None, ALU.mult, None, ["itab"], ["btab"])
        P.memset("pool", Dh[64:128, :, 0:64], -30000.0, ["Dh"])
        P.tt("dve", lamw[:, 0, :], lamv[:, 0, :], lamv[:, 1, :], ALU.mult, ["lamv"], ["lamw"])
        P.tt("dve", lamw[:, 1, :], lamv[:, 2, :], lamv[:, 3, :], ALU.mult, ["lamv", "lamw"], ["lamw"])
        P.op("dve", lambda e: e.tensor_reduce(out=lamc[:, 0:2].unsqueeze(2), in_=lamw[:, 0:2, :], axis=AX.X, op=ALU.add),
             ["lamw"], ["lamc"])
        P.act(lamc[:, 2:4], lamc[:, 0:2], AF.Exp, ["lamc"], ["lamc2"])
        P.tt("dve", lamc[:, 4:5], lamc[:, 3:4], lamc[:, 2:3], ALU.subtract, ["lamc2"], ["lamc4"])
        P.ts("dve", lamc[:, 5:6], lamc[:, 4:5], -LAM_INIT1, None, ALU.add, None, ["lamc4"], ["neglam"])
        neglam = lamc[:, 5:6]
        P.ts("dve", subln_b, subln_b, 0.5 * (1.0 - LAM_INIT1), None, ALU.mult, None, ["subln_b"], ["subln_b"])
        for q in range(2):
            P.memset("pool", QT2[q], 0.0, ["QT%d" % q])

        def proj1(i):
            xk = "x%d" % i
            tp = i % 2
            t3_ = i % 3
            QT, sg, qmT = QT2[tp], sg2b[t3_], qmT2b[t3_]
            qtk, sk, qk = "QT%d" % tp, "sgb%d" % t3_, "qmTb%d" % t3_
            rmsnorm_T(x_res[:, i, :], [xk], [(4, kvT, "kvT"), (1, hT1, "hT1")], A)
            xdead = ["x%d" % (i - 3)] if i >= 3 else []
            P.memset("pool", Vat[i][:, :, 128:129], 1.0, ["Vaug%d" % i] + xdead)
            yield
            for hh in range(0, 6, 2):
                wt, wkey = stage(ws_kv, k_kv, hh * 128, 256)
                bk, bkey = P.bank()
                for q in range(2):
                    for kc in range(8):
                        P.mm(bk[:, q * 128:(q + 1) * 128], wt[:, kc, q * 128:(q + 1) * 128], kvT[:, kc, :], kc == 0, kc == 7,
                             ["kvT", wkey], [bkey])
                P.copy("dve", KTt[i][:, hh:hh + 2, :], bk[:, 0:256].rearrange("p (a b) -> p a b", a=2),
                       [bkey], ["KT%d" % i] + xdead)
                yield
            for q in range(3):
                (vb1, vb1k), = tm_job(ws_kv, k_kv, 768 + q * 256, 256, [(kvT, "kvT")], None)
                P.copy("dve", Vat[i][:, 2 * q:2 * q + 2, 0:128], vb1[:, 0:256].rearrange("p (a b) -> p a b", a=2),
                       [vb1k], ["Vaug%d" % i] + xdead)
                yield
            for hh in range(0, 6, 2):
                wt, wkey = stage(ws_bin, k_bin, hh * 128, 256)
                bk, bkey = P.bank(nr=2)
                for q in range(2):
                    for kc in range(8):
                        P.mm(bk[:, q * 128:(q + 1) * 128], wt[:, kc, q * 128:(q + 1) * 128], hT1[:, kc, :], kc == 0, kc == 7,
                             ["hT1", wkey], [bkey])
                for c in range(2):
                    P.copy("dve", QT[c * 64:(c + 1) * 64, hh:hh + 2, c, :],
                           bk[c * 64:(c + 1) * 64, 0:256].rearrange("p (a b) -> p a b", a=2), [bkey, qtk], [qtk])
                yield
            for q in range(4):
                c0 = 768 + q * 256 if q < 3 else 1792
                (g1, g1k), = tm_job(ws_bin, k_bin, c0, 256, [(hT1, "hT1")], [2])
                P.act(sg[:, q * 256:(q + 1) * 256], g1[:, 0:256], AF.Tanh, [g1k], [sk], scale=0.5)
                P.stt(sg[:, q * 256:(q + 1) * 256], sg[:, q * 256:(q + 1) * 256], 1.0, g1[:, 0:256], ALU.add, ALU.mult,
                      [sk, g1k], [sk])
                yield
            qb, qbk = fm64_job(ws_bin, k_bin, 1536, hT1, "hT1")
            P.copy("dve", qmT, qb[0:64, :].rearrange("p (a b) -> p a b", a=4), [qbk], [qk])
            yield

        def attn1(i):
            xk = "x%d" % i
            tp = i % 2
            QT = QT2[tp]
            qtk = "QT%d" % tp

            def scores(h):
                PT = PT4[h % 2]
                ptk = "PT%d" % (h % 2)
                for j in range(i + 1):
                    bk, bkey = P.bank()
                    P.mm(bk[:, 0:256], KTt[j][:, h, :], QT[:, h, :, :].rearrange("p c q -> p (c q)"),
                         True, True, ["KT%d" % j, qtk], [bkey])
                    sv = bk[:, 0:256].rearrange("p (c q) -> p c q", c=2)
                    if j == i:
                        P.stt(stmp, sv, 0.125, bc(Dh[:, h, :].unsqueeze(1), [128, 2, 128]), ALU.mult, ALU.add,
                              [bkey, "Dh"], ["stmp"])
                        P.act(PT[:, :, j, :], stmp, AF.Exp, ["stmp"], [ptk])
                    else:
                        P.act(PT[:, :, j, :], sv, AF.Exp, [bkey, "btab"], [ptk],
                              bias=btab[:, h, (i - j):(i - j) + 1], scale=0.125)
                    if j % 2 == 1:
                        yield

            def pv(h):
                PT = PT4[h % 2]
                ptk = "PT%d" % (h % 2)
                bk, bkey = P.bank(6, 8, nr=3)
                ov = bk[:, 0:258].rearrange("p (a b) -> p a b", a=2)
                for c in range(2):
                    for j in range(i + 1):
                        P.mm(ov[:, c, :], PT[:, c, j, :], Vat[j][:, h, :], j == 0, j == i, [ptk, "Vaug%d" % j], [bkey])
                        if j % 4 == 3:
                            yield
                P.op("dve", lambda e, ov=ov: e.reciprocal(out=rzz, in_=ov[:, :, 128:129]), [bkey], ["rzz"])
                P.tt("dve", w1, rzz[:, 1, :], neglam, ALU.mult, ["rzz", "neglam"], ["w1"])
                P.ts("dve", t1, ov[:, 1, 0:128], w1, None, ALU.mult, None, [bkey, "w1"], ["t1"])
                P.stt(AT[:, h, :], ov[:, 0, 0:128], rzz[:, 0, :], t1, ALU.mult, ALU.add, [bkey, "rzz", "t1"], ["AT"])
                yield

            yield from scores(0)
            for h in range(6):
                if h + 1 < 6:
                    yield from scores(h + 1)
                yield from pv(h)

        def tail1(i):
            t3_ = i % 3
            sg, qmT = sg2b[t3_], qmT2b[t3_]
            sk, qk = "sgb%d" % t3_, "qmTb%d" % t3_
            P.tt("pool", ATs, AT, AT, ALU.mult, ["AT"], ["ATs"])
            P.op("dve", lambda e: e.tensor_reduce(out=s6, in_=ATs, axis=AX.X, op=ALU.add), ["ATs"], ["s6"])
            P.ts("dve", r6, s6, 1.0 / 128, NORM_EPS, ALU.mult, ALU.add, ["s6"], ["r6"])
            rsqrt_op(P, s6, r6, mhalf[:, 0:6].unsqueeze(2), ["r6"], ["s6"])
            yield
            P.tt("pool", ATs, AT, bc(s6, [128, 6, 128]), ALU.mult, ["AT", "s6", "ATs"], ["ATs"])
            P.tt("pool", ATs, ATs, bc(subln_b.unsqueeze(1), [128, 6, 128]), ALU.mult, ["ATs", "subln_b"], ["ATs"])
            yield
            P.tt("dve", YG[:, 0:768], ATs.rearrange("p a b -> p (a b)"), sg[:, 0:768], ALU.mult, ["ATs", sk], ["YG1"])
            yield
            mem_attn(1, A, qmT, sg, YG, qk=qk, sk=sk, yk="YG1")
            yield
            final_ops.append(out_proj(1, i, A, YG, final=True, yk="YG1"))
            yield

        def interleave2(ga, gb, ra=1, rb=1):
            alive = [True, True]
            gens = [ga, gb]
            rr = [ra, rb]
            while alive[0] or alive[1]:
                for q in range(2):
                    for _ in range(rr[q]):
                        if alive[q]:
                            try:
                                next(gens[q])
                            except StopIteration:
                                alive[q] = False

        P.bank_rng = (0, 6)
        for _ in proj1(0):
            pass
        for i in range(NT):
            na = max(1, (6 * ((i + 2) // 2 + (i + 1) // 4 + 2) + 6) // 18)
            gens, rates = [attn1(i)], [na]
            if i + 1 < NT:
                gens.append(proj1(i + 1))
                rates.append(1)
            if i > 0:
                gens.append(tail1(i - 1))
                rates.append(1)
            interleave_n(gens, rates)
        for _ in tail1(NT - 1):
            pass

    print("arena use", arena.off, "cst use", cst.off)
    P.emit(final_wait_ops=final_ops)
    return nc, P


_CACHE = {}


def prep_inputs(inp, b):
    f = lambda a: np.ascontiguousarray(np.asarray(a, dtype=np.float32))
    mu = f(inp["a_shift_mu"])[0]
    cols = []
    for g in range(3):
        for j in range(4):
            cols.append(np.arange((g * 4 + j) * 64, (g * 4 + j + 1) * 64))
        for j in range(4):
            cols.append(768 + np.arange((g * 4 + j) * 64, (g * 4 + j + 1) * 64))
    cols.append(np.arange(2304, 2368))
    cols.append(np.arange(2368, 2432))
    mu64 = np.stack([mu[c] for c in cols], axis=1)
    gains = np.stack([f(inp["pre_norm"])[0], f(inp["pre_norm"])[1], f(inp["mem_norm"])[0], f(inp["mem_norm"])[1],
                      f(inp["kv_norm"])], 0)
    gains = gains.reshape(5, 8, 128).transpose(2, 0, 1)
    hp = np.stack([f(inp["a_k_k"])[0].reshape(12, 64).T, f(inp["a_k_a"])[0].reshape(12, 64).T,
                   f(inp["a_r_k"])[0].T], 1)
    d = {
        "x": f(inp["x"][b]),
        "mem": f(inp["mem"][b]),
        "a_w_in": f(inp["a_w_in"])[0],
        "w_out": f(inp["w_out"]),
        "w_mem_kv": f(inp["w_mem_kv"]),
        "w_kv": f(inp["w_kv"]),
        "b_w_in": f(inp["b_w_in"])[0],
        "gains": f(gains),
        "mu64": f(mu64),
        "muv_b": f(np.broadcast_to(mu[1536:2304], (128, 768))),
        "w2aug": f(np.concatenate([f(inp["a_w2"])[0], f(inp["a_w0"])], 0)),
        "a2aug": f(np.concatenate([f(inp["a_a2"])[0], f(inp["a_a0"])], 0)),
        "headp": f(hp),
        "lnx_b": f(np.broadcast_to(np.stack([f(inp["a_lnx_w"])[0], f(inp["a_lnx_b"])[0]], 0), (128, 2, 768))),
        "postg_b": f(np.broadcast_to(f(inp["post_norm"]), (128, 2, 1024))),
        "subln_b": f(np.broadcast_to(f(inp["b_subln"])[0], (128, 128))),
        "lamv_b": f(np.broadcast_to(np.stack([f(inp["b_lam_q1"])[0], f(inp["b_lam_k1"])[0], f(inp["b_lam_q2"])[0],
                                              f(inp["b_lam_k2"])[0]], 0), (128, 4, 64))),
    }
    return d


def kernel(**inputs):
    if "nc" not in _CACHE:
        _CACHE["nc"] = build()[0]
    nc = _CACHE["nc"]
    in_maps = [prep_inputs(inputs, b) for b in range(8)]
    res = run_bass_kernel_spmd(nc, in_maps, core_ids=list(range(8)))
    out = np.stack([np.asarray(r["out"], dtype=np.float32) for r in res.results], 0)
    return out
```
